# Optimizing a Trainium2 kernel written in Bass

```python
import jax
import jax.numpy as jnp
from jax import lax
import numpy as np

D_MODEL = 1024
BATCH = 16
SEQ = 4096
DEPTH = 1

PLE_DIM = 256
HEAD_DIM = 64
NSA_WIDTH = D_MODEL // 2
RWKV_WIDTH = D_MODEL - NSA_WIDTH
NSA_HEADS = NSA_WIDTH // HEAD_DIM
NSA_KV_HEADS = 2
NSA_GROUP = NSA_HEADS // NSA_KV_HEADS
NSA_KV_WIDTH = NSA_KV_HEADS * HEAD_DIM
CMP_LEN = 32
CMP_STRIDE = 16
CMP_HIDDEN = 2 * HEAD_DIM
SEL_BLOCK = 64
SEL_TOPK = 16
WINDOW = 512
Q_BLOCK = 32
RWKV_HEADS = RWKV_WIDTH // HEAD_DIM
DECAY_LORA = 64
ICLR_LORA = 64
GATE_LORA = 128
D_FF = 4 * D_MODEL
NORM_EPS = 1e-6
GN_EPS = 64e-5
NEG_INF = -1e30
FORCE_SCORE = 1e4
NSA_SIZES = (NSA_WIDTH,) + (NSA_KV_WIDTH,) * 6 + (3 * NSA_HEADS,)
RWKV_SIZES = (RWKV_WIDTH,) * 3 + (DECAY_LORA, ICLR_LORA, GATE_LORA)
NSA_COLS = sum(NSA_SIZES)
RWKV_COLS = sum(RWKV_SIZES)
IN_COLS = NSA_COLS + RWKV_COLS

kernel_name = "hymba_nsa_rwkv7_sandwich_ple"


def rmsnorm(x, g):
    xf = x.astype(jnp.float32)
    y = xf * lax.rsqrt(jnp.mean(xf * xf, axis=-1, keepdims=True) + NORM_EPS)
    return (y * g.astype(jnp.float32)).astype(x.dtype)


def split_cols(z, sizes):
    return jnp.split(z, np.cumsum(sizes)[:-1].tolist(), axis=-1)


def alibi_slopes(n):
    return jnp.asarray([2.0 ** (-8.0 * (h + 1) / n) for h in range(n)], jnp.float32)


def masked_softmax(s, mask):
    p = jax.nn.softmax(jnp.where(mask, s.astype(jnp.float32), NEG_INF), axis=-1)
    return p * jnp.any(mask, axis=-1, keepdims=True)


def compress_blocks(kv, pe, w1, b1, w2):
    S = kv.shape[2]
    n_cmp = (S - CMP_LEN) // CMP_STRIDE + 1
    idx = jnp.arange(n_cmp)[:, None] * CMP_STRIDE + jnp.arange(CMP_LEN)[None, :]
    blocks = kv[:, :, idx] + pe
    flat = blocks.reshape(blocks.shape[:3] + (CMP_LEN * HEAD_DIM,))
    return jax.nn.gelu(flat @ w1 + b1) @ w2


def cmp_to_sel_map(n_cmp, n_sel):
    c0 = jnp.arange(n_cmp) * CMP_STRIDE
    s0 = jnp.arange(n_sel) * SEL_BLOCK
    ov = (jnp.minimum(c0[:, None] + CMP_LEN - 1, s0[None, :] + SEL_BLOCK - 1)
          - jnp.maximum(c0[:, None], s0[None, :]) + 1)
    return jnp.clip(ov, 0).astype(jnp.float32) / CMP_STRIDE


def nsa_mixer(q, k_cmp, v_cmp, k_slc, v_slc, k_win, v_win, gates, cmp_k_params, cmp_v_params):
    B, S, _ = q.shape
    G, R = NSA_KV_HEADS, NSA_GROUP
    f32 = jnp.float32
    qh = q.reshape(B, S, G, R, HEAD_DIM).transpose(0, 2, 3, 1, 4) * (HEAD_DIM ** -0.5)

    def kv_heads(t):
        return t.reshape(B, S, G, HEAD_DIM).transpose(0, 2, 1, 3)

    kc = compress_blocks(kv_heads(k_cmp), *cmp_k_params)
    vc = compress_blocks(kv_heads(v_cmp), *cmp_v_params)
    n_cmp = kc.shape[2]
    n_sel = S // SEL_BLOCK
    top_k = min(SEL_TOPK, n_sel)
    cmp_end = jnp.arange(n_cmp) * CMP_STRIDE + (CMP_LEN - 1)
    sel_map = cmp_to_sel_map(n_cmp, n_sel)
    ks_blk = kv_heads(k_slc).reshape(B, G, n_sel, SEL_BLOCK, HEAD_DIM)
    vs_blk = kv_heads(v_slc).reshape(B, G, n_sel, SEL_BLOCK, HEAD_DIM)
    pad = ((0, 0), (0, 0), (WINDOW, 0), (0, 0))
    kw_pad = jnp.pad(kv_heads(k_win), pad)
    vw_pad = jnp.pad(kv_heads(v_win), pad)
    gates = gates.reshape(B, S, G, R, 3).transpose(0, 2, 3, 1, 4)
    slopes = alibi_slopes(NSA_HEADS).reshape(G, R, 1, 1)
    b_idx = jnp.arange(B)[:, None, None, None]
    g_idx = jnp.arange(G)[None, :, None, None]
    sel_ids = jnp.arange(n_sel)

    def query_block(c0):
        t = c0 + jnp.arange(Q_BLOCK)
        qb = lax.dynamic_slice_in_dim(qh, c0, Q_BLOCK, axis=3)
        dist = (t[:, None] - cmp_end[None, :]).astype(f32)
        s = jnp.einsum('bgrqd,bgnd->bgrqn', qb, kc).astype(f32) - slopes * dist
        p_cmp = masked_softmax(s, dist >= 0)
        o_cmp = jnp.einsum('bgrqn,bgnd->bgrqd', p_cmp.astype(vc.dtype), vc)
        imp = jnp.einsum('bgrqn,nj->bgqj', p_cmp, sel_map)
        cur = (t // SEL_BLOCK)[:, None]
        forced = (sel_ids == 0) | (sel_ids == cur) | (sel_ids == cur - 1)
        score = jnp.where(forced, FORCE_SCORE, jnp.where(sel_ids <= cur, imp, -1.0))
        _, idx = lax.top_k(score, top_k)
        k_sel = ks_blk[b_idx, g_idx, idx]
        v_sel = vs_blk[b_idx, g_idx, idx]
        pos = idx[..., None] * SEL_BLOCK + jnp.arange(SEL_BLOCK)
        dist = (t[:, None, None] - pos)[:, :, None].astype(f32)
        s = jnp.einsum('bgrqd,bgqkld->bgrqkl', qb, k_sel).astype(f32) - slopes[..., None] * dist
        n_keys = top_k * SEL_BLOCK
        p_sel = masked_softmax(s.reshape(B, G, R, Q_BLOCK, n_keys),
                               (dist >= 0).reshape(B, G, 1, Q_BLOCK, n_keys))
        o_sel = jnp.einsum('bgrqkl,bgqkld->bgrqd', p_sel.reshape(s.shape).astype(v_sel.dtype), v_sel)
        kpos = c0 - WINDOW + jnp.arange(WINDOW + Q_BLOCK)
        kw = lax.dynamic_slice_in_dim(kw_pad, c0, WINDOW + Q_BLOCK, axis=2)
        vw = lax.dynamic_slice_in_dim(vw_pad, c0, WINDOW + Q_BLOCK, axis=2)
        dist = t[:, None] - kpos[None, :]
        mask = (kpos >= 0)[None, :] & (dist >= 0) & (dist < WINDOW)
        s = jnp.einsum('bgrqd,bgnd->bgrqn', qb, kw).astype(f32) - slopes * dist.astype(f32)
        p_win = masked_softmax(s, mask)
        o_win = jnp.einsum('bgrqn,bgnd->bgrqd', p_win.astype(vw.dtype), vw)
        gb = lax.dynamic_slice_in_dim(gates, c0, Q_BLOCK, axis=3)
        return gb[..., 0:1] * o_cmp + gb[..., 1:2] * o_sel + gb[..., 2:3] * o_win

    out = lax.map(query_block, jnp.arange(S // Q_BLOCK) * Q_BLOCK)
    return out.transpose(1, 0, 4, 2, 3, 5).reshape(B, S, NSA_WIDTH)


def wkv7_scan(r, w, k, v, a, b):
    B, S, H, N = r.shape
    xs = tuple(t.astype(jnp.float32).transpose(1, 0, 2, 3) for t in (r, w, k, v, a, b))

    def step(state, inp):
        r_t, w_t, k_t, v_t, a_t, b_t = inp
        sa = jnp.einsum('bhvk,bhk->bhv', state, a_t)
        state = (state * w_t[:, :, None, :] + sa[..., None] * b_t[:, :, None, :]
                 + v_t[..., None] * k_t[:, :, None, :])
        return state, jnp.einsum('bhvk,bhk->bhv', state, r_t)

    _, y = lax.scan(step, jnp.zeros((B, H, N, N), jnp.float32), xs)
    return y.transpose(1, 0, 2, 3)


def rwkv7_mixer(z, shift_mu, w0, w_lora_up, a0, a_lora_up, g_lora_up, k_k, k_a, r_k, lnx_w, lnx_b):
    B, S, _ = z.shape
    f32 = jnp.float32
    z_prev = jnp.pad(z, ((0, 0), (1, 0), (0, 0)))[:, :-1]
    z = z + (z_prev - z) * shift_mu
    r, k, v, wd, ad, gd = split_cols(z, RWKV_SIZES)
    w = -jax.nn.softplus(-(w0 + jnp.tanh(wd) @ w_lora_up)) - 0.5
    a = jax.nn.sigmoid(a0 + ad @ a_lora_up)
    g = jax.nn.sigmoid(gd) @ g_lora_up

    def hd(t):
        return t.reshape(B, S, RWKV_HEADS, HEAD_DIM)

    kk = hd(k * k_k).astype(f32)
    kk = kk * lax.rsqrt(jnp.maximum(jnp.sum(kk * kk, axis=-1, keepdims=True), 1e-24))
    k = k * (1.0 + (a - 1.0) * k_a)
    r, k, v, a = hd(r), hd(k), hd(v), hd(a)
    decay = jnp.exp(-jnp.exp(hd(w).astype(f32)))
    y = wkv7_scan(r, decay, k, v, -kk, kk * a)
    mu = jnp.mean(y, axis=-1, keepdims=True)
    var = jnp.mean(jnp.square(y - mu), axis=-1, keepdims=True)
    y = ((y - mu) * lax.rsqrt(var + GN_EPS) * lnx_w.reshape(RWKV_HEADS, HEAD_DIM)
         + lnx_b.reshape(RWKV_HEADS, HEAD_DIM))
    y = y + jnp.sum(r * k * r_k, axis=-1, keepdims=True) * v
    return (y.reshape(B, S, RWKV_WIDTH) * g).astype(z.dtype)


def hybrid_layer(x, p_l, g_mix_pre, g_mix_post, g_mlp_pre, g_mlp_post, w_in, nsa_gate_bias,
                 cmp_pe_k, cmp_k_w1, cmp_k_b1, cmp_k_w2, cmp_pe_v, cmp_v_w1, cmp_v_b1, cmp_v_w2,
                 shift_mu, w0, w_lora_up, a0, a_lora_up, g_lora_up, k_k, k_a, r_k, lnx_w, lnx_b,
                 w_out, w_up, w_down, w_ple, w_ple_gate):
    h = rmsnorm(x, g_mix_pre)
    z = h @ w_in
    q, kc, vc, ks, vs, kw, vw, gate_logits = split_cols(z[..., :NSA_COLS], NSA_SIZES)
    y_nsa = nsa_mixer(q, kc, vc, ks, vs, kw, vw, jax.nn.sigmoid(gate_logits + nsa_gate_bias),
                      (cmp_pe_k, cmp_k_w1, cmp_k_b1, cmp_k_w2), (cmp_pe_v, cmp_v_w1, cmp_v_b1, cmp_v_w2))
    y_rwkv = rwkv7_mixer(z[..., NSA_COLS:], shift_mu, w0, w_lora_up, a0, a_lora_up, g_lora_up,
                         k_k, k_a, r_k, lnx_w, lnx_b)
    mix = jnp.concatenate([y_nsa, y_rwkv], axis=-1) @ w_out
    x = x + rmsnorm(mix, g_mix_post)
    h = rmsnorm(x, g_mlp_pre)
    f = jnp.square(jax.nn.relu(h @ w_up)) @ w_down
    x = x + rmsnorm(f, g_mlp_post)
    return x + jax.nn.sigmoid(x @ w_ple_gate) * (p_l @ w_ple)


def setup_inputs(seed: int = 0) -> dict:
    key = jax.random.key(seed)
    keys = iter(jax.random.split(key, 40))
    L = DEPTH

    def nrm(shape, scale):
        return jax.random.normal(next(keys), shape, jnp.float32) * scale

    def gain(shape):
        return 1.0 + nrm(shape, 0.05)

    return {
        "x": nrm((BATCH, SEQ, D_MODEL), 1.0),
        "p": nrm((L, BATCH, SEQ, PLE_DIM), 1.0),
        "g_mix_pre": gain((L, D_MODEL)),
        "g_mix_post": gain((L, D_MODEL)),
        "g_mlp_pre": gain((L, D_MODEL)),
        "g_mlp_post": gain((L, D_MODEL)),
        "w_in": nrm((L, D_MODEL, IN_COLS), D_MODEL ** -0.5),
        "nsa_gate_bias": nrm((L, 3 * NSA_HEADS), 0.1),
        "cmp_pe_k": nrm((L, CMP_LEN, HEAD_DIM), 0.1),
        "cmp_k_w1": nrm((L, CMP_LEN * HEAD_DIM, CMP_HIDDEN), (CMP_LEN * HEAD_DIM) ** -0.5),
        "cmp_k_b1": nrm((L, CMP_HIDDEN), 0.02),
        "cmp_k_w2": nrm((L, CMP_HIDDEN, HEAD_DIM), CMP_HIDDEN ** -0.5),
        "cmp_pe_v": nrm((L, CMP_LEN, HEAD_DIM), 0.1),
        "cmp_v_w1": nrm((L, CMP_LEN * HEAD_DIM, CMP_HIDDEN), (CMP_LEN * HEAD_DIM) ** -0.5),
        "cmp_v_b1": nrm((L, CMP_HIDDEN), 0.02),
        "cmp_v_w2": nrm((L, CMP_HIDDEN, HEAD_DIM), CMP_HIDDEN ** -0.5),
        "shift_mu": jax.random.uniform(next(keys), (L, RWKV_COLS), jnp.float32),
        "w0": nrm((L, RWKV_WIDTH), 0.5),
        "w_lora_up": nrm((L, DECAY_LORA, RWKV_WIDTH), 0.5 * DECAY_LORA ** -0.5),
        "a0": nrm((L, RWKV_WIDTH), 0.5),
        "a_lora_up": nrm((L, ICLR_LORA, RWKV_WIDTH), ICLR_LORA ** -0.5),
        "g_lora_up": nrm((L, GATE_LORA, RWKV_WIDTH), GATE_LORA ** -0.5),
        "k_k": 0.85 + nrm((L, RWKV_WIDTH), 0.05),
        "k_a": gain((L, RWKV_WIDTH)),
        "r_k": nrm((L, RWKV_HEADS, HEAD_DIM), 0.1),
        "lnx_w": gain((L, RWKV_WIDTH)),
        "lnx_b": nrm((L, RWKV_WIDTH), 0.02),
        "w_out": nrm((L, D_MODEL, D_MODEL), D_MODEL ** -0.5),
        "w_up": nrm((L, D_MODEL, D_FF), D_MODEL ** -0.5),
        "w_down": nrm((L, D_FF, D_MODEL), D_FF ** -0.5),
        "w_ple": nrm((L, PLE_DIM, D_MODEL), PLE_DIM ** -0.5),
        "w_ple_gate": nrm((L, D_MODEL, D_MODEL), D_MODEL ** -0.5),
    }


def reference(x, p, g_mix_pre, g_mix_post, g_mlp_pre, g_mlp_post, w_in, nsa_gate_bias,
              cmp_pe_k, cmp_k_w1, cmp_k_b1, cmp_k_w2, cmp_pe_v, cmp_v_w1, cmp_v_b1, cmp_v_w2,
              shift_mu, w0, w_lora_up, a0, a_lora_up, g_lora_up, k_k, k_a, r_k, lnx_w, lnx_b,
              w_out, w_up, w_down, w_ple, w_ple_gate):
    for i in range(DEPTH):
        x = hybrid_layer(x, p[i], g_mix_pre[i], g_mix_post[i], g_mlp_pre[i], g_mlp_post[i], w_in[i],
                         nsa_gate_bias[i], cmp_pe_k[i], cmp_k_w1[i], cmp_k_b1[i], cmp_k_w2[i],
                         cmp_pe_v[i], cmp_v_w1[i], cmp_v_b1[i], cmp_v_w2[i], shift_mu[i], w0[i],
                         w_lora_up[i], a0[i], a_lora_up[i], g_lora_up[i], k_k[i], k_a[i], r_k[i],
                         lnx_w[i], lnx_b[i], w_out[i], w_up[i], w_down[i], w_ple[i], w_ple_gate[i])
    return x
```

```python
import numpy as np
import ml_dtypes
from contextlib import ExitStack
import concourse.bass as bass
import concourse.mybir as mybir
from concourse.bass_utils import run_bass_kernel_spmd

F32 = mybir.dt.float32
BF16 = mybir.dt.bfloat16
AF = mybir.ActivationFunctionType
ALU = mybir.AluOpType
AX = mybir.AxisListType
NPBF = ml_dtypes.bfloat16

D = 1024
NB = 2
HD = 64
NCOLS = 3096
C_Q, C_KC, C_VC, C_KS, C_VS, C_KW, C_VW, C_G = 0, 512, 640, 768, 896, 1024, 1152, 1280
C_RW = 1304
DECAY_C = 0.6065306597126334
import os
STOP = int(os.environ.get('KSTOP', '99'))
SKIP = os.environ.get('KSKIP', '')
SCHED = os.environ.get('KSCHED', '1') == '1'


class Buf:
    __slots__ = ("name", "w", "rs", "excl", "wx", "_s")

    def __init__(self, name="", excl=False):
        self.name = name
        self.w = None
        self.rs = []
        self.excl = excl
        self.wx = []
        self._s = None


import types
import os


def _freeze(fn, depth=0):
    if not isinstance(fn, types.FunctionType) or fn.__closure__ is None or depth > 3:
        return fn
    cells = []
    for c in fn.__closure__:
        try:
            v = c.cell_contents
        except ValueError:
            cells.append(c)
            continue
        if isinstance(v, types.FunctionType):
            v = _freeze(v, depth + 1)
        cells.append(types.CellType(v))
    return types.FunctionType(fn.__code__, fn.__globals__, fn.__name__, fn.__defaults__, tuple(cells))


COST = {"pe": 0.2, "act": 0.58, "dve": 0.5, "pool": 1.2, "sp": 0.08}
SEM_LAT = 0.12
DMA_LAT = 3.0
SCHED_W = int(os.environ.get("KSCHEDW", "24"))


class Eng:
    def __init__(self, name, h, semi):
        self.name = name
        self.h = h
        self.semi = semi
        self.count = 0
        self.waited = {}
        self.dsems = []
        self.dn = 0


class KB:
    DK = 8

    def __init__(self, nc, es):
        self.nc = nc
        self.es = es
        self.sems = []
        self.E = {}
        for name, h in (("pe", nc.tensor), ("act", nc.scalar), ("dve", nc.vector),
                        ("pool", nc.gpsimd), ("sp", nc.sync)):
            self.E[name] = Eng(name, h, self.newsem("c_" + name))
        for name in ("sp", "pool", "act"):
            e = self.E[name]
            e.dsems = [self.newsem("d_%s%d" % (name, i)) for i in range(self.DK)]
        self.semmax = {}
        self.ninstr = 0
        self.recording = False
        self.units = []
        self.cur_pe = None
        self.gen = 0

    def newsem(self, name):
        s = self.es.enter_context(self.nc.semaphore(name))
        self.sems.append(s)
        return len(self.sems) - 1

    def _wait(self, eng, semi, val):
        if eng.waited.get(semi, 0) >= val:
            return
        eng.h.wait_ge(self.sems[semi], val)
        eng.waited[semi] = val
        self.ninstr += 1

    def _deps(self, eng, r, w):
        for b in r:
            ev = b.w
            if ev is not None:
                if ev[2] is not None and ev[1] > ev[2].count:
                    raise RuntimeError("unsignaled producer for %s" % b.name)
                self._wait(eng, ev[0], ev[1])
            for ev in b.wx:
                self._wait(eng, ev[0], ev[1])
            if b.excl:
                for ev in b.rs:
                    if ev[2] is not eng:
                        self._wait(eng, ev[0], ev[1])
        pe = self.E["pe"]
        for b in w:
            for ev in b.wx:
                self._wait(eng, ev[0], ev[1])
            ev = b.w
            if ev is not None and not (ev[2] is eng and eng is pe):
                if ev[2] is not None and ev[1] > ev[2].count:
                    raise RuntimeError("unsignaled producer (waw) for %s" % b.name)
                self._wait(eng, ev[0], ev[1])
            for ev in b.rs:
                if not (ev[2] is eng and eng is pe):
                    if ev[2] is not None and ev[1] > ev[2].count:
                        raise RuntimeError("unsignaled reader for %s" % b.name)
                    self._wait(eng, ev[0], ev[1])

    def _rec(self, en, item, r, w, sig, cost, is_dma):
        if en == "pe" and self.cur_pe is not None:
            u = self.cur_pe
        else:
            u = dict(eng=en, items=[], deps=set(), cost=0.0, dma=is_dma, idx=len(self.units))
            self.units.append(u)
        ui = u["idx"]
        u["items"].append(item)
        u["cost"] += cost
        g = self.gen
        for b in r:
            st = getattr(b, "_s", None)
            if st is None or st[0] != g:
                st = [g, None, []]
                b._s = st
            if st[1] is not None and st[1] != ui:
                u["deps"].add(st[1])
            if b.excl:
                for x in st[2]:
                    if x != ui:
                        u["deps"].add(x)
        for b in w:
            st = getattr(b, "_s", None)
            if st is None or st[0] != g:
                st = [g, None, []]
                b._s = st
            if st[1] is not None and st[1] != ui:
                u["deps"].add(st[1])
            for x in st[2]:
                if x != ui:
                    u["deps"].add(x)
        for b in r:
            b._s[2].append(ui)
        for b in w:
            b._s[1] = ui
            b._s[2] = []
        if en == "pe":
            self.cur_pe = None if sig else u

    def flush(self):
        units = self.units
        self.units = []
        self.cur_pe = None
        self.gen += 1
        if not units:
            return
        n = len(units)
        pend = {en: [] for en in self.E}
        for u in units:
            pend[u["eng"]].append(u["idx"])
        ptr = {en: 0 for en in self.E}
        emitted = [False] * n
        finish = [0.0] * n
        tfree = {en: 0.0 for en in self.E}
        order = []
        live = [en for en in self.E if pend[en]]
        while len(order) < n:
            best = None
            for en in live:
                lst = pend[en]
                p = ptr[en]
                cnt = 0
                i = p
                tf = tfree[en]
                while i < len(lst) and cnt < SCHED_W:
                    ui = lst[i]
                    i += 1
                    if emitted[ui]:
                        continue
                    cnt += 1
                    u = units[ui]
                    ok = True
                    st = tf
                    for d in u["deps"]:
                        if not emitted[d]:
                            ok = False
                            break
                        f = finish[d] + (SEM_LAT if units[d]["eng"] != en or units[d]["dma"] else 0.03)
                        if f > st:
                            st = f
                    if not ok:
                        continue
                    if best is None or st < best[0] - 1e-9 or (abs(st - best[0]) <= 1e-9 and ui < best[1]):
                        best = (st, ui, en)
                    if st <= tf + 1e-9:
                        break
            st, ui, en = best
            u = units[ui]
            emitted[ui] = True
            if u["dma"]:
                tfree[en] = st + u["cost"]
                finish[ui] = st + DMA_LAT
            else:
                tfree[en] = st + u["cost"]
                finish[ui] = st + u["cost"]
            order.append(ui)
            lst = pend[en]
            while ptr[en] < len(lst) and emitted[lst[ptr[en]]]:
                ptr[en] += 1
            if ptr[en] >= len(lst):
                live.remove(en)
        self.sim_time = getattr(self, "sim_time", 0.0) + max(finish)
        rec = self.recording
        self.recording = False
        for ui in order:
            u = units[ui]
            for it in u["items"]:
                if it[0] == "op":
                    self.op(u["eng"], it[1], it[2], it[3], it[4])
                else:
                    self.dma(u["eng"], it[1], it[2], it[3], it[4], **it[5])
        self.recording = rec

    def op(self, en, fn, r=(), w=(), sig=True, c=None):
        if self.recording:
            self._rec(en, ("op", _freeze(fn), tuple(r), tuple(w), sig), r, w, sig,
                      COST[en] if c is None else c, False)
            return None
        eng = self.E[en]
        self._deps(eng, r, w)
        ins = fn(eng.h)
        self.ninstr += 1
        ticket = eng.count + 1
        if sig:
            ins.then_inc(self.sems[eng.semi], 1)
            eng.count = ticket
            self.semmax[eng.semi] = ticket
        ev = (eng.semi, ticket, eng)
        for b in r:
            b.rs.append(ev)
        for b in w:
            b.w = ev
            b.rs = []
            b.wx = []
        return ins

    def dma(self, qn, out, in_, r=(), w=(), **kw):
        if self.recording:
            self._rec(qn, ("dma", out, in_, tuple(r), tuple(w), kw), r, w, True, 0.08, True)
            return None
        eng = self.E[qn]
        self._deps(eng, r, w)
        i = eng.dn % self.DK
        tgt = 16 * (eng.dn // self.DK + 1)
        semi = eng.dsems[i]
        if tgt > 16:
            self._wait(eng, semi, tgt - 16)
        eng.h.dma_start(out=out, in_=in_, **kw).then_inc(self.sems[semi], 16)
        self.ninstr += 1
        eng.dn += 1
        self.semmax[semi] = tgt
        ev = (semi, tgt, None)
        for b in r:
            b.rs.append(ev)
        for b in w:
            if b.rs or b.w is None or b.w[2] is not None:
                b.wx = []
            else:
                b.wx.append(b.w)
            b.w = ev
            b.rs = []

    def barrier(self, engs=("pe", "act", "dve", "pool", "sp")):
        if self.recording:
            self.flush()
        for en in engs:
            eng = self.E[en]
            for semi, val in self.semmax.items():
                self._wait(eng, semi, val)


class Tl:
    uid = 0

    def __init__(self, kb, name, shape, dt, nbuf=1, psum=False):
        nc = kb.nc
        self.t = []
        self.b = []
        for i in range(nbuf):
            Tl.uid += 1
            nm = "%s_%d_%d" % (name, i, Tl.uid)
            if psum:
                t = kb.es.enter_context(nc.psum_tensor(nm, shape, dt))
            else:
                t = kb.es.enter_context(nc.sbuf_tensor(nm, shape, dt))
            self.t.append(t)
            self.b.append(Buf(nm, excl=psum))
        self.n = nbuf
        self.i = -1

    def nxt(self):
        self.i = (self.i + 1) % self.n
        return self.t[self.i], self.b[self.i]

    def cur(self):
        return self.t[self.i], self.b[self.i]


def _bf(a):
    return np.ascontiguousarray(a.astype(NPBF))


def host_consts(S):
    c = {}
    c["ident_bf"] = _bf(np.eye(128, dtype=np.float32))
    c["ident_f"] = np.eye(128, dtype=np.float32)
    bo = np.zeros((128, 128), np.float32)
    bo[:64, :64] = 1.0
    bo[64:, 64:] = 1.0
    c["blockones"] = bo
    bs = np.zeros((128, 2), np.float32)
    bs[:64, 0] = 1.0
    bs[64:, 1] = 1.0
    c["blocksel"] = _bf(bs)
    rm = np.ones((128, 512), np.float32)
    rm[:, ::64] = 0.0
    c["resetmask"] = rm
    c.update(nsa_consts(S))
    c.update(rwkv_consts())
    return c


def dram(nc, name, shape, dt, kind="Internal"):
    return nc.dram_tensor(name, list(shape), dt, kind=kind).ap()


def phase_a(nc, kb0, S, I, SC):
    with ExitStack() as es:
        kb = kb0
        kb.es_phase = es
        old_es = kb.es
        kb.es = es
        try:
            _phase_a(nc, kb, S, I, SC)
        finally:
            kb.es = old_es


def _phase_a(nc, kb, S, I, SC):
    NST = NB * S // 512
    op, dma = kb.op, kb.dma

    def tl(name, shape, dt, nbuf=1, psum=False):
        return Tl(kb, name, shape, dt, nbuf, psum)

    Wb = tl("Wb", [128, 8, NCOLS], BF16)
    Wbt, Wbb = Wb.nxt()
    gcol = tl("gcol", [128, 8], F32)
    gct, gcb = gcol.nxt()
    dma("sp", gct[:], I["g_mix_pre"].rearrange("o (k p) -> p (o k)", p=128), w=[gcb],
        allow_slow_non_contiguous=True)
    ident = tl("ident", [128, 128], BF16)
    idt, idb = ident.nxt()
    dma("sp", idt[:], I["ident_bf"][:, :], w=[idb])
    bones = tl("bones", [128, 128], F32)
    bot, bob = bones.nxt()
    dma("sp", bot[:], I["blockones"][:, :], w=[bob])
    bsel = tl("bsel", [128, 2], BF16)
    bst, bsb = bsel.nxt()
    dma("sp", bst[:], I["blocksel"][:, :], w=[bsb])
    rmask = tl("rmask", [128, 512], F32)
    rmt, rmb = rmask.nxt()
    dma("sp", rmt[:], I["resetmask"][:, :], w=[rmb])
    mu = tl("mu", [128, 14], F32)
    mut, mub = mu.nxt()
    dma("sp", mut[:], I["shift_mu"].rearrange("o (s p) -> p (o s)", p=128), w=[mub],
        allow_slow_non_contiguous=True)
    omu = tl("omu", [128, 14], F32)
    omut, omub = omu.nxt()
    op("dve", lambda e: e.tensor_scalar(out=omut[:], in0=mut[:], scalar1=-1.0, scalar2=1.0,
                                        op0=ALU.mult, op1=ALU.add), r=[mub], w=[omub])
    cols = {}
    for nm in ("w0", "a0", "k_k", "k_a", "r_k"):
        t = tl("c_" + nm, [128, 4], F32)
        tt, tb = t.nxt()
        src = I[nm]
        if nm == "r_k":
            src = src.rearrange("o h d -> o (h d)")
        dma("sp", tt[:], src.rearrange("o (s p) -> p (o s)", p=128), w=[tb],
            allow_slow_non_contiguous=True)
        cols[nm] = (tt, tb)
    omka = tl("omka", [128, 4], F32)
    omkat, omkab = omka.nxt()
    op("dve", lambda e: e.tensor_scalar(out=omkat[:], in0=cols["k_a"][0][:], scalar1=-1.0, scalar2=1.0,
                                        op0=ALU.mult, op1=ALU.add), r=[cols["k_a"][1]], w=[omkab])
    gbias = tl("gbias", [128, 24], F32)
    gbt, gbb = gbias.nxt()
    dma("sp", gbt[:], I["nsa_gate_bias"].partition_broadcast(128), w=[gbb])
    wlora = tl("wlora", [64, 512], F32)
    wlt, wlb = wlora.nxt()
    dma("sp", wlt[:], I["w_lora_up"][0], w=[wlb])
    alora = tl("alora", [64, 512], F32)
    alt, alb = alora.nxt()
    dma("sp", alt[:], I["a_lora_up"][0], w=[alb])
    glora_f = tl("glora_f", [128, 512], F32)
    glft, glfb = glora_f.nxt()
    dma("sp", glft[:], I["g_lora_up"][0], w=[glfb])
    glora = tl("glora", [128, 512], BF16)
    glt, glb = glora.nxt()
    op("dve", lambda e: e.tensor_copy(glt[:], glft[:]), r=[glfb], w=[glb])

    with ExitStack() as es2:
        old = kb.es
        kb.es = es2
        wst = tl("wst", [128, NCOLS], F32, nbuf=2)
        kb.es = old
        for k in range(8):
            st, sb = wst.nxt()
            dma("sp", st[:], I["w_in"][0, k * 128:(k + 1) * 128, :], w=[sb])
            if k % 2 == 0:
                op("dve", lambda e, st=st, k=k: e.tensor_scalar(out=Wbt[:, k, :], in0=st[:], scalar1=gct[:, k:k + 1],
                                                                 scalar2=None, op0=ALU.mult),
                   r=[sb, gcb], w=[Wbb])
            else:
                op("act", lambda e, st=st, k=k: e.activation(out=Wbt[:, k, :], in_=st[:], func=AF.Copy,
                                                              scale=gct[:, k:k + 1]), r=[sb, gcb], w=[Wbb])
        kb.barrier()
    if STOP == 0:
        return

    xt = tl("xt", [128, 1024], F32, nbuf=4)
    ss = tl("ss", [128, 4], F32, nbuf=2)
    rstd = tl("rstd", [128, 4], F32, nbuf=2)
    xn = tl("xn", [128, 1024], BF16, nbuf=4)
    hT = tl("hT", [128, 8, 512], BF16, nbuf=2)
    tp = tl("tp", [128, 1024], BF16, nbuf=2, psum=True)
    fm = tl("fm", [128, 512], F32, nbuf=3, psum=True)
    tm = tl("tm", [128, 512], F32, nbuf=1, psum=True)
    tr2 = tl("tr2", [128, 512], BF16, nbuf=1, psum=True)
    ssp = tl("ssp", [128, 512], F32, nbuf=1, psum=True)
    carry = tl("carry", [128, 14], F32)
    cat, cab = carry.nxt()
    Z = tl("Z", [128, 513], F32, nbuf=2)
    f32t = {nm: tl("f_" + nm, [128, 512], F32, nbuf=(2 if nm in ("r", "k0", "v") else 1)) for nm in
            ("r", "k0", "v", "sg", "Lp", "Ep", "Em", "En", "a", "kk", "t1", "kkn", "k", "ba", "bt", "kt", "tmp")}
    bft = {nm: tl("b_" + nm, [128, 512], BF16, nbuf=2) for nm in
           ("aT", "bT", "kT", "rT", "BT", "KT", "vT", "rk")}
    osb = tl("osb", [128, 512], BF16, nbuf=2)
    tokmaj = tl("tokmaj", [128, 4, 4, 512], BF16)
    tkt, tkb = tokmaj.nxt()
    tkbs = [[Buf("tk%d%d" % (j, hc)) for hc in range(4)] for j in range(4)]
    lor = tl("lor", [64, 2, 512], F32)
    lot, lob = lor.nxt()
    sgd = tl("sgd", [128, 512], BF16)
    sgt, sgb = sgd.nxt()
    vtok = tl("vtok", [128, 4, 2, 2, 128], BF16)
    vtt, vtb = vtok.nxt()
    op("dve", lambda e: e.memset(vtt[:], 1.0), w=[vtb])
    gtok = tl("gtok", [128, 4, 24], F32)
    gtt, gtb = gtok.nxt()
    gout = tl("gout", [128, 4, 512], BF16)
    got, gob = gout.nxt()
    bon = tl("bon", [128, 4, 8], F32)
    bont, bonb = bon.nxt()
    pcs = tl("pcs", [128, 4, 8], F32)
    pct, pcb = pcs.nxt()
    wraw = tl("wraw", [128, 4, 512], F32)
    wrt, wrb = wraw.nxt()
    araw = tl("araw", [128, 4, 512], F32)
    art, arb = araw.nxt()

    def T(nm):
        return f32t[nm].nxt()

    for st_i in range(NST):
        b = st_i // (S // 512)
        t0 = (st_i % (S // 512)) * 512
        first = (t0 == 0)
        hTt, hTb = hT.nxt()
        sst, ssb = ss.nxt()
        rst, rsb = rstd.nxt()
        xns = []
        xnl = []
        for j in range(4):
            xtt, xtb = xt.nxt()
            dma("sp", xtt[:], I["x"][b, t0 + j * 128:t0 + (j + 1) * 128, :], w=[xtb])
            xnt, xnb = xn.nxt()
            op("act", lambda e, xtt=xtt, j=j: e.activation(out=xnt[:], in_=xtt[:], func=AF.Square,
                                                            accum_out=sst[:, j:j + 1]),
               r=[xtb], w=[xnb, ssb])
            xns.append((xtt, xtb))
            xnl.append((xnt, xnb))
        op("dve", lambda e: e.tensor_scalar(out=rst[:], in0=sst[:], scalar1=1.0 / D, scalar2=1e-6,
                                            op0=ALU.mult, op1=ALU.add), r=[ssb], w=[rsb])
        op("act", lambda e: e.activation(out=rst[:], in_=rst[:], func=AF.Sqrt), r=[rsb], w=[rsb])
        op("dve", lambda e: e.reciprocal(out=rst[:], in_=rst[:]), r=[rsb], w=[rsb])
        for j in range(4):
            xtt, xtb = xns[j]
            xnt, xnb = xnl[j]
            if j % 2 == 0:
                op("dve", lambda e, xtt=xtt, xnt=xnt, j=j: e.tensor_scalar(out=xnt[:], in0=xtt[:],
                                                                            scalar1=rst[:, j:j + 1], scalar2=None,
                                                                            op0=ALU.mult),
                   r=[xtb, rsb], w=[xnb])
            else:
                op("act", lambda e, xtt=xtt, xnt=xnt, j=j: e.activation(out=xnt[:], in_=xtt[:], func=AF.Copy,
                                                                         scale=rst[:, j:j + 1]),
                   r=[xtb, rsb], w=[xnb])
        for m in range(4):
            tpt, tpb = tp.nxt()
            for kk_ in range(2):
                k = 2 * m + kk_
                for j in range(4):
                    xnt, xnb = xnl[j]
                    last = (kk_ == 1 and j == 3)
                    op("pe", lambda e, xnt=xnt, k=k, j=j, kk_=kk_, tpt=tpt: e.transpose(
                        tpt[:, kk_ * 512 + j * 128: kk_ * 512 + (j + 1) * 128],
                        xnt[:, k * 128:(k + 1) * 128], idt[:]),
                       r=[xnb, idb], w=[tpb], sig=last)
            en = "act" if m % 2 == 0 else "dve"
            if en == "act":
                op("act", lambda e, tpt=tpt, m=m: e.copy(out=hTt[:, 2 * m:2 * m + 2, :],
                                                         in_=tpt[:].rearrange("p (a b) -> p a b", b=512)),
                   r=[tpb], w=[hTb])
            else:
                op("dve", lambda e, tpt=tpt, m=m: e.tensor_copy(hTt[:, 2 * m:2 * m + 2, :],
                                                                tpt[:].rearrange("p (a b) -> p a b", b=512)),
                   r=[tpb], w=[hTb])

        if STOP == 1:
            return

        def fm_mm(c0, width):
            pt, pb = fm.nxt()
            for k in range(8):
                op("pe", lambda e, k=k, pt=pt: e.matmul(pt[0:width, :], Wbt[:, k, c0:c0 + width], hTt[:, k, :],
                                                         start=(k == 0), stop=(k == 7)),
                   r=[Wbb, hTb], w=[pb], sig=(k == 7))
            return pt, pb

        for ci, (c0, dst, row0, scale) in enumerate(
                [(C_Q + 128 * i, "QT", 128 * i, 0.125) for i in range(4)] +
                [(C_KC, "KCT", 0, 1.0), (C_VC, "VCT", 0, 1.0), (C_KS, "KST", 0, 1.0), (C_KW, "KWT", 0, 1.0)]):
            pt, pb = fm_mm(c0, 128)
            ot, ob = osb.nxt()
            if ci % 2 == 0:
                op("act", lambda e, pt=pt, ot=ot, scale=scale: e.activation(out=ot[:], in_=pt[:], func=AF.Copy,
                                                                            scale=scale), r=[pb], w=[ob])
            else:
                op("dve", lambda e, pt=pt, ot=ot, scale=scale: e.tensor_scalar(out=ot[:], in0=pt[:], scalar1=scale,
                                                                               scalar2=None, op0=ALU.mult),
                   r=[pb], w=[ob])
            dma("pool", SC[dst][b, row0:row0 + 128, t0:t0 + 512], ot[:], r=[ob])

        if STOP == 2:
            return
        for j in range(4):
            tmt, tmb = tm.nxt()
            for gi, (c0, wd_, o0) in enumerate([(C_VS, 128, 0), (C_VW, 128 if 'n128' in SKIP else 152, 128)]):
                for k in range(8):
                    op("pe", lambda e, k=k, j=j, c0=c0, wd_=wd_, o0=o0: e.matmul(
                        tmt[:, o0:o0 + wd_], hTt[:, k, j * 128:(j + 1) * 128], Wbt[:, k, c0:c0 + wd_],
                        start=(k == 0), stop=(k == 7)),
                       r=[Wbb, hTb], w=[tmb], sig=(k == 7 and gi == 1))
            if 'cp' not in SKIP:
                op("act", lambda e, j=j: e.copy(out=vtt[:, j, :, :, 0:64],
                                                in_=tmt[:, 0:256].rearrange("p (a g d) -> p a g d", a=2, g=2)),
                   r=[tmb], w=[vtb])
            if 'ad' not in SKIP:
                op("act", lambda e, j=j: e.copy(out=gtt[:, j, :], in_=tmt[:, 256:280]), r=[tmb], w=[gtb])
                op("dve", lambda e, j=j: e.tensor_tensor(out=gtt[:, j, :], in0=gtt[:, j, :], in1=gbt[:],
                                                     op=ALU.add), r=[gtb, gbb], w=[gtb])
        if 'sig' not in SKIP:
            op("act", lambda e: e.activation(out=gtt[:], in_=gtt[:], func=AF.Sigmoid), r=[gtb], w=[gtb])
        if 'dv' not in SKIP:
            for sw, nm_ in enumerate(("VSA", "VWA")):
                for g in range(2):
                    dma("pool", SC[nm_][b, g, :, t0 // 128:t0 // 128 + 4, :], vtt[:, :, sw, g, :], r=[vtb])
        if 'dg' not in SKIP:
            dma("pool", SC["GATE"][b, t0:t0 + 512, :].rearrange("(j p) c -> p j c", p=128), gtt[:], r=[gtb])

        if STOP == 3:
            return
        def shift(pt, pb, slot, dst_t, dst_b, eng2="dve"):
            zt, zb = Z.nxt()
            op("act", lambda e: e.copy(out=zt[:, 1:513], in_=pt[:]), r=[pb], w=[zb])
            if first:
                op("dve", lambda e: e.memset(zt[:, 0:1], 0.0), w=[zb])
            else:
                op("dve", lambda e: e.tensor_copy(zt[:, 0:1], cat[:, slot:slot + 1]), r=[cab], w=[zb])
            op("dve", lambda e: e.tensor_copy(cat[:, slot:slot + 1], zt[:, 512:513]), r=[zb], w=[cab])
            op("act", lambda e: e.activation(out=dst_t, in_=zt[:, 1:513], func=AF.Copy,
                                             scale=omut[:, slot:slot + 1]), r=[zb, omub], w=[dst_b])
            op("dve", lambda e: e.scalar_tensor_tensor(out=dst_t, in0=zt[:, 0:512], scalar=mut[:, slot:slot + 1],
                                                      in1=dst_t, op0=ALU.mult, op1=ALU.add),
               r=[zb, mub, dst_b], w=[dst_b])

        pt, pb = fm_mm(C_RW + 1536, 128)
        tmpt, tmpb = T("tmp")
        shift(pt, pb, 12, tmpt[:], tmpb)
        op("act", lambda e: e.activation(out=lot[:, 0, :], in_=tmpt[0:64, :], func=AF.Tanh), r=[tmpb], w=[lob])
        op("dve", lambda e: e.tensor_copy(lot[:, 1, :], tmpt[64:128, :]), r=[tmpb], w=[lob])
        for hc in range(4):
            pt, pb = fm.nxt()
            op("pe", lambda e, pt=pt, hc=hc: e.matmul(pt[:], wlt[:, hc * 128:(hc + 1) * 128], lot[:, 0, :],
                                                       start=True, stop=True), r=[wlb, lob], w=[pb])
            op("act", lambda e, pt=pt, hc=hc: e.activation(out=wrt[:, hc, :], in_=pt[:], func=AF.Sigmoid,
                                                            bias=cols["w0"][0][:, hc:hc + 1]),
               r=[pb, cols["w0"][1]], w=[wrb])
            pt, pb = fm.nxt()
            op("pe", lambda e, pt=pt, hc=hc: e.matmul(pt[:], alt[:, hc * 128:(hc + 1) * 128], lot[:, 1, :],
                                                       start=True, stop=True), r=[alb, lob], w=[pb])
            op("act", lambda e, pt=pt, hc=hc: e.activation(out=art[:, hc, :], in_=pt[:], func=AF.Sigmoid,
                                                            bias=cols["a0"][0][:, hc:hc + 1]),
               r=[pb, cols["a0"][1]], w=[arb])
        pt, pb = fm_mm(C_RW + 1664, 128)
        shift(pt, pb, 13, tmpt[:], tmpb)
        op("act", lambda e: e.activation(out=sgt[:], in_=tmpt[:], func=AF.Sigmoid), r=[tmpb], w=[sgb])
        for j in range(4):
            tmt, tmb = tm.nxt()
            op("pe", lambda e, j=j: e.matmul(tmt[:], sgt[:, j * 128:(j + 1) * 128], glt[:], start=True, stop=True),
               r=[sgb, glb], w=[tmb])
            op("act", lambda e, j=j: e.copy(out=got[:, j, :], in_=tmt[:]), r=[tmb], w=[gob])
        dma("pool", SC["GT"][b, t0:t0 + 512, :].rearrange("(j p) c -> p j c", p=128), got[:], r=[gob])

        if STOP == 4:
            return
        for hc in range(4):
            rt, rb = T("r")
            k0t, k0b = T("k0")
            vt, vb = T("v")
            pt, pb = fm_mm(C_RW + hc * 128, 128)
            shift(pt, pb, hc, rt[:], rb, "pool")
            pt, pb = fm_mm(C_RW + 512 + hc * 128, 128)
            shift(pt, pb, 4 + hc, k0t[:], k0b, "dve")
            pt, pb = fm_mm(C_RW + 1024 + hc * 128, 128)
            shift(pt, pb, 8 + hc, vt[:], vb, "pool")
            Lpt, Lpb = T("Lp")
            op("dve", lambda e: e.tensor_tensor_scan(out=Lpt[:], data0=rmt[:], data1=wrt[:, hc, :], initial=0.0,
                                                     op0=ALU.mult, op1=ALU.add), r=[rmb, wrb], w=[Lpb])
            Ept, Epb = T("Ep")
            Ent, Enb = T("En")
            Emt, Emb = T("Em")
            op("act", lambda e: e.activation(out=Ept[:], in_=Lpt[:], func=AF.Exp, scale=-DECAY_C), r=[Lpb], w=[Epb])
            op("act", lambda e: e.activation(out=Ent[:], in_=Lpt[:], func=AF.Exp, scale=DECAY_C), r=[Lpb], w=[Enb])
            op("pool", lambda e: e.tensor_tensor(out=Emt[:], in0=Lpt[:], in1=wrt[:, hc, :], op=ALU.subtract),
               r=[Lpb, wrb], w=[Emb])
            op("act", lambda e: e.activation(out=Emt[:], in_=Emt[:], func=AF.Exp, scale=-DECAY_C), r=[Emb], w=[Emb])
            op("dve", lambda e: e.tensor_copy(pct[:, hc, :], Ept[:].rearrange("p (c t) -> p c t", t=64)[:, :, 63]),
               r=[Epb], w=[pcb])
            kkt, kkb = T("kk")
            t1t, t1b = T("t1")
            op("dve", lambda e: e.tensor_scalar(out=kkt[:], in0=k0t[:], scalar1=cols["k_k"][0][:, hc:hc + 1],
                                                scalar2=None, op0=ALU.mult), r=[k0b, cols["k_k"][1]], w=[kkb])
            op("act", lambda e: e.activation(out=t1t[:], in_=kkt[:], func=AF.Square), r=[kkb], w=[t1b])
            spt, spb = ssp.nxt()
            op("pe", lambda e: e.matmul(spt[:], bot[:], t1t[:], start=True, stop=True), r=[bob, t1b], w=[spb])
            op("dve", lambda e: e.tensor_scalar(out=t1t[:], in0=spt[:], scalar1=1e-24, scalar2=None, op0=ALU.max),
               r=[spb], w=[t1b])
            op("act", lambda e: e.activation(out=t1t[:], in_=t1t[:], func=AF.Ln), r=[t1b], w=[t1b])
            op("act", lambda e: e.activation(out=t1t[:], in_=t1t[:], func=AF.Exp, scale=-0.5), r=[t1b], w=[t1b])
            kknt, kknb = T("kkn")
            op("dve", lambda e: e.tensor_tensor(out=kknt[:], in0=kkt[:], in1=t1t[:], op=ALU.mult),
               r=[kkb, t1b], w=[kknb])
            kt_, kb_ = T("k")
            op("dve", lambda e: e.tensor_scalar(out=kt_[:], in0=art[:, hc, :], scalar1=cols["k_a"][0][:, hc:hc + 1],
                                                scalar2=omkat[:, hc:hc + 1], op0=ALU.mult, op1=ALU.add),
               r=[arb, cols["k_a"][1], omkab], w=[kb_])
            op("pool", lambda e: e.tensor_tensor(out=kt_[:], in0=kt_[:], in1=k0t[:], op=ALU.mult),
               r=[kb_, k0b], w=[kb_])
            aTt, aTb = bft["aT"].nxt()
            op("dve", lambda e: e.scalar_tensor_tensor(out=aTt[:], in0=kknt[:], scalar=-1.0, in1=Emt[:],
                                                       op0=ALU.mult, op1=ALU.mult), r=[kknb, Emb], w=[aTb])
            bat, bab = T("ba")
            op("pool", lambda e: e.tensor_tensor(out=bat[:], in0=kknt[:], in1=art[:, hc, :], op=ALU.mult),
               r=[kknb, arb], w=[bab])
            btt, btb = T("bt")
            op("dve", lambda e: e.tensor_tensor(out=btt[:], in0=bat[:], in1=Ent[:], op=ALU.mult),
               r=[bab, Enb], w=[btb])
            bTt, bTb = bft["bT"].nxt()
            op("act", lambda e: e.copy(out=bTt[:], in_=btt[:]), r=[btb], w=[bTb])
            pcbc = pct[:, hc, :].unsqueeze(2).to_broadcast([128, 8, 64])
            BTt, BTb = bft["BT"].nxt()
            op("pool", lambda e: e.tensor_tensor(out=BTt[:].rearrange("p (c t) -> p c t", t=64),
                                                in0=btt[:].rearrange("p (c t) -> p c t", t=64), in1=pcbc,
                                                op=ALU.mult), r=[btb, pcb], w=[BTb])
            ktt, ktb = T("kt")
            op("pool", lambda e: e.tensor_tensor(out=ktt[:], in0=kt_[:], in1=Ent[:], op=ALU.mult),
               r=[kb_, Enb], w=[ktb])
            kTt, kTb = bft["kT"].nxt()
            op("act", lambda e: e.copy(out=kTt[:], in_=ktt[:]), r=[ktb], w=[kTb])
            KTt, KTb = bft["KT"].nxt()
            op("pool", lambda e: e.tensor_tensor(out=KTt[:].rearrange("p (c t) -> p c t", t=64),
                                                in0=ktt[:].rearrange("p (c t) -> p c t", t=64), in1=pcbc,
                                                op=ALU.mult), r=[ktb, pcb], w=[KTb])
            rTt, rTb = bft["rT"].nxt()
            op("pool", lambda e: e.tensor_tensor(out=rTt[:], in0=rt[:], in1=Ept[:], op=ALU.mult),
               r=[rb, Epb], w=[rTb])
            vTt, vTb = bft["vT"].nxt()
            op("act", lambda e: e.copy(out=vTt[:], in_=vt[:]), r=[vb], w=[vTb])
            rkt, rkb = bft["rk"].nxt()
            op("dve", lambda e: e.scalar_tensor_tensor(out=rkt[:], in0=rt[:], scalar=cols["r_k"][0][:, hc:hc + 1],
                                                       in1=kt_[:], op0=ALU.mult, op1=ALU.mult),
               r=[rb, cols["r_k"][1], kb_], w=[rkb])
            for kind, (tt_, tb_) in enumerate([(aTt, aTb), (bTt, bTb), (kTt, kTb), (rTt, rTb)]):
                dma("pool", SC["RWF"][b, kind, hc * 128:(hc + 1) * 128, t0:t0 + 512], tt_[:], r=[tb_])
            for j in range(4):
                tmt, tmb = tm.nxt()
                op("pe", lambda e, j=j: e.matmul(tmt[:, 0:2], rkt[:, j * 128:(j + 1) * 128], bst[:],
                                                 start=True, stop=True), r=[rkb, bsb], w=[tmb])
                op("dve", lambda e, j=j: e.tensor_copy(bont[:, j, 2 * hc:2 * hc + 2], tmt[:, 0:2]),
                   r=[tmb], w=[bonb])
                t2t, t2b = tr2.nxt()
                for kind, (tt_, tb_) in enumerate([(aTt, aTb), (BTt, BTb), (KTt, KTb), (vTt, vTb)]):
                    op("pe", lambda e, tt_=tt_, kind=kind, j=j: e.transpose(
                        t2t[:, kind * 128:(kind + 1) * 128], tt_[:, j * 128:(j + 1) * 128], idt[:]),
                       r=[tb_, idb], w=[t2b], sig=(kind == 3))
                en = "act" if j % 2 == 0 else "dve"
                if en == "act":
                    op("act", lambda e, j=j: e.copy(out=tkt[:, j, :, hc * 128:(hc + 1) * 128],
                                                    in_=t2t[:].rearrange("p (a c) -> p a c", c=128)),
                       r=[t2b], w=[tkbs[j][hc]])
                else:
                    op("dve", lambda e, j=j: e.tensor_copy(tkt[:, j, :, hc * 128:(hc + 1) * 128],
                                                           t2t[:].rearrange("p (a c) -> p a c", c=128)),
                       r=[t2b], w=[tkbs[j][hc]])
        for kind in range(4):
            dma("pool", SC["RWT"][b, kind, t0:t0 + 512, :].rearrange("(j p) c -> p j c", p=128),
                tkt[:, :, kind, :], r=[tkbs[j][hc] for j in range(4) for hc in range(4)])
        dma("pool", SC["BON"][b, t0:t0 + 512, :].rearrange("(j p) c -> p j c", p=128), bont[:], r=[bonb])
        for hc in range(4):
            dma("pool", SC["PC"][b, hc * 128:(hc + 1) * 128, t0 // 64:t0 // 64 + 8], pct[:, hc, :], r=[pcb])


def scratch_defs(S):
    return {
        "QT": ([NB, 512, S], BF16), "KCT": ([NB, 128, S], BF16), "VCT": ([NB, 128, S], BF16),
        "KST": ([NB, 128, S], BF16), "KWT": ([NB, 128, S], BF16),
        "VSA": ([NB, 2, 128, S // 128, 128], BF16), "VWA": ([NB, 2, 128, S // 128, 128], BF16), "GATE": ([NB, S, 24], F32),
        "GT": ([NB, S, 512], BF16),
        "RWF": ([NB, 4, 512, S], BF16), "RWT": ([NB, 4, S, 512], BF16),
        "BON": ([NB, S, 8], F32), "PC": ([NB, 512, S // 64], F32),
        "YMIX": ([NB, S, 1024], BF16), "X1": ([NB, S, 1024], F32), "XN": ([NB, S, 1024], BF16),
    }


INPUT_NAMES = ["x", "p", "g_mix_pre", "g_mix_post", "g_mlp_pre", "g_mlp_post", "w_in", "nsa_gate_bias",
               "cmp_pe_k", "cmp_k_w1", "cmp_k_b1", "cmp_k_w2", "cmp_pe_v", "cmp_v_w1", "cmp_v_b1", "cmp_v_w2",
               "shift_mu", "w0", "w_lora_up", "a0", "a_lora_up", "g_lora_up", "k_k", "k_a", "r_k", "lnx_w",
               "lnx_b", "w_out", "w_up", "w_down", "w_ple", "w_ple_gate"]


def run_phase(kb, fn, *args):
    with ExitStack() as es:
        old = kb.es
        kb.es = es
        try:
            fn(kb, *args)
        finally:
            kb.es = old
    kb.barrier()


def load_weight_bf16(kb, tl, Wt, Wb_, src, nk, ncols, gcol=None, stage_cols=None):
    with ExitStack() as es2:
        old = kb.es
        kb.es = es2
        wst = tl("wst", [128, ncols], F32, nbuf=2)
        kb.es = old
        for k in range(nk):
            st, sb = wst.nxt()
            kb.dma("sp", st[:], src[k * 128:(k + 1) * 128, :], w=[sb])
            if gcol is not None:
                if k % 2 == 0:
                    kb.op("dve", lambda e: e.tensor_scalar(out=Wt[:, k, :], in0=st[:], scalar1=gcol[0][:, k:k + 1],
                                                           scalar2=None, op0=ALU.mult), r=[sb, gcol[1]], w=[Wb_])
                else:
                    kb.op("act", lambda e: e.activation(out=Wt[:, k, :], in_=st[:], func=AF.Copy,
                                                        scale=gcol[0][:, k:k + 1]), r=[sb, gcol[1]], w=[Wb_])
            else:
                if k % 2 == 0:
                    kb.op("dve", lambda e: e.tensor_copy(Wt[:, k, :], st[:]), r=[sb], w=[Wb_])
                else:
                    kb.op("act", lambda e: e.copy(out=Wt[:, k, :], in_=st[:]), r=[sb], w=[Wb_])
        kb.barrier()


def rms_finish(kb, sst, ssb, rst, rsb, n):
    if n == 2:
        kb.op("dve", lambda e: e.tensor_tensor(out=rst[:, 0:1], in0=sst[:, 0:1], in1=sst[:, 1:2], op=ALU.add),
              r=[ssb], w=[rsb])
        kb.op("dve", lambda e: e.tensor_scalar(out=rst[:, 0:1], in0=rst[:, 0:1], scalar1=1.0 / D, scalar2=1e-6,
                                               op0=ALU.mult, op1=ALU.add), r=[rsb], w=[rsb])
    else:
        kb.op("dve", lambda e: e.tensor_scalar(out=rst[:, 0:1], in0=sst[:, 0:1], scalar1=1.0 / D, scalar2=1e-6,
                                               op0=ALU.mult, op1=ALU.add), r=[ssb], w=[rsb])
    kb.op("act", lambda e: e.activation(out=rst[:, 0:1], in_=rst[:, 0:1], func=AF.Sqrt), r=[rsb], w=[rsb])
    kb.op("dve", lambda e: e.reciprocal(out=rst[:, 0:1], in_=rst[:, 0:1]), r=[rsb], w=[rsb])


def phase_d1(kb, S, I, SC):
    op, dma = kb.op, kb.dma

    def tl(name, shape, dt, nbuf=1, psum=False):
        return Tl(kb, name, shape, dt, nbuf, psum)

    Wo = tl("Wo", [128, 8, 1024], BF16)
    Wot, Wob = Wo.nxt()
    load_weight_bf16(kb, tl, Wot, Wob, I["w_out"][0], 8, 1024)
    ident = tl("ident", [128, 128], BF16)
    idt, idb = ident.nxt()
    dma("sp", idt[:], I["ident_bf"][:, :], w=[idb])
    gbc = tl("gbc", [128, 1024], F32)
    gbt, gbb = gbc.nxt()
    dma("sp", gbt[:], I["g_mix_post"].partition_broadcast(128), w=[gbb])
    ym = tl("ym", [128, 4, 1024], BF16, nbuf=2)
    yT = tl("yT", [128, 8, 512], BF16, nbuf=2)
    tp = tl("tp", [128, 1024], BF16, nbuf=2, psum=True)
    mm = tl("mm", [128, 512], F32, nbuf=4, psum=True)
    xt = tl("xt", [128, 1024], F32, nbuf=2)
    junk = tl("junk", [128, 512], BF16)
    jt, jb = junk.nxt()
    ss = tl("ss", [128, 2], F32, nbuf=2)
    rstd = tl("rstd", [128, 1], F32, nbuf=2)
    tmp = tl("tmp", [128, 1024], F32, nbuf=2)
    x1 = tl("x1", [128, 1024], F32, nbuf=2)
    xnd = tl("xnd", [128, 1024], BF16, nbuf=2)
    for st_i in range(NB * S // 512):
        b = st_i // (S // 512)
        t0 = (st_i % (S // 512)) * 512
        ymt, ymb = ym.nxt()
        dma("sp", ymt[:], SC["YMIX"][b, t0:t0 + 512, :].rearrange("(j p) c -> p j c", p=128), w=[ymb])
        yTt, yTb = yT.nxt()
        for m in range(4):
            tpt, tpb = tp.nxt()
            for kk_ in range(2):
                k = 2 * m + kk_
                for j in range(4):
                    op("pe", lambda e: e.transpose(tpt[:, kk_ * 512 + j * 128: kk_ * 512 + (j + 1) * 128],
                                                   ymt[:, j, k * 128:(k + 1) * 128], idt[:]),
                       r=[ymb, idb], w=[tpb], sig=(kk_ == 1 and j == 3))
            if m % 2 == 0:
                op("act", lambda e: e.copy(out=yTt[:, 2 * m:2 * m + 2, :],
                                           in_=tpt[:].rearrange("p (a b) -> p a b", b=512)), r=[tpb], w=[yTb])
            else:
                op("dve", lambda e: e.tensor_copy(yTt[:, 2 * m:2 * m + 2, :],
                                                  tpt[:].rearrange("p (a b) -> p a b", b=512)), r=[tpb], w=[yTb])
        for j in range(4):
            xtt, xtb = xt.nxt()
            dma("sp", xtt[:], I["x"][b, t0 + j * 128:t0 + (j + 1) * 128, :], w=[xtb])
            sst, ssb = ss.nxt()
            rst, rsb = rstd.nxt()
            halves = []
            for hf in range(2):
                mt, mb = mm.nxt()
                for k in range(8):
                    op("pe", lambda e: e.matmul(mt[:], yTt[:, k, j * 128:(j + 1) * 128],
                                                Wot[:, k, hf * 512:(hf + 1) * 512], start=(k == 0), stop=(k == 7)),
                       r=[yTb, Wob], w=[mb], sig=(k == 7))
                op("act", lambda e: e.activation(out=jt[:], in_=mt[:], func=AF.Square,
                                                 accum_out=sst[:, hf:hf + 1]), r=[mb], w=[jb, ssb])
                halves.append((mt, mb))
            rms_finish(kb, sst, ssb, rst, rsb, 2)
            tmt, tmb = tmp.nxt()
            for hf in range(2):
                mt, mb = halves[hf]
                op("dve", lambda e: e.scalar_tensor_tensor(out=tmt[:, hf * 512:(hf + 1) * 512], in0=mt[:],
                                                           scalar=rst[:, 0:1], in1=gbt[:, hf * 512:(hf + 1) * 512],
                                                           op0=ALU.mult, op1=ALU.mult),
                   r=[mb, rsb, gbb], w=[tmb])
            x1t, x1b = x1.nxt()
            op("dve", lambda e: e.tensor_tensor(out=x1t[:], in0=tmt[:], in1=xtt[:], op=ALU.add),
               r=[tmb, xtb], w=[x1b])
            dma("pool", SC["X1"][b, t0 + j * 128:t0 + (j + 1) * 128, :], x1t[:], r=[x1b])
            s2t, s2b = ss.nxt()
            r2t, r2b = rstd.nxt()
            xnt, xnb = xnd.nxt()
            op("act", lambda e: e.activation(out=xnt[:], in_=x1t[:], func=AF.Square, accum_out=s2t[:, 0:1]),
               r=[x1b], w=[xnb, s2b])
            rms_finish(kb, s2t, s2b, r2t, r2b, 1)
            op("act", lambda e: e.activation(out=xnt[:], in_=x1t[:], func=AF.Copy, scale=r2t[:, 0:1]),
               r=[x1b, r2b], w=[xnb])
            dma("pool", SC["XN"][b, t0 + j * 128:t0 + (j + 1) * 128, :], xnt[:], r=[xnb])


def phase_d2(kb, S, I, SC, OUT):
    op, dma = kb.op, kb.dma

    def tl(name, shape, dt, nbuf=1, psum=False):
        return Tl(kb, name, shape, dt, nbuf, psum)

    gcol = tl("gcol", [128, 8], F32)
    gct, gcb = gcol.nxt()
    dma("sp", gct[:], I["g_mlp_pre"].rearrange("o (k p) -> p (o k)", p=128), w=[gcb],
        allow_slow_non_contiguous=True)
    Wu = tl("Wu", [128, 8, 4096], BF16)
    Wut, Wub = Wu.nxt()
    Wd = tl("Wd", [128, 32, 1024], BF16)
    Wdt, Wdb = Wd.nxt()
    Wg = tl("Wg", [128, 8, 1024], BF16)
    Wgt, Wgb = Wg.nxt()
    Wp = tl("Wp", [128, 2, 1024], BF16)
    Wpt, Wpb = Wp.nxt()
    load_weight_bf16(kb, tl, Wut, Wub, I["w_up"][0], 8, 4096, gcol=(gct, gcb))
    load_weight_bf16(kb, tl, Wdt, Wdb, I["w_down"][0], 32, 1024)
    load_weight_bf16(kb, tl, Wgt, Wgb, I["w_ple_gate"][0], 8, 1024)
    load_weight_bf16(kb, tl, Wpt, Wpb, I["w_ple"][0], 2, 1024)
    ident = tl("ident", [128, 128], BF16)
    idt, idb = ident.nxt()
    dma("sp", idt[:], I["ident_bf"][:, :], w=[idb])
    gbc = tl("gbc", [128, 1024], F32)
    gbt, gbb = gbc.nxt()
    dma("sp", gbt[:], I["g_mlp_post"].partition_broadcast(128), w=[gbb])

    x1 = tl("x1", [128, 2, 1024], F32, nbuf=1)
    xn = tl("xn", [128, 2, 1024], BF16, nbuf=2)
    hT = tl("hT", [128, 8, 256], BF16, nbuf=2)
    aT = tl("aT", [128, 32, 256], BF16)
    rl = tl("rl", [128, 512], BF16, nbuf=2)
    tmp = tl("tmp", [128, 1024], F32)
    pt_ = tl("pt", [128, 2, 256], F32)
    pb_ = tl("pb", [128, 2, 256], BF16)
    pT = tl("pT", [128, 2, 256], BF16)
    sg = tl("sg", [128, 512], F32, nbuf=1)
    ot = tl("ot", [128, 512], F32, nbuf=1)
    ss = tl("ss", [128, 2], F32, nbuf=2)
    rstd = tl("rstd", [128, 1], F32, nbuf=2)
    tp = tl("tp", [128, 1024], BF16, nbuf=2, psum=True)
    up = tl("up", [128, 512], F32, nbuf=2, psum=True)
    dn = tl("dn", [128, 512], F32, nbuf=2, psum=True)
    gp = tl("gp", [128, 512], F32, nbuf=1, psum=True)
    pp = tl("pp", [128, 512], F32, nbuf=1, psum=True)

    def transposes(srct, srcb, dstt, dstb):
        for m in range(2):
            tpt, tpb = tp.nxt()
            for kk_ in range(4):
                k = 4 * m + kk_
                for j in range(2):
                    op("pe", lambda e: e.transpose(tpt[:, kk_ * 256 + j * 128: kk_ * 256 + (j + 1) * 128],
                                                   srct[:, j, k * 128:(k + 1) * 128], idt[:]),
                       r=[srcb, idb], w=[tpb], sig=(kk_ == 3 and j == 1))
            if m % 2 == 0:
                op("act", lambda e: e.copy(out=dstt[:, 4 * m:4 * m + 4, :],
                                           in_=tpt[:].rearrange("p (a b) -> p a b", b=256)), r=[tpb], w=[dstb])
            else:
                op("dve", lambda e: e.tensor_copy(dstt[:, 4 * m:4 * m + 4, :],
                                                  tpt[:].rearrange("p (a b) -> p a b", b=256)), r=[tpb], w=[dstb])

    for ti in range(NB * S // 256):
        b = ti // (S // 256)
        t0 = (ti % (S // 256)) * 256
        x1t, x1b = x1.nxt()
        dma("sp", x1t[:], SC["X1"][b, t0:t0 + 256, :].rearrange("(j p) c -> p j c", p=128), w=[x1b])
        ptt, ptb = pt_.nxt()
        dma("sp", ptt[:], I["p"][0, b, t0:t0 + 256, :].rearrange("(j p) c -> p j c", p=128), w=[ptb])
        xnt, xnb = xn.nxt()
        dma("sp", xnt[:], SC["XN"][b, t0:t0 + 256, :].rearrange("(j p) c -> p j c", p=128), w=[xnb])
        hTt, hTb = hT.nxt()
        transposes(xnt, xnb, hTt, hTb)
        aTt, aTb = aT.nxt()
        for fp in range(16):
            ut, ub = up.nxt()
            for i in range(2):
                ffc = 2 * fp + i
                for k in range(8):
                    op("pe", lambda e: e.matmul(ut[:, i * 256:(i + 1) * 256], Wut[:, k, ffc * 128:(ffc + 1) * 128],
                                                hTt[:, k, :], start=(k == 0), stop=(k == 7)),
                       r=[Wub, hTb], w=[ub], sig=(k == 7 and i == 1), c=0.12)
            rlt, rlb = rl.nxt()
            op("act", lambda e: e.activation(out=rlt[:], in_=ut[:], func=AF.Relu), r=[ub], w=[rlb])
            op("dve" if fp % 4 != 3 else "pool", lambda e: e.tensor_tensor(
                out=aTt[:, 2 * fp:2 * fp + 2, :], in0=rlt[:].rearrange("p (a b) -> p a b", b=256),
                in1=rlt[:].rearrange("p (a b) -> p a b", b=256), op=ALU.mult), r=[rlb], w=[aTb])
        for j in range(2):
            sst, ssb = ss.nxt()
            rst, rsb = rstd.nxt()
            halves = []
            for hf in range(2):
                dt_, db_ = dn.nxt()
                for ffc in range(32):
                    op("pe", lambda e: e.matmul(dt_[:], aTt[:, ffc, j * 128:(j + 1) * 128],
                                                Wdt[:, ffc, hf * 512:(hf + 1) * 512], start=(ffc == 0),
                                                stop=(ffc == 31)), r=[aTb, Wdb], w=[db_], sig=(ffc == 31))
                jt, jb = rl.nxt()
                op("act", lambda e: e.activation(out=jt[:], in_=dt_[:], func=AF.Square,
                                                 accum_out=sst[:, hf:hf + 1]), r=[db_], w=[jb, ssb])
                halves.append((dt_, db_))
            rms_finish(kb, sst, ssb, rst, rsb, 2)
            tmt, tmb = tmp.nxt()
            for hf in range(2):
                dt_, db_ = halves[hf]
                op("dve", lambda e: e.scalar_tensor_tensor(out=tmt[:, hf * 512:(hf + 1) * 512], in0=dt_[:],
                                                           scalar=rst[:, 0:1], in1=gbt[:, hf * 512:(hf + 1) * 512],
                                                           op0=ALU.mult, op1=ALU.mult),
                   r=[db_, rsb, gbb], w=[tmb])
            op("dve", lambda e: e.tensor_tensor(out=x1t[:, j, :], in0=tmt[:], in1=x1t[:, j, :], op=ALU.add),
               r=[tmb, x1b], w=[x1b])
        xnt, xnb = xn.nxt()
        op("act", lambda e: e.copy(out=xnt[:], in_=x1t[:]), r=[x1b], w=[xnb])
        hTt, hTb = hT.nxt()
        transposes(xnt, xnb, hTt, hTb)
        pbt, pbb = pb_.nxt()
        op("dve", lambda e: e.tensor_copy(pbt[:], ptt[:]), r=[ptb], w=[pbb])
        tpt, tpb = tp.nxt()
        for kc in range(2):
            for j in range(2):
                op("pe", lambda e: e.transpose(tpt[:, kc * 256 + j * 128: kc * 256 + (j + 1) * 128],
                                               pbt[:, j, kc * 128:(kc + 1) * 128], idt[:]),
                   r=[pbb, idb], w=[tpb], sig=(kc == 1 and j == 1))
        pTt, pTb = pT.nxt()
        op("act", lambda e: e.copy(out=pTt[:], in_=tpt[:, 0:512].rearrange("p (a b) -> p a b", b=256)),
           r=[tpb], w=[pTb])
        for j in range(2):
            for hf in range(2):
                gt_, gb_ = gp.nxt()
                for k in range(8):
                    op("pe", lambda e: e.matmul(gt_[:], hTt[:, k, j * 128:(j + 1) * 128],
                                                Wgt[:, k, hf * 512:(hf + 1) * 512], start=(k == 0), stop=(k == 7)),
                       r=[hTb, Wgb], w=[gb_], sig=(k == 7))
                ppt, ppb = pp.nxt()
                for kc in range(2):
                    op("pe", lambda e: e.matmul(ppt[:], pTt[:, kc, j * 128:(j + 1) * 128],
                                                Wpt[:, kc, hf * 512:(hf + 1) * 512], start=(kc == 0), stop=(kc == 1)),
                       r=[pTb, Wpb], w=[ppb], sig=(kc == 1))
                sgt, sgb = sg.nxt()
                op("act", lambda e: e.activation(out=sgt[:], in_=gt_[:], func=AF.Sigmoid), r=[gb_], w=[sgb])
                ott, otb = ot.nxt()
                op("dve", lambda e: e.tensor_tensor(out=ott[:], in0=ppt[:], in1=sgt[:], op=ALU.mult),
                   r=[ppb, sgb], w=[otb])
                op("dve", lambda e: e.tensor_tensor(out=ott[:], in0=ott[:], in1=x1t[:, j, hf * 512:(hf + 1) * 512],
                                                    op=ALU.add), r=[otb, x1b], w=[otb])
                dma("pool", OUT[b, t0 + j * 128:t0 + (j + 1) * 128, hf * 512:(hf + 1) * 512], ott[:], r=[otb])


def nsa_dims(S):
    NCMP = (S - 32) // 16 + 1
    NNT = (NCMP + 127) // 128
    NSEL = S // 64
    return NCMP, NNT, NSEL, min(16, NSEL)


def nsa_consts(S):
    NCMP, NNT, NSEL, TOPK = nsa_dims(S)
    c = {}
    t = np.arange(S)
    qa = np.zeros((8, 4, S), np.float32)
    for h in range(8):
        sl = 2.0 ** (-(h + 1))
        qa[h, 0] = sl
        qa[h, 1] = sl
        qa[h, 2] = -sl * 128.0 * (t // 128)
        qa[h, 3] = -sl * (t % 128)
    c["QAUG"] = _bf(qa)
    ka = np.zeros((4, S), np.float32)
    ka[0] = 128.0 * (t // 128)
    ka[1] = t % 128
    ka[2] = 1.0
    ka[3] = 1.0
    c["KAUGP"] = _bf(ka)
    n = np.arange(NNT * 128)
    pos = 16 * n + 31
    kc = np.zeros((4, NNT * 128), np.float32)
    kc[0] = 128.0 * (pos // 128)
    kc[1] = pos % 128
    kc[2] = 1.0
    kc[3] = 1.0
    c["KAUGC"] = _bf(kc)
    et = np.zeros((128, S), np.float32)
    et[t // 64, t] = 1.0
    c["ETAB"] = _bf(et)
    sm = np.zeros((NNT * 128, 65), np.float32)
    c0 = np.arange(NCMP) * 16
    s0 = np.arange(NSEL) * 64
    ov = (np.minimum(c0[:, None] + 31, s0[None, :] + 63) - np.maximum(c0[:, None], s0[None, :]) + 1)
    sm[:NCMP, :NSEL] = np.clip(ov, 0, None) / 16.0
    sm[:NCMP, 64] = 1.0
    c["SELMAP"] = _bf(sm.reshape(NNT, 128, 65).transpose(1, 0, 2))
    cur = t // 64
    j = np.arange(NSEL)
    forced = (j[None, :] == 0) | (j[None, :] == cur[:, None]) | (j[None, :] == cur[:, None] - 1)
    fut = j[None, :] > cur[:, None]
    add = np.where(forced, 1e4, np.where(fut, -1.0, 0.0)).astype(np.float32)
    mul = np.where(forced | fut, 0.0, 1.0).astype(np.float32)
    c["SCADD"] = np.ascontiguousarray(add.reshape(S // 128, 128, NSEL))
    c["SCMUL"] = np.ascontiguousarray(mul.reshape(S // 128, 128, NSEL))
    return c


def phase_n(kb, S, I, SC):
    op, dma = kb.op, kb.dma
    NCMP, NNT, NSEL, TOPK = nsa_dims(S)
    NQT = S // 512
    NKT = S // 128

    def tl(name, shape, dt, nbuf=1, psum=False):
        return Tl(kb, name, shape, dt, nbuf, psum)

    def one(name, shape, dt, psum=False):
        return tl(name, shape, dt, 1, psum).nxt()

    idf, idfb = one("identf", [128, 128], F32)
    dma("sp", idf[:], I["ident_f"][:, :], w=[idfb])
    idbf, idbfb = one("identb", [128, 128], BF16)
    dma("sp", idbf[:], I["ident_bf"][:, :], w=[idbfb])
    etab, etb = one("etab", [128, S], BF16)
    dma("sp", etab[:], I["ETAB"][:, :], w=[etb])
    selmap, smb = one("selmap", [128, NNT, 65], BF16)
    dma("sp", selmap[:], I["SELMAP"][:, :, :], w=[smb])

    cw = {}
    for kv in ("k", "v"):
        w1f, w1fb = one("w1f" + kv, [64, 32, 128], F32)
        dma("sp", w1f[:], I["cmp_%s_w1" % kv][0].rearrange("(l d) h -> d l h", d=64), w=[w1fb])
        w1b, w1bb = one("w1b" + kv, [64, 32, 128], BF16)
        op("dve", lambda e: e.tensor_copy(w1b[:], w1f[:]), r=[w1fb], w=[w1bb])
        peT, peb = one("peT" + kv, [64, 32], F32)
        dma("sp", peT[:], I["cmp_pe_" + kv][0].rearrange("l d -> d l"), w=[peb], allow_slow_non_contiguous=True)
        b1c, b1b = one("b1c" + kv, [128, 1], F32)
        dma("sp", b1c[:], I["cmp_%s_b1" % kv].rearrange("o h -> h o"), w=[b1b], allow_slow_non_contiguous=True)
        w2f, w2fb = one("w2f" + kv, [128, 64], F32)
        dma("sp", w2f[:], I["cmp_%s_w2" % kv][0], w=[w2fb])
        w2b, w2bb = one("w2b" + kv, [128, 64], BF16)
        op("dve", lambda e: e.tensor_copy(w2b[:], w2f[:]), r=[w2fb], w=[w2bb])
        w2p, w2pb = one("w2p" + kv, [128, 128], BF16)
        op("dve", lambda e: e.memset(w2p[:], 0.0), w=[w2pb])
        op("dve", lambda e: e.tensor_copy(w2p[:, 64:128], w2f[:]), r=[w2fb], w=[w2pb])
        cw[kv] = dict(w1f=(w1f, w1fb), w1b=(w1b, w1bb), peT=(peT, peb), b1=(b1c, b1b), w2b=(w2b, w2bb),
                      w2p=(w2p, w2pb))

    sps = tl("sps", [128, 512], F32, nbuf=3, psum=True)
    ops_ = tl("ops", [128, 512], F32, nbuf=2, psum=True)
    trpt, _ = one("trp", [128, 8, 128], BF16, psum=True)
    _tb = Buf("trp0", excl=True)
    trpbufs = [_tb, _tb]
    fincnt = [0]
    stp = tl("stp", [128, 512], F32, nbuf=1, psum=True)
    ipp = tl("ipp", [128, 4, 65], F32, nbuf=1, psum=True)

    for kv in ("k", "v"):
        pt, pb = sps.nxt()
        w1f, w1fb = cw[kv]["w1f"]
        peT, peb = cw[kv]["peT"]
        for l in range(32):
            op("pe", lambda e: e.matmul(pt[:, 0:1], w1f[:, l, :], peT[:, l:l + 1], start=(l == 0), stop=(l == 31)),
               r=[w1fb, peb], w=[pb], sig=(l == 31))
        bt, btb = one("btot" + kv, [128, 1], F32)
        op("dve", lambda e: e.tensor_tensor(out=bt[:], in0=pt[:, 0:1], in1=cw[kv]["b1"][0][:], op=ALU.add),
           r=[pb, cw[kv]["b1"][1]], w=[btb])
        cw[kv]["bt"] = (bt, btb)

    ksl, kslb = one("ksl", [128, S], BF16)
    kwn, kwnb = one("kwn", [128, S], BF16)
    op("dve", lambda e: e.memset(kwn[:], 0.0), w=[kwnb])
    dma("sp", ksl[0:60, :], I["ETAB"][0:60, :], w=[kslb])
    dma("sp", ksl[60:64, :], I["KAUGP"][:, :], w=[kslb])
    dma("sp", kwn[60:64, :], I["KAUGP"][:, :], w=[kwnb])
    vsl, vslb = one("vsl", [128, NKT, 128], BF16)
    vwn, vwnb = one("vwn", [128, NKT, 128], BF16)
    qa = [one("qa%d" % r, [128, S], BF16) for r in range(4)]
    qasel = [Buf("qasel%d" % r) for r in range(4)]
    for r in range(4):
        op("dve", lambda e: e.memset(qa[r][0][:], 0.0), w=[qa[r][1], qasel[r]])
    kct, kctb = one("kct", [64, S], BF16)
    vct, vctb = one("vct", [64, S], BF16)
    hTk, hTkb = one("hTk", [128, NNT * 128], BF16)
    hTv, hTvb = one("hTv", [128, NNT * 128], BF16)
    op("dve", lambda e: e.memset(hTk[:], 0.0), w=[hTkb])
    op("dve", lambda e: e.memset(hTv[:], 0.0), w=[hTvb])
    kca, kcab = one("kca", [128, NNT * 128], BF16)
    op("dve", lambda e: e.memset(kca[:], 0.0), w=[kcab])
    dma("sp", kca[60:64, :], I["KAUGC"][:, :], w=[kcab])
    vca, vcab = one("vca", [128, NNT, 128], BF16)
    op("dve", lambda e: e.memset(vca[:], 0.0), w=[vcab])
    selbT, selbTb = one("selbT", [128, S], BF16)
    pT = tl("pT", [128, 512], BF16, nbuf=4)
    osb = tl("osb", [128, 512], BF16, nbuf=2)
    gate = tl("gate", [128, 4, 24], F32, nbuf=2)
    imp, impb = one("imp", [128, 4, 64], F32)
    rec = tl("rec", [128, 4], F32, nbuf=2)
    coef = tl("coef", [128, 4], F32, nbuf=2)
    ytl = tl("ytl", [128, 4, 256], F32, nbuf=2)
    ybf = tl("ybf", [128, 4, 256], BF16, nbuf=2)
    scm = tl("scm", [128, NSEL], F32, nbuf=2)
    sca = tl("sca", [128, NSEL], F32, nbuf=2)
    score = tl("score", [128, NSEL], F32, nbuf=2)
    repl = tl("repl", [128, NSEL], F32, nbuf=2)
    mx = tl("mx", [128, 16], F32, nbuf=2)
    sbias = tl("sbias", [128, 128], F32, nbuf=4)
    for i_ in range(4):
        t_, b_ = sbias.nxt()
        op("dve", lambda e: e.memset(t_[:], 0.0), w=[b_])

    for b in range(NB):
        for g in range(2):
            dma("sp", ksl[64:128, :], SC["KST"][b, g * 64:(g + 1) * 64, :], w=[kslb])
            dma("sp", kwn[64:128, :], SC["KWT"][b, g * 64:(g + 1) * 64, :], w=[kwnb])
            dma("sp", vsl[:], SC["VSA"][b, g], w=[vslb])
            dma("sp", vwn[:], SC["VWA"][b, g], w=[vwnb])
            for r in range(4):
                h = 4 * g + r
                dma("sp", qa[r][0][64:128, :], SC["QT"][b, h * 64:(h + 1) * 64, :], w=[qa[r][1]])
                dma("sp", qa[r][0][60:64, :], I["QAUG"][h], w=[qa[r][1]])
            dma("sp", kct[:], SC["KCT"][b, g * 64:(g + 1) * 64, :], w=[kctb])
            dma("sp", vct[:], SC["VCT"][b, g * 64:(g + 1) * 64, :], w=[vctb])
            for kv, src, srcb, hT_, hTb_ in (("k", kct, kctb, hTk, hTkb), ("v", vct, vctb, hTv, hTvb)):
                pt, pb = sps.nxt()
                w1b, w1bb = cw[kv]["w1b"]
                for l in range(32):
                    op("pe", lambda e: e.matmul(pt[:, 0:NCMP], w1b[:, l, :], src[:, l:l + 16 * (NCMP - 1) + 1:16],
                                                start=(l == 0), stop=(l == 31)),
                       r=[w1bb, srcb], w=[pb], sig=(l == 31))
                op("act", lambda e: e.activation(out=hT_[:, 0:NCMP], in_=pt[:, 0:NCMP], func=AF.Gelu_apprx_tanh,
                                                 bias=cw[kv]["bt"][0][:, 0:1]), r=[pb, cw[kv]["bt"][1]], w=[hTb_])
            pt, pb = sps.nxt()
            op("pe", lambda e: e.matmul(pt[:, 0:NCMP], cw["k"]["w2p"][0][:], hTk[:, 0:NCMP], start=True, stop=True),
               r=[cw["k"]["w2p"][1], hTkb], w=[pb])
            op("act", lambda e: e.copy(out=kca[64:128, 0:NCMP], in_=pt[64:128, 0:NCMP]), r=[pb], w=[kcab])
            pt, pb = sps.nxt()
            for nt in range(NNT):
                op("pe", lambda e: e.matmul(pt[:, nt * 64:(nt + 1) * 64], hTv[:, nt * 128:(nt + 1) * 128],
                                            cw["v"]["w2b"][0][:], start=True, stop=True),
                   r=[cw["v"]["w2b"][1], hTvb], w=[pb], sig=(nt == NNT - 1))
            op("act", lambda e: e.copy(out=vca[:, :, 0:64],
                                       in_=pt[:, 0:NNT * 64].rearrange("p (a d) -> p a d", d=64)), r=[pb], w=[vcab])
            for nt in range(NNT):
                nn = min(128, NCMP - nt * 128)
                op("dve", lambda e: e.memset(vca[0:nn, nt, 64:128], 1.0), w=[vcab])

            for qt in range(NQT):
                q0 = qt * 512
                gt, gtb = gate.nxt()
                dma("sp", gt[:], SC["GATE"][b, q0:q0 + 512, :].rearrange("(j p) c -> p j c", p=128), w=[gtb])
                yt, ytb = ytl.nxt()

                def fin(ot_, ob_, r, br):
                    h = 4 * g + r
                    st_, sb_ = osb.nxt()
                    op("dve", lambda e: e.tensor_copy(st_[:], ot_[:]), r=[ob_], w=[sb_])
                    so = 4 * (fincnt[0] % 2)
                    tb_ = trpbufs[fincnt[0] % 2]
                    fincnt[0] += 1
                    tt_ = trpt[:, so:so + 4, :]
                    for sub in range(4):
                        op("pe", lambda e: e.transpose(tt_[:, sub, :], st_[:, sub * 128:(sub + 1) * 128], idbf[:]),
                           r=[sb_, idbfb], w=[tb_], sig=(sub == 3), c=0.08)
                    rt_, rb_ = rec.nxt()
                    op("dve", lambda e: e.tensor_scalar(out=rt_[:], in0=tt_[:, :, 64], scalar1=1e-30, scalar2=None,
                                                        op0=ALU.max), r=[tb_], w=[rb_])
                    op("dve", lambda e: e.reciprocal(out=rt_[:], in_=rt_[:]), r=[rb_], w=[rb_])
                    ct_, cb_ = coef.nxt()
                    op("dve", lambda e: e.tensor_tensor(out=ct_[:], in0=rt_[:], in1=gt[:, :, 3 * h + br], op=ALU.mult),
                       r=[rb_, gtb], w=[cb_])
                    for sub in range(4):
                        if br == 0:
                            op("dve", lambda e: e.tensor_scalar(out=yt[:, sub, r * 64:(r + 1) * 64],
                                                                in0=tt_[:, sub, 0:64], scalar1=ct_[:, sub:sub + 1],
                                                                scalar2=None, op0=ALU.mult),
                               r=[tb_, cb_], w=[ytb])
                        else:
                            op("dve", lambda e: e.scalar_tensor_tensor(out=yt[:, sub, r * 64:(r + 1) * 64],
                                                                       in0=tt_[:, sub, 0:64],
                                                                       scalar=ct_[:, sub:sub + 1],
                                                                       in1=yt[:, sub, r * 64:(r + 1) * 64],
                                                                       op0=ALU.mult, op1=ALU.add),
                               r=[tb_, cb_, ytb], w=[ytb])

                nts = [nt for nt in range(NNT) if 16 * (nt * 128) + 31 <= q0 + 511]
                for r in range(4):
                    ot_, ob_ = ops_.nxt()
                    pts = []
                    for ii, nt in enumerate(nts):
                        st_, sb_ = sps.nxt()
                        op("pe", lambda e: e.matmul(st_[:], kca[:, nt * 128:(nt + 1) * 128], qa[r][0][:, q0:q0 + 512],
                                                    start=True, stop=True), r=[kcab, qa[r][1]], w=[sb_])
                        pt_, pb_ = pT.nxt()
                        op("act", lambda e: e.activation(out=pt_[:], in_=st_[:], func=AF.Exp), r=[sb_], w=[pb_])
                        if not (16 * (nt * 128 + 127) + 31 <= q0):
                            op("pool", lambda e: e.affine_select(out=pt_[:], in_=pt_[:], pattern=[[1, 512]],
                                                                 compare_op=ALU.is_ge, fill=0.0,
                                                                 base=q0 - 16 * nt * 128 - 31,
                                                                 channel_multiplier=-16), r=[pb_], w=[pb_], c=0.6)
                        op("pe", lambda e: e.matmul(ot_[:], vca[:, nt, :], pt_[:], start=(ii == 0),
                                                    stop=(ii == len(nts) - 1)),
                           r=[vcab, pb_], w=[ob_])
                        pts.append((pt_, pb_))
                    it_, ib_ = ipp.nxt()
                    for sub in range(4):
                        for ii, nt in enumerate(nts):
                            op("pe", lambda e: e.matmul(it_[:, sub, :], pts[ii][0][:, sub * 128:(sub + 1) * 128],
                                                        selmap[:, nt, :], start=(ii == 0), stop=(ii == len(nts) - 1)),
                               r=[pts[ii][1], smb], w=[ib_], sig=(sub == 3 and ii == len(nts) - 1))
                    rt_, rb_ = rec.nxt()
                    op("dve", lambda e: e.tensor_scalar(out=rt_[:], in0=it_[:, :, 64], scalar1=1e-30, scalar2=None,
                                                        op0=ALU.max), r=[ib_], w=[rb_])
                    op("dve", lambda e: e.reciprocal(out=rt_[:], in_=rt_[:]), r=[rb_], w=[rb_])
                    for sub in range(4):
                        if r == 0:
                            op("dve", lambda e: e.tensor_scalar(out=imp[:, sub, :], in0=it_[:, sub, 0:64],
                                                                scalar1=rt_[:, sub:sub + 1], scalar2=None,
                                                                op0=ALU.mult), r=[ib_, rb_], w=[impb])
                        else:
                            op("dve", lambda e: e.scalar_tensor_tensor(out=imp[:, sub, :], in0=it_[:, sub, 0:64],
                                                                       scalar=rt_[:, sub:sub + 1], in1=imp[:, sub, :],
                                                                       op0=ALU.mult, op1=ALU.add),
                               r=[ib_, rb_, impb], w=[impb])
                    fin(ot_, ob_, r, 0)
                def stream(r, br, ktile, vtile, ktb, vtb, tiles, fold):
                    ot_, ob_ = ops_.nxt()
                    for ii, (kt, qlo, qhi, selk) in enumerate(tiles):
                        k0 = kt * 128
                        c0, c1 = qlo - q0, qhi - q0
                        nq = qhi - qlo
                        cc = 0.08 + 0.12 * nq / 512.0
                        st_, sb_ = sps.nxt()
                        rb_ = [ktb, qa[r][1]] + ([qasel[r]] if fold else [])
                        if fold and kt * 2 + 1 >= 60:
                            op("pe", lambda e: e.matmul(st_[:, c0:c1], ktile[:, k0:k0 + 128], qa[r][0][:, qlo:qhi],
                                                        start=True, stop=False), r=rb_, w=[sb_], sig=False, c=cc)
                            op("pe", lambda e: e.matmul(st_[:, c0:c1], etab[:, k0:k0 + 128], selbT[:, qlo:qhi],
                                                        start=False, stop=True), r=[etb, selbTb], w=[sb_], c=cc)
                        else:
                            op("pe", lambda e: e.matmul(st_[:, c0:c1], ktile[:, k0:k0 + 128], qa[r][0][:, qlo:qhi],
                                                        start=True, stop=True), r=rb_, w=[sb_], c=cc)
                        pt_, pb_ = pT.nxt()
                        op("act", lambda e: e.activation(out=pt_[:, c0:c1], in_=st_[:, c0:c1], func=AF.Exp),
                           r=[sb_], w=[pb_], c=0.12 + 0.45 * nq / 512.0)
                        if selk == 1:
                            op("pool", lambda e: e.affine_select(out=pt_[:, c0:c1], in_=pt_[:, c0:c1],
                                                                 pattern=[[1, nq]], compare_op=ALU.is_ge, fill=0.0,
                                                                 base=qlo - k0, channel_multiplier=-1),
                               r=[pb_], w=[pb_], c=0.15 + 0.45 * nq / 512.0)
                        elif selk == 2:
                            op("pool", lambda e: e.affine_select(out=pt_[:, c0:c1], in_=pt_[:, c0:c1],
                                                                 pattern=[[-1, nq]], compare_op=ALU.is_ge, fill=0.0,
                                                                 base=k0 - qlo + 511, channel_multiplier=1),
                               r=[pb_], w=[pb_], c=0.15 + 0.45 * nq / 512.0)
                        op("pe", lambda e: e.matmul(ot_[:, c0:c1], vtile[:, kt, :], pt_[:, c0:c1], start=(ii == 0),
                                                    stop=(ii == len(tiles) - 1)),
                           r=[vtb, pb_], w=[ob_], c=cc)
                    fin(ot_, ob_, r, br)

                def win_tiles():
                    full, part = [], []
                    for kt in range(max(0, (q0 - 512) // 128), (q0 + 511) // 128 + 1):
                        k0 = kt * 128
                        if k0 >= q0:
                            t_ = (kt, max(q0, k0), q0 + 512, 1)
                        else:
                            t_ = (kt, q0, min(q0 + 512, k0 + 128 + 511), 2)
                        (full if t_[2] - t_[1] == 512 else part).append(t_)
                    return full + part

                def sel_tiles():
                    full, part = [], []
                    for kt in range(0, (q0 + 511) // 128 + 1):
                        k0 = kt * 128
                        if k0 + 127 > q0:
                            t_ = (kt, max(q0, k0), q0 + 512, 1)
                        else:
                            t_ = (kt, q0, q0 + 512, 0)
                        (full if t_[2] - t_[1] == 512 else part).append(t_)
                    return full + part

                for r in range(4):
                    stream(r, 2, kwn, vwn, kwnb, vwnb, win_tiles(), False)
                spt, spb = stp.nxt()
                for sub in range(4):
                    q128 = qt * 4 + sub
                    mt_, mb_ = scm.nxt()
                    at_, ab_ = sca.nxt()
                    dma("sp", mt_[:], I["SCMUL"][q128], w=[mb_])
                    dma("sp", at_[:], I["SCADD"][q128], w=[ab_])
                    sct, scb = score.nxt()
                    op("dve", lambda e: e.tensor_tensor(out=sct[:], in0=imp[:, sub, 0:NSEL], in1=mt_[:], op=ALU.mult),
                       r=[impb, mb_], w=[scb])
                    op("dve", lambda e: e.tensor_tensor(out=sct[:], in0=sct[:], in1=at_[:], op=ALU.add),
                       r=[scb, ab_], w=[scb])
                    mxt, mxb = mx.nxt()
                    op("dve", lambda e: e.max(out=mxt[:, 0:8], in_=sct[:]), r=[scb], w=[mxb])
                    if TOPK == 16:
                        rpt, rpb = repl.nxt()
                        op("dve", lambda e: e.match_replace(out=rpt[:], in_to_replace=mxt[:, 0:8], in_values=sct[:],
                                                            imm_value=-1e30), r=[mxb, scb], w=[rpb])
                        op("dve", lambda e: e.max(out=mxt[:, 8:16], in_=rpt[:]), r=[rpb], w=[mxb])
                        thr = mxt[:, 15:16]
                    else:
                        thr = mxt[:, 7:8]
                    sbt, sbb = sbias.nxt()
                    op("dve", lambda e: e.tensor_scalar(out=sbt[:, 0:NSEL], in0=sct[:], scalar1=thr, scalar2=-30000.0,
                                                        op0=ALU.is_lt, op1=ALU.mult), r=[scb, mxb], w=[sbb])
                    op("pe", lambda e: e.transpose(spt[:, sub * 128:(sub + 1) * 128], sbt[:], idf[:]),
                       r=[sbb, idfb], w=[spb], sig=(sub == 3))
                op("act", lambda e: e.copy(out=selbT[:, q0:q0 + 512], in_=spt[:]), r=[spb], w=[selbTb])
                for r in range(4):
                    if r % 2 == 0:
                        op("dve", lambda e: e.tensor_copy(qa[r][0][0:60, q0:q0 + 512], spt[0:60, :]),
                           r=[spb], w=[qasel[r]])
                    else:
                        op("act", lambda e: e.copy(out=qa[r][0][0:60, q0:q0 + 512], in_=spt[0:60, :]),
                           r=[spb], w=[qasel[r]])
                for r in range(4):
                    stream(r, 1, ksl, vsl, kslb, vslb, sel_tiles(), True)
                ybt, ybb = ybf.nxt()
                op("act", lambda e: e.copy(out=ybt[:], in_=yt[:]), r=[ytb], w=[ybb])
                dma("pool", SC["YMIX"][b, q0:q0 + 512, g * 256:(g + 1) * 256].rearrange("(j p) c -> p j c", p=128),
                    ybt[:], r=[ybb])


def rwkv_consts():
    c = {}
    p = np.arange(128) % 64
    t = np.arange(64)
    c["MU_S"] = (p[:, None] < t[None, :]).astype(np.float32)
    c["MU_I"] = (p[:, None] <= t[None, :]).astype(np.float32)
    c["ML_S"] = (p[:, None] > t[None, :]).astype(np.float32)
    idm = (p[:, None] == t[None, :]).astype(np.float32)
    c["ID64F"] = np.ascontiguousarray(np.broadcast_to(idm[:, None, :], (128, 8, 64))).astype(np.float32)
    c["ID64B"] = _bf(c["ID64F"])
    return c


def phase_r(kb, S, I, SC):
    op, dma = kb.op, kb.dma
    MC = 4
    NMAC = S // (64 * MC)
    NCH = S // 64

    def tl(name, shape, dt, nbuf=1, psum=False):
        return Tl(kb, name, shape, dt, nbuf, psum)

    def one(name, shape, dt, psum=False):
        return tl(name, shape, dt, 1, psum).nxt()

    masks = {}
    for nm in ("MU_S", "MU_I", "ML_S"):
        t_, b_ = one(nm, [128, 64], F32)
        dma("sp", t_[:], I[nm][:, :], w=[b_])
        masks[nm] = (t_, b_)
    idf, idfb = one("id64f", [128, 8, 64], F32)
    dma("sp", idf[:], I["ID64F"][:, :, :], w=[idfb])
    idb, idbb = one("id64b", [128, 8, 64], BF16)
    dma("sp", idb[:], I["ID64B"][:, :, :], w=[idbb])
    lnw, lnwb = one("lnw", [128, 512], F32)
    dma("sp", lnw[:], I["lnx_w"].partition_broadcast(128), w=[lnwb])
    lnb, lnbb = one("lnb", [128, 512], F32)
    dma("sp", lnb[:], I["lnx_b"].partition_broadcast(128), w=[lnbb])

    FM = tl("FM", [128, 4, 8, MC * 64], BF16, nbuf=3)
    TM = tl("TM", [128, 4, MC, 512], BF16, nbuf=3)
    PCt = tl("PCt", [128, 8, MC], F32, nbuf=3)
    GTt = tl("GTt", [128, MC, 512], BF16, nbuf=3)
    BONt = tl("BONt", [128, MC, 8], F32, nbuf=3)
    ps = tl("ps", [128, 512], F32, nbuf=8, psum=True)

    def bt(name, nbuf=3):
        return tl(name, [128, 8, 64], BF16, nbuf=nbuf)

    NT, NN, ARB, ARK, AAK = bt("NT", 4), bt("NN", 4), bt("ARB", 4), bt("ARK", 4), bt("AAK", 4)
    Xs, Ys, Pbs = bt("Xs", 4), bt("Ys", 4), bt("Pbs", 5)
    Pm = tl("Pm", [128, 8, 64], F32, nbuf=4)
    WT, Gs, Ub = bt("WT", 4), bt("Gs", 4), bt("Ub", 3)
    Xl = tl("Xl", [128, 8, 64], F32, nbuf=4)
    M, Mb_ = one("M", [128, 8, 64], F32)
    Mbf, Mbfb = one("Mbf", [128, 8, 64], BF16)
    op("dve", lambda e: e.memset(M[:], 0.0), w=[Mb_])
    op("dve", lambda e: e.memset(Mbf[:], 0.0), w=[Mbfb])
    Mtmp = tl("Mtmp", [128, 8, 64], F32, nbuf=2)
    yt = tl("yt", [128, 8, 64], F32, nbuf=4)
    g1 = tl("g1", [128, 8, 64], F32, nbuf=2)
    g2 = tl("g2", [128, 8, 64], F32, nbuf=2)
    st1 = tl("st1", [128, 8], F32, nbuf=2)
    st2 = tl("st2", [128, 8], F32, nbuf=2)
    st3 = tl("st3", [128, 8], F32, nbuf=2)
    yo = tl("yo", [128, 8, 64], BF16, nbuf=3)

    macro = {}

    def load_macro(m):
        t0 = m * MC * 64
        fm, fmb = FM.nxt()
        tm, tmb = TM.nxt()
        pc, pcb = PCt.nxt()
        gt, gtb = GTt.nxt()
        bo, bob = BONt.nxt()
        for b in range(NB):
            ph = slice(b * 64, (b + 1) * 64)
            for kind in range(4):
                dma("sp", fm[ph, kind, :, :],
                    SC["RWF"][b, kind].rearrange("(h k) t -> k h t", k=64)[:, :, t0:t0 + MC * 64], w=[fmb])
                dma("sp", tm[ph, kind, :, :],
                    SC["RWT"][b, kind, t0:t0 + MC * 64, :].rearrange("(c t) ch -> t c ch", t=64), w=[tmb])
            dma("sp", pc[ph, :, :], SC["PC"][b].rearrange("(h k) c -> k h c", k=64)[:, :, m * MC:(m + 1) * MC],
                w=[pcb], allow_slow_non_contiguous=True)
            dma("sp", gt[ph, :, :], SC["GT"][b, t0:t0 + MC * 64, :].rearrange("(c t) ch -> t c ch", t=64), w=[gtb])
            dma("sp", bo[ph, :, :], SC["BON"][b, t0:t0 + MC * 64, :].rearrange("(c t) ch -> t c ch", t=64), w=[bob])
        macro[m] = dict(fm=(fm, fmb), tm=(tm, tmb), pc=(pc, pcb), gt=(gt, gtb), bo=(bo, bob))

    units = [(b, h) for h in range(8) for b in range(NB)]
    pre = {}

    def mm_all(dst, dstb, lhs_fn, rhs_fn, rbufs):
        for i, (b, h) in enumerate(units):
            ph = slice(b * 64, (b + 1) * 64)
            op("pe", lambda e: e.matmul(dst[ph, h * 64:(h + 1) * 64], lhs_fn(ph, h), rhs_fn(ph, h),
                                        start=True, stop=True), r=rbufs, w=[dstb], sig=(i == len(units) - 1), c=0.055)

    def v3(t_):
        return t_[:].rearrange("p (h c) -> p h c", c=64)

    def stage_p(c):
        m, cl = c // MC, c % MC
        fm, fmb = macro[m]["fm"]
        tm, tmb = macro[m]["tm"]
        cs = slice(cl * 64, (cl + 1) * 64)
        outs = {}
        for lk, rk, tile_, mask in ((1, 0, NT, "MU_S"), (1, 3, ARB, "MU_I"), (2, 0, AAK, "MU_S"),
                                    (2, 3, ARK, "MU_I"), (0, 1, NN, "ML_S")):
            pt, pb = ps.nxt()
            mm_all(pt, pb, lambda ph, h: fm[ph, lk, h, cs], lambda ph, h: fm[ph, rk, h, cs], [fmb])
            ot, ob = tile_.nxt()
            mk, mkb = masks[mask]
            op("dve", lambda e: e.tensor_tensor(out=ot[:], in0=v3(pt), in1=mk[:].unsqueeze(1).to_broadcast([128, 8, 64]),
                                                op=ALU.mult), r=[pb, mkb], w=[ob])
            outs[tile_] = (ot, ob)
        yield
        X, Xb = outs[NT]
        Y, Yb = outs[NN]
        Pb, Pbb = idb, idbb
        P, Pmb = Pm.nxt()
        op("dve", lambda e: e.tensor_copy(P[:], idf[:]), r=[idfb], w=[Pmb])
        for i in range(6):
            last = (i == 5)
            Bp, Bpb = ps.nxt()
            mm_all(Bp, Bpb, lambda ph, h: Y[ph, h, :], lambda ph, h: Pb[ph, h, :], [Yb, Pbb])
            if not last:
                Ap, Apb = ps.nxt()
                mm_all(Ap, Apb, lambda ph, h: Y[ph, h, :], lambda ph, h: X[ph, h, :], [Yb, Xb])
                Cp, Cpb = ps.nxt()
                mm_all(Cp, Cpb, lambda ph, h: X[ph, h, :], lambda ph, h: Y[ph, h, :], [Yb, Xb])
            op("dve", lambda e: e.tensor_tensor(out=P[:], in0=v3(Bp), in1=P[:], op=ALU.add), r=[Bpb, Pmb], w=[Pmb])
            Pn, Pnb = Pbs.nxt()
            op("act", lambda e: e.copy(out=Pn[:], in_=P[:]), r=[Pmb], w=[Pnb])
            Pb, Pbb = Pn, Pnb
            if not last:
                Xn, Xnb = Xs.nxt()
                op("dve", lambda e: e.tensor_copy(Xn[:], v3(Ap)), r=[Apb], w=[Xnb])
                Yn, Ynb = Ys.nxt()
                op("act", lambda e: e.copy(out=Yn[:], in_=v3(Cp)), r=[Cpb], w=[Ynb])
                X, Xb, Y, Yb = Xn, Xnb, Yn, Ynb
            yield
        TT, TTb = Pb, Pbb
        Wp, Wpb = ps.nxt()
        mm_all(Wp, Wpb, lambda ph, h: tm[ph, 0, cl, h * 64:(h + 1) * 64], lambda ph, h: TT[ph, h, :], [tmb, TTb])
        Gp, Gpb = ps.nxt()
        aak, aakb = outs[AAK]
        mm_all(Gp, Gpb, lambda ph, h: aak[ph, h, :], lambda ph, h: tm[ph, 3, cl, h * 64:(h + 1) * 64], [aakb, tmb])
        wt, wtb = WT.nxt()
        op("act", lambda e: e.copy(out=wt[:], in_=v3(Wp)), r=[Wpb], w=[wtb])
        gs, gsb = Gs.nxt()
        op("dve", lambda e: e.tensor_copy(gs[:], v3(Gp)), r=[Gpb], w=[gsb])
        yield
        Xp, Xpb = ps.nxt()
        mm_all(Xp, Xpb, lambda ph, h: TT[ph, h, :], lambda ph, h: gs[ph, h, :], [TTb, gsb])
        xl, xlb = Xl.nxt()
        op("act", lambda e: e.copy(out=xl[:], in_=v3(Xp)), r=[Xpb], w=[xlb])
        pre[c] = dict(wt=(wt, wtb), xl=(xl, xlb), arb=outs[ARB], ark=outs[ARK])
        yield

    ych = {}

    def stage_q(c):
        m, cl = c // MC, c % MC
        fm, fmb = macro[m]["fm"]
        tm, tmb = macro[m]["tm"]
        pc, pcb = macro[m]["pc"]
        cs = slice(cl * 64, (cl + 1) * 64)
        pr = pre.pop(c)
        wt, wtb = pr["wt"]
        xl, xlb = pr["xl"]
        arb, arbb = pr["arb"]
        ark, arkb = pr["ark"]
        Up, Upb = ps.nxt()
        mm_all(Up, Upb, lambda ph, h: wt[ph, h, :], lambda ph, h: Mbf[ph, h, :], [wtb, Mbfb])
        ub, ubb = Ub.nxt()
        op("dve", lambda e: e.tensor_tensor(out=ub[:], in0=v3(Up), in1=xl[:], op=ALU.add), r=[Upb, xlb], w=[ubb])
        mt_, mtb_ = Mtmp.nxt()
        op("pool", lambda e: e.tensor_tensor(out=mt_[:], in0=M[:],
                                             in1=pc[:, :, cl].unsqueeze(2).to_broadcast([128, 8, 64]), op=ALU.mult),
           r=[Mb_, pcb], w=[mtb_])
        yield
        Yp, Ypb = ps.nxt()
        Mp, Mpb = ps.nxt()
        for i, (b, h) in enumerate(units):
            ph = slice(b * 64, (b + 1) * 64)
            hs = slice(h * 64, (h + 1) * 64)
            op("pe", lambda e: e.matmul(Yp[ph, hs], fm[ph, 3, h, cs], Mbf[ph, h, :], start=True, stop=False),
               r=[fmb, Mbfb], w=[Ypb], sig=False, c=0.055)
            op("pe", lambda e: e.matmul(Yp[ph, hs], ark[ph, h, :], tm[ph, 3, cl, hs], start=False, stop=False),
               r=[arkb, tmb], w=[Ypb], sig=False, c=0.055)
            op("pe", lambda e: e.matmul(Yp[ph, hs], arb[ph, h, :], ub[ph, h, :], start=False, stop=True),
               r=[arbb, ubb], w=[Ypb], sig=(i == len(units) - 1), c=0.055)
        for i, (b, h) in enumerate(units):
            ph = slice(b * 64, (b + 1) * 64)
            hs = slice(h * 64, (h + 1) * 64)
            op("pe", lambda e: e.matmul(Mp[ph, hs], tm[ph, 2, cl, hs], tm[ph, 3, cl, hs], start=True, stop=False),
               r=[tmb], w=[Mpb], sig=False, c=0.055)
            op("pe", lambda e: e.matmul(Mp[ph, hs], tm[ph, 1, cl, hs], ub[ph, h, :], start=False, stop=True),
               r=[tmb, ubb], w=[Mpb], sig=(i == len(units) - 1), c=0.055)
        op("dve", lambda e: e.tensor_tensor(out=M[:], in0=v3(Mp), in1=mt_[:], op=ALU.add), r=[Mpb, mtb_], w=[Mb_])
        op("act", lambda e: e.copy(out=Mbf[:], in_=M[:]), r=[Mb_], w=[Mbfb])
        y_, yb_ = yt.nxt()
        op("act", lambda e: e.copy(out=y_[:], in_=v3(Yp)), r=[Ypb], w=[yb_])
        ych[c] = (y_, yb_)
        yield

    def stage_g(c):
        m, cl = c // MC, c % MC
        tm, tmb = macro[m]["tm"]
        gt, gtb = macro[m]["gt"]
        bo, bob = macro[m]["bo"]
        y_, yb_ = ych.pop(c)
        s1, s1b = st1.nxt()
        s2, s2b = st2.nxt()
        s3, s3b = st3.nxt()
        a1, a1b = g1.nxt()
        a2, a2b = g2.nxt()

        def bc(t_):
            return t_[:].unsqueeze(2).to_broadcast([128, 8, 64])
        op("dve", lambda e: e.tensor_reduce(out=s1[:], in_=y_[:], axis=AX.X, op=ALU.add), r=[yb_], w=[s1b])
        op("act", lambda e: e.activation(out=a1[:], in_=y_[:], func=AF.Square), r=[yb_], w=[a1b])
        op("dve", lambda e: e.tensor_reduce(out=s2[:], in_=a1[:], axis=AX.X, op=ALU.add), r=[a1b], w=[s2b])
        op("dve", lambda e: e.tensor_scalar(out=s1[:], in0=s1[:], scalar1=1.0 / 64, scalar2=None, op0=ALU.mult),
           r=[s1b], w=[s1b])
        op("dve", lambda e: e.tensor_tensor(out=s3[:], in0=s1[:], in1=s1[:], op=ALU.mult), r=[s1b], w=[s3b])
        op("dve", lambda e: e.scalar_tensor_tensor(out=s2[:], in0=s2[:], scalar=1.0 / 64, in1=s3[:], op0=ALU.mult,
                                                   op1=ALU.subtract), r=[s2b, s3b], w=[s2b])
        op("dve", lambda e: e.tensor_scalar(out=s2[:], in0=s2[:], scalar1=64e-5, scalar2=None, op0=ALU.add),
           r=[s2b], w=[s2b])
        op("act", lambda e: e.activation(out=s2[:], in_=s2[:], func=AF.Sqrt), r=[s2b], w=[s2b])
        op("dve", lambda e: e.reciprocal(out=s2[:], in_=s2[:]), r=[s2b], w=[s2b])
        op("pool", lambda e: e.tensor_tensor(out=a1[:], in0=y_[:], in1=bc(s1), op=ALU.subtract),
           r=[yb_, s1b], w=[a1b])
        op("pool", lambda e: e.tensor_tensor(out=a1[:], in0=a1[:], in1=bc(s2), op=ALU.mult), r=[a1b, s2b], w=[a1b])
        op("pool", lambda e: e.tensor_tensor(out=a1[:], in0=a1[:], in1=lnw[:].rearrange("p (h c) -> p h c", c=64),
                                             op=ALU.mult), r=[a1b, lnwb], w=[a1b])
        op("pool", lambda e: e.tensor_tensor(out=a1[:], in0=a1[:], in1=lnb[:].rearrange("p (h c) -> p h c", c=64),
                                             op=ALU.add), r=[a1b, lnbb], w=[a1b])
        op("dve", lambda e: e.tensor_tensor(out=a2[:], in0=tm[:, 3, cl, :].rearrange("p (h c) -> p h c", c=64),
                                            in1=bc(bo[:, cl, :]) if False else bo[:, cl, :].unsqueeze(2).to_broadcast([128, 8, 64]),
                                            op=ALU.mult), r=[tmb, bob], w=[a2b])
        op("pool", lambda e: e.tensor_tensor(out=a1[:], in0=a1[:], in1=a2[:], op=ALU.add), r=[a1b, a2b], w=[a1b])
        o_, ob_ = yo.nxt()
        op("dve", lambda e: e.tensor_tensor(out=o_[:], in0=a1[:], in1=gt[:, cl, :].rearrange("p (h c) -> p h c", c=64),
                                            op=ALU.mult), r=[a1b, gtb], w=[ob_])
        for b in range(NB):
            ph = slice(b * 64, (b + 1) * 64)
            dma("pool", SC["YMIX"][b, c * 64:(c + 1) * 64, 512:1024], o_[ph].rearrange("p h c -> p (h c)"), r=[ob_])
        yield

    def run_all(gens):
        gens = list(gens)
        while gens:
            nxt_ = []
            for g_ in gens:
                try:
                    next(g_)
                    nxt_.append(g_)
                except StopIteration:
                    pass
            gens = nxt_

    load_macro(0)
    for c2 in range(0, NCH + 4, 2):
        m_next = c2 // MC + 1
        if c2 % MC == 2 and m_next < NMAC:
            load_macro(m_next)
        tasks = []
        for cc in (c2, c2 + 1):
            if cc < NCH:
                tasks.append(stage_p(cc))
        qs = [stage_q(cc) for cc in (c2 - 2, c2 - 1) if 0 <= cc < NCH]
        if qs:
            def seq(gs):
                for g_ in gs:
                    yield from g_
            tasks.append(seq(qs))
        for cc in (c2 - 4, c2 - 3):
            if 0 <= cc < NCH:
                tasks.append(stage_g(cc))
        run_all(tasks)


def build(S, shapes, consts, dbg=(), dbg_in=(), phases="A,N,R,D1,D2"):
    nc = bass.Bass("TRN2", target_bir_lowering=False)
    I = {}
    for nm in INPUT_NAMES:
        shp = list(shapes[nm])
        I[nm] = dram(nc, nm, shp, F32, kind="ExternalInput")
    for nm, arr in consts.items():
        I[nm] = dram(nc, nm, arr.shape, BF16 if arr.dtype == NPBF else F32, kind="ExternalInput")
    SC = {}
    for nm, (shp, dt) in scratch_defs(S).items():
        kind = "Internal"
        if nm in dbg:
            kind = "ExternalOutput"
        if nm in dbg_in:
            kind = "ExternalInput"
        SC[nm] = dram(nc, nm, shp, dt, kind=kind)
    out = dram(nc, "out", [NB, S, D], F32, kind="ExternalOutput")
    phases = phases.split(",")
    with ExitStack() as es:
        kb = KB(nc, es)
        kb.recording = SCHED
        if "A" in phases:
            phase_a(nc, kb, S, I, SC)
            kb.barrier()
        if "N" in phases:
            run_phase(kb, phase_n, S, I, SC)
        if "R" in phases:
            run_phase(kb, phase_r, S, I, SC)
        if "D1" in phases:
            run_phase(kb, phase_d1, S, I, SC)
        if "D2" in phases:
            run_phase(kb, phase_d2, S, I, SC, out)
        kb.barrier()
        print("instructions:", kb.ninstr)
    return nc


SEQ = 4096
NCORES = 8


def kernel(**inputs):
    S = SEQ
    consts = host_consts(S)
    shapes = {k: tuple(np.asarray(v).shape) for k, v in inputs.items()}
    shapes["x"] = (NB, S, D)
    shapes["p"] = (1, NB, S, 256)
    nc = build(S, shapes, consts)
    base = {k: np.ascontiguousarray(np.asarray(v, dtype=np.float32)) for k, v in inputs.items()
            if k not in ("x", "p")}
    base.update(consts)
    x = np.asarray(inputs["x"], dtype=np.float32)
    p = np.asarray(inputs["p"], dtype=np.float32)
    in_maps = []
    for c in range(NCORES):
        m = dict(base)
        m["x"] = np.ascontiguousarray(x[NB * c:NB * (c + 1)])
        m["p"] = np.ascontiguousarray(p[:, NB * c:NB * (c + 1)])
        in_maps.append(m)
    res = run_bass_kernel_spmd(nc, in_maps, core_ids=list(range(NCORES)))
    return np.concatenate([np.asarray(r["out"], dtype=np.float32) for r in res.results], axis=0)
```

```python
import numpy as np
import ml_dtypes
from contextlib import ExitStack
import concourse.bass as bass
import concourse.mybir as mybir
from concourse.bass_utils import run_bass_kernel_spmd

F32 = mybir.dt.float32
BF16 = mybir.dt.bfloat16
AF = mybir.ActivationFunctionType
ALU = mybir.AluOpType
AX = mybir.AxisListType
NPBF = ml_dtypes.bfloat16

D = 1024
NB = 2
HD = 64
NCOLS = 3096
C_Q, C_KC, C_VC, C_KS, C_VS, C_KW, C_VW, C_G = 0, 512, 640, 768, 896, 1024, 1152, 1280
C_RW = 1304
DECAY_C = 0.6065306597126334
import os
STOP = int(os.environ.get('KSTOP', '99'))
SKIP = os.environ.get('KSKIP', '')
SCHED = os.environ.get('KSCHED', '1') == '1'


class Buf:
    __slots__ = ("name", "w", "rs", "excl", "wx", "_s")

    def __init__(self, name="", excl=False):
        self.name = name
        self.w = None
        self.rs = []
        self.excl = excl
        self.wx = []
        self._s = None


import types
import os


def _freeze(fn, depth=0):
    if not isinstance(fn, types.FunctionType) or fn.__closure__ is None or depth > 3:
        return fn
    cells = []
    for c in fn.__closure__:
        try:
            v = c.cell_contents
        except ValueError:
            cells.append(c)
            continue
        if isinstance(v, types.FunctionType):
            v = _freeze(v, depth + 1)
        cells.append(types.CellType(v))
    return types.FunctionType(fn.__code__, fn.__globals__, fn.__name__, fn.__defaults__, tuple(cells))


COST = {"pe": 0.2, "act": 0.58, "dve": 0.5, "pool": 1.2, "sp": 0.08}
SEM_LAT = 0.12
DMA_LAT = 3.0
SCHED_W = int(os.environ.get("KSCHEDW", "24"))


class Eng:
    def __init__(self, name, h, semi):
        self.name = name
        self.h = h
        self.semi = semi
        self.count = 0
        self.waited = {}
        self.dsems = []
        self.dn = 0


class KB:
    DK = 8

    def __init__(self, nc, es):
        self.nc = nc
        self.es = es
        self.sems = []
        self.E = {}
        for name, h in (("pe", nc.tensor), ("act", nc.scalar), ("dve", nc.vector),
                        ("pool", nc.gpsimd), ("sp", nc.sync)):
            self.E[name] = Eng(name, h, self.newsem("c_" + name))
        for name in ("sp", "pool", "act"):
            e = self.E[name]
            e.dsems = [self.newsem("d_%s%d" % (name, i)) for i in range(self.DK)]
        self.semmax = {}
        self.ninstr = 0
        self.recording = False
        self.units = []
        self.cur_pe = None
        self.gen = 0

    def newsem(self, name):
        s = self.es.enter_context(self.nc.semaphore(name))
        self.sems.append(s)
        return len(self.sems) - 1

    def _wait(self, eng, semi, val):
        if eng.waited.get(semi, 0) >= val:
            return
        eng.h.wait_ge(self.sems[semi], val)
        eng.waited[semi] = val
        self.ninstr += 1

    def _deps(self, eng, r, w):
        for b in r:
            ev = b.w
            if ev is not None:
                if ev[2] is not None and ev[1] > ev[2].count:
                    raise RuntimeError("unsignaled producer for %s" % b.name)
                self._wait(eng, ev[0], ev[1])
            for ev in b.wx:
                self._wait(eng, ev[0], ev[1])
            if b.excl:
                for ev in b.rs:
                    if ev[2] is not eng:
                        self._wait(eng, ev[0], ev[1])
        pe = self.E["pe"]
        for b in w:
            for ev in b.wx:
                self._wait(eng, ev[0], ev[1])
            ev = b.w
            if ev is not None and not (ev[2] is eng and eng is pe):
                if ev[2] is not None and ev[1] > ev[2].count:
                    raise RuntimeError("unsignaled producer (waw) for %s" % b.name)
                self._wait(eng, ev[0], ev[1])
            for ev in b.rs:
                if not (ev[2] is eng and eng is pe):
                    if ev[2] is not None and ev[1] > ev[2].count:
                        raise RuntimeError("unsignaled reader for %s" % b.name)
                    self._wait(eng, ev[0], ev[1])

    def _rec(self, en, item, r, w, sig, cost, is_dma):
        if en == "pe" and self.cur_pe is not None:
            u = self.cur_pe
        else:
            u = dict(eng=en, items=[], deps=set(), cost=0.0, dma=is_dma, idx=len(self.units))
            self.units.append(u)
        ui = u["idx"]
        u["items"].append(item)
        u["cost"] += cost
        g = self.gen
        for b in r:
            st = getattr(b, "_s", None)
            if st is None or st[0] != g:
                st = [g, None, []]
                b._s = st
            if st[1] is not None and st[1] != ui:
                u["deps"].add(st[1])
            if b.excl:
                for x in st[2]:
                    if x != ui:
                        u["deps"].add(x)
        for b in w:
            st = getattr(b, "_s", None)
            if st is None or st[0] != g:
                st = [g, None, []]
                b._s = st
            if st[1] is not None and st[1] != ui:
                u["deps"].add(st[1])
            for x in st[2]:
                if x != ui:
                    u["deps"].add(x)
        for b in r:
            b._s[2].append(ui)
        for b in w:
            b._s[1] = ui
            b._s[2] = []
        if en == "pe":
            self.cur_pe = None if sig else u

    def flush(self):
        units = self.units
        self.units = []
        self.cur_pe = None
        self.gen += 1
        if not units:
            return
        n = len(units)
        pend = {en: [] for en in self.E}
        for u in units:
            pend[u["eng"]].append(u["idx"])
        ptr = {en: 0 for en in self.E}
        emitted = [False] * n
        finish = [0.0] * n
        tfree = {en: 0.0 for en in self.E}
        order = []
        live = [en for en in self.E if pend[en]]
        while len(order) < n:
            best = None
            for en in live:
                lst = pend[en]
                p = ptr[en]
                cnt = 0
                i = p
                tf = tfree[en]
                while i < len(lst) and cnt < SCHED_W:
                    ui = lst[i]
                    i += 1
                    if emitted[ui]:
                        continue
                    cnt += 1
                    u = units[ui]
                    ok = True
                    st = tf
                    for d in u["deps"]:
                        if not emitted[d]:
                            ok = False
                            break
                        f = finish[d] + (SEM_LAT if units[d]["eng"] != en or units[d]["dma"] else 0.03)
                        if f > st:
                            st = f
                    if not ok:
                        continue
                    if best is None or st < best[0] - 1e-9 or (abs(st - best[0]) <= 1e-9 and ui < best[1]):
                        best = (st, ui, en)
                    if st <= tf + 1e-9:
                        break
            st, ui, en = best
            u = units[ui]
            emitted[ui] = True
            if u["dma"]:
                tfree[en] = st + u["cost"]
                finish[ui] = st + DMA_LAT
            else:
                tfree[en] = st + u["cost"]
                finish[ui] = st + u["cost"]
            order.append(ui)
            lst = pend[en]
            while ptr[en] < len(lst) and emitted[lst[ptr[en]]]:
                ptr[en] += 1
            if ptr[en] >= len(lst):
                live.remove(en)
        self.sim_time = getattr(self, "sim_time", 0.0) + max(finish)
        rec = self.recording
        self.recording = False
        for ui in order:
            u = units[ui]
            for it in u["items"]:
                if it[0] == "op":
                    self.op(u["eng"], it[1], it[2], it[3], it[4])
                else:
                    self.dma(u["eng"], it[1], it[2], it[3], it[4], **it[5])
        self.recording = rec

    def op(self, en, fn, r=(), w=(), sig=True, c=None):
        if self.recording:
            self._rec(en, ("op", _freeze(fn), tuple(r), tuple(w), sig), r, w, sig,
                      COST[en] if c is None else c, False)
            return None
        eng = self.E[en]
        self._deps(eng, r, w)
        ins = fn(eng.h)
        self.ninstr += 1
        ticket = eng.count + 1
        if sig:
            ins.then_inc(self.sems[eng.semi], 1)
            eng.count = ticket
            self.semmax[eng.semi] = ticket
        ev = (eng.semi, ticket, eng)
        for b in r:
            b.rs.append(ev)
        for b in w:
            b.w = ev
            b.rs = []
            b.wx = []
        return ins

    def dma(self, qn, out, in_, r=(), w=(), **kw):
        if self.recording:
            self._rec(qn, ("dma", out, in_, tuple(r), tuple(w), kw), r, w, True, 0.08, True)
            return None
        eng = self.E[qn]
        self._deps(eng, r, w)
        i = eng.dn % self.DK
        tgt = 16 * (eng.dn // self.DK + 1)
        semi = eng.dsems[i]
        if tgt > 16:
            self._wait(eng, semi, tgt - 16)
        eng.h.dma_start(out=out, in_=in_, **kw).then_inc(self.sems[semi], 16)
        self.ninstr += 1
        eng.dn += 1
        self.semmax[semi] = tgt
        ev = (semi, tgt, None)
        for b in r:
            b.rs.append(ev)
        for b in w:
            if b.rs or b.w is None or b.w[2] is not None:
                b.wx = []
            else:
                b.wx.append(b.w)
            b.w = ev
            b.rs = []

    def barrier(self, engs=("pe", "act", "dve", "pool", "sp")):
        if self.recording:
            self.flush()
        for en in engs:
            eng = self.E[en]
            for semi, val in self.semmax.items():
                self._wait(eng, semi, val)


class Tl:
    uid = 0

    def __init__(self, kb, name, shape, dt, nbuf=1, psum=False):
        nc = kb.nc
        self.t = []
        self.b = []
        for i in range(nbuf):
            Tl.uid += 1
            nm = "%s_%d_%d" % (name, i, Tl.uid)
            if psum:
                t = kb.es.enter_context(nc.psum_tensor(nm, shape, dt))
            else:
                t = kb.es.enter_context(nc.sbuf_tensor(nm, shape, dt))
            self.t.append(t)
            self.b.append(Buf(nm, excl=psum))
        self.n = nbuf
        self.i = -1

    def nxt(self):
        self.i = (self.i + 1) % self.n
        return self.t[self.i], self.b[self.i]

    def cur(self):
        return self.t[self.i], self.b[self.i]


def _bf(a):
    return np.ascontiguousarray(a.astype(NPBF))


def host_consts(S):
    c = {}
    c["ident_bf"] = _bf(np.eye(128, dtype=np.float32))
    c["ident_f"] = np.eye(128, dtype=np.float32)
    bo = np.zeros((128, 128), np.float32)
    bo[:64, :64] = 1.0
    bo[64:, 64:] = 1.0
    c["blockones"] = bo
    bs = np.zeros((128, 2), np.float32)
    bs[:64, 0] = 1.0
    bs[64:, 1] = 1.0
    c["blocksel"] = _bf(bs)
    rm = np.ones((128, 512), np.float32)
    rm[:, ::64] = 0.0
    c["resetmask"] = rm
    c.update(nsa_consts(S))
    c.update(rwkv_consts())
    return c


def dram(nc, name, shape, dt, kind="Internal"):
    return nc.dram_tensor(name, list(shape), dt, kind=kind).ap()


def phase_a(nc, kb0, S, I, SC):
    with ExitStack() as es:
        kb = kb0
        kb.es_phase = es
        old_es = kb.es
        kb.es = es
        try:
            _phase_a(nc, kb, S, I, SC)
        finally:
            kb.es = old_es


def _phase_a(nc, kb, S, I, SC):
    NST = NB * S // 512
    op, dma = kb.op, kb.dma

    def tl(name, shape, dt, nbuf=1, psum=False):
        return Tl(kb, name, shape, dt, nbuf, psum)

    Wb = tl("Wb", [128, 8, NCOLS], BF16)
    Wbt, Wbb = Wb.nxt()
    gcol = tl("gcol", [128, 8], F32)
    gct, gcb = gcol.nxt()
    dma("sp", gct[:], I["g_mix_pre"].rearrange("o (k p) -> p (o k)", p=128), w=[gcb],
        allow_slow_non_contiguous=True)
    ident = tl("ident", [128, 128], BF16)
    idt, idb = ident.nxt()
    dma("sp", idt[:], I["ident_bf"][:, :], w=[idb])
    bones = tl("bones", [128, 128], F32)
    bot, bob = bones.nxt()
    dma("sp", bot[:], I["blockones"][:, :], w=[bob])
    bsel = tl("bsel", [128, 2], BF16)
    bst, bsb = bsel.nxt()
    dma("sp", bst[:], I["blocksel"][:, :], w=[bsb])
    rmask = tl("rmask", [128, 512], F32)
    rmt, rmb = rmask.nxt()
    dma("sp", rmt[:], I["resetmask"][:, :], w=[rmb])
    mu = tl("mu", [128, 14], F32)
    mut, mub = mu.nxt()
    dma("sp", mut[:], I["shift_mu"].rearrange("o (s p) -> p (o s)", p=128), w=[mub],
        allow_slow_non_contiguous=True)
    omu = tl("omu", [128, 14], F32)
    omut, omub = omu.nxt()
    op("dve", lambda e: e.tensor_scalar(out=omut[:], in0=mut[:], scalar1=-1.0, scalar2=1.0,
                                        op0=ALU.mult, op1=ALU.add), r=[mub], w=[omub])
    cols = {}
    for nm in ("w0", "a0", "k_k", "k_a", "r_k"):
        t = tl("c_" + nm, [128, 4], F32)
        tt, tb = t.nxt()
        src = I[nm]
        if nm == "r_k":
            src = src.rearrange("o h d -> o (h d)")
        dma("sp", tt[:], src.rearrange("o (s p) -> p (o s)", p=128), w=[tb],
            allow_slow_non_contiguous=True)
        cols[nm] = (tt, tb)
    omka = tl("omka", [128, 4], F32)
    omkat, omkab = omka.nxt()
    op("dve", lambda e: e.tensor_scalar(out=omkat[:], in0=cols["k_a"][0][:], scalar1=-1.0, scalar2=1.0,
                                        op0=ALU.mult, op1=ALU.add), r=[cols["k_a"][1]], w=[omkab])
    gbias = tl("gbias", [128, 24], F32)
    gbt, gbb = gbias.nxt()
    dma("sp", gbt[:], I["nsa_gate_bias"].partition_broadcast(128), w=[gbb])
    wlora = tl("wlora", [64, 512], F32)
    wlt, wlb = wlora.nxt()
    dma("sp", wlt[:], I["w_lora_up"][0], w=[wlb])
    alora = tl("alora", [64, 512], F32)
    alt, alb = alora.nxt()
    dma("sp", alt[:], I["a_lora_up"][0], w=[alb])
    glora_f = tl("glora_f", [128, 512], F32)
    glft, glfb = glora_f.nxt()
    dma("sp", glft[:], I["g_lora_up"][0], w=[glfb])
    glora = tl("glora", [128, 512], BF16)
    glt, glb = glora.nxt()
    op("dve", lambda e: e.tensor_copy(glt[:], glft[:]), r=[glfb], w=[glb])

    with ExitStack() as es2:
        old = kb.es
        kb.es = es2
        wst = tl("wst", [128, NCOLS], F32, nbuf=2)
        kb.es = old
        for k in range(8):
            st, sb = wst.nxt()
            dma("sp", st[:], I["w_in"][0, k * 128:(k + 1) * 128, :], w=[sb])
            if k % 2 == 0:
                op("dve", lambda e, st=st, k=k: e.tensor_scalar(out=Wbt[:, k, :], in0=st[:], scalar1=gct[:, k:k + 1],
                                                                 scalar2=None, op0=ALU.mult),
                   r=[sb, gcb], w=[Wbb])
            else:
                op("act", lambda e, st=st, k=k: e.activation(out=Wbt[:, k, :], in_=st[:], func=AF.Copy,
                                                              scale=gct[:, k:k + 1]), r=[sb, gcb], w=[Wbb])
        kb.barrier()
    if STOP == 0:
        return

    xt = tl("xt", [128, 1024], F32, nbuf=4)
    ss = tl("ss", [128, 4], F32, nbuf=2)
    rstd = tl("rstd", [128, 4], F32, nbuf=2)
    xn = tl("xn", [128, 1024], BF16, nbuf=4)
    hT = tl("hT", [128, 8, 512], BF16, nbuf=2)
    tp = tl("tp", [128, 1024], BF16, nbuf=2, psum=True)
    fm = tl("fm", [128, 512], F32, nbuf=3, psum=True)
    tm = tl("tm", [128, 512], F32, nbuf=1, psum=True)
    tr2 = tl("tr2", [128, 512], BF16, nbuf=1, psum=True)
    ssp = tl("ssp", [128, 512], F32, nbuf=1, psum=True)
    carry = tl("carry", [128, 14], F32)
    cat, cab = carry.nxt()
    Z = tl("Z", [128, 513], F32, nbuf=2)
    f32t = {nm: tl("f_" + nm, [128, 512], F32, nbuf=(2 if nm in ("r", "k0", "v") else 1)) for nm in
            ("r", "k0", "v", "sg", "Lp", "Ep", "Em", "En", "a", "kk", "t1", "kkn", "k", "ba", "bt", "kt", "tmp")}
    bft = {nm: tl("b_" + nm, [128, 512], BF16, nbuf=2) for nm in
           ("aT", "bT", "kT", "rT", "BT", "KT", "vT", "rk")}
    osb = tl("osb", [128, 512], BF16, nbuf=2)
    tokmaj = tl("tokmaj", [128, 4, 4, 512], BF16)
    tkt, tkb = tokmaj.nxt()
    tkbs = [[Buf("tk%d%d" % (j, hc)) for hc in range(4)] for j in range(4)]
    lor = tl("lor", [64, 2, 512], F32)
    lot, lob = lor.nxt()
    sgd = tl("sgd", [128, 512], BF16)
    sgt, sgb = sgd.nxt()
    vtok = tl("vtok", [128, 4, 2, 2, 128], BF16)
    vtt, vtb = vtok.nxt()
    op("dve", lambda e: e.memset(vtt[:], 1.0), w=[vtb])
    gtok = tl("gtok", [128, 4, 24], F32)
    gtt, gtb = gtok.nxt()
    gout = tl("gout", [128, 4, 512], BF16)
    got, gob = gout.nxt()
    bon = tl("bon", [128, 4, 8], F32)
    bont, bonb = bon.nxt()
    pcs = tl("pcs", [128, 4, 8], F32)
    pct, pcb = pcs.nxt()
    wraw = tl("wraw", [128, 4, 512], F32)
    wrt, wrb = wraw.nxt()
    araw = tl("araw", [128, 4, 512], F32)
    art, arb = araw.nxt()

    def T(nm):
        return f32t[nm].nxt()

    for st_i in range(NST):
        b = st_i // (S // 512)
        t0 = (st_i % (S // 512)) * 512
        first = (t0 == 0)
        hTt, hTb = hT.nxt()
        sst, ssb = ss.nxt()
        rst, rsb = rstd.nxt()
        xns = []
        xnl = []
        for j in range(4):
            xtt, xtb = xt.nxt()
            dma("sp", xtt[:], I["x"][b, t0 + j * 128:t0 + (j + 1) * 128, :], w=[xtb])
            xnt, xnb = xn.nxt()
            op("act", lambda e, xtt=xtt, j=j: e.activation(out=xnt[:], in_=xtt[:], func=AF.Square,
                                                            accum_out=sst[:, j:j + 1]),
               r=[xtb], w=[xnb, ssb])
            xns.append((xtt, xtb))
            xnl.append((xnt, xnb))
        op("dve", lambda e: e.tensor_scalar(out=rst[:], in0=sst[:], scalar1=1.0 / D, scalar2=1e-6,
                                            op0=ALU.mult, op1=ALU.add), r=[ssb], w=[rsb])
        op("act", lambda e: e.activation(out=rst[:], in_=rst[:], func=AF.Sqrt), r=[rsb], w=[rsb])
        op("dve", lambda e: e.reciprocal(out=rst[:], in_=rst[:]), r=[rsb], w=[rsb])
        for j in range(4):
            xtt, xtb = xns[j]
            xnt, xnb = xnl[j]
            if j % 2 == 0:
                op("dve", lambda e, xtt=xtt, xnt=xnt, j=j: e.tensor_scalar(out=xnt[:], in0=xtt[:],
                                                                            scalar1=rst[:, j:j + 1], scalar2=None,
                                                                            op0=ALU.mult),
                   r=[xtb, rsb], w=[xnb])
            else:
                op("act", lambda e, xtt=xtt, xnt=xnt, j=j: e.activation(out=xnt[:], in_=xtt[:], func=AF.Copy,
                                                                         scale=rst[:, j:j + 1]),
                   r=[xtb, rsb], w=[xnb])
        for m in range(4):
            tpt, tpb = tp.nxt()
            for kk_ in range(2):
                k = 2 * m + kk_
                for j in range(4):
                    xnt, xnb = xnl[j]
                    last = (kk_ == 1 and j == 3)
                    op("pe", lambda e, xnt=xnt, k=k, j=j, kk_=kk_, tpt=tpt: e.transpose(
                        tpt[:, kk_ * 512 + j * 128: kk_ * 512 + (j + 1) * 128],
                        xnt[:, k * 128:(k + 1) * 128], idt[:]),
                       r=[xnb, idb], w=[tpb], sig=last)
            en = "act" if m % 2 == 0 else "dve"
            if en == "act":
                op("act", lambda e, tpt=tpt, m=m: e.copy(out=hTt[:, 2 * m:2 * m + 2, :],
                                                         in_=tpt[:].rearrange("p (a b) -> p a b", b=512)),
                   r=[tpb], w=[hTb])
            else:
                op("dve", lambda e, tpt=tpt, m=m: e.tensor_copy(hTt[:, 2 * m:2 * m + 2, :],
                                                                tpt[:].rearrange("p (a b) -> p a b", b=512)),
                   r=[tpb], w=[hTb])

        if STOP == 1:
            return

        def fm_mm(c0, width):
            pt, pb = fm.nxt()
            for k in range(8):
                op("pe", lambda e, k=k, pt=pt: e.matmul(pt[0:width, :], Wbt[:, k, c0:c0 + width], hTt[:, k, :],
                                                         start=(k == 0), stop=(k == 7)),
                   r=[Wbb, hTb], w=[pb], sig=(k == 7))
            return pt, pb

        for ci, (c0, dst, row0, scale) in enumerate(
                [(C_Q + 128 * i, "QT", 128 * i, 0.125) for i in range(4)] +
                [(C_KC, "KCT", 0, 1.0), (C_VC, "VCT", 0, 1.0), (C_KS, "KST", 0, 1.0), (C_KW, "KWT", 0, 1.0)]):
            pt, pb = fm_mm(c0, 128)
            ot, ob = osb.nxt()
            if ci % 2 == 0:
                op("act", lambda e, pt=pt, ot=ot, scale=scale: e.activation(out=ot[:], in_=pt[:], func=AF.Copy,
                                                                            scale=scale), r=[pb], w=[ob])
            else:
                op("dve", lambda e, pt=pt, ot=ot, scale=scale: e.tensor_scalar(out=ot[:], in0=pt[:], scalar1=scale,
                                                                               scalar2=None, op0=ALU.mult),
                   r=[pb], w=[ob])
            dma("pool", SC[dst][b, row0:row0 + 128, t0:t0 + 512], ot[:], r=[ob])

        if STOP == 2:
            return
        for j in range(4):
            tmt, tmb = tm.nxt()
            for gi, (c0, wd_, o0) in enumerate([(C_VS, 128, 0), (C_VW, 128 if 'n128' in SKIP else 152, 128)]):
                for k in range(8):
                    op("pe", lambda e, k=k, j=j, c0=c0, wd_=wd_, o0=o0: e.matmul(
                        tmt[:, o0:o0 + wd_], hTt[:, k, j * 128:(j + 1) * 128], Wbt[:, k, c0:c0 + wd_],
                        start=(k == 0), stop=(k == 7)),
                       r=[Wbb, hTb], w=[tmb], sig=(k == 7 and gi == 1))
            if 'cp' not in SKIP:
                op("act", lambda e, j=j: e.copy(out=vtt[:, j, :, :, 0:64],
                                                in_=tmt[:, 0:256].rearrange("p (a g d) -> p a g d", a=2, g=2)),
                   r=[tmb], w=[vtb])
            if 'ad' not in SKIP:
                op("act", lambda e, j=j: e.copy(out=gtt[:, j, :], in_=tmt[:, 256:280]), r=[tmb], w=[gtb])
                op("dve", lambda e, j=j: e.tensor_tensor(out=gtt[:, j, :], in0=gtt[:, j, :], in1=gbt[:],
                                                     op=ALU.add), r=[gtb, gbb], w=[gtb])
        if 'sig' not in SKIP:
            op("act", lambda e: e.activation(out=gtt[:], in_=gtt[:], func=AF.Sigmoid), r=[gtb], w=[gtb])
        if 'dv' not in SKIP:
            for sw, nm_ in enumerate(("VSA", "VWA")):
                for g in range(2):
                    dma("pool", SC[nm_][b, g, :, t0 // 128:t0 // 128 + 4, :], vtt[:, :, sw, g, :], r=[vtb])
        if 'dg' not in SKIP:
            dma("pool", SC["GATE"][b, t0:t0 + 512, :].rearrange("(j p) c -> p j c", p=128), gtt[:], r=[gtb])

        if STOP == 3:
            return
        def shift(pt, pb, slot, dst_t, dst_b, eng2="dve"):
            zt, zb = Z.nxt()
            op("act", lambda e: e.copy(out=zt[:, 1:513], in_=pt[:]), r=[pb], w=[zb])
            if first:
                op("dve", lambda e: e.memset(zt[:, 0:1], 0.0), w=[zb])
            else:
                op("dve", lambda e: e.tensor_copy(zt[:, 0:1], cat[:, slot:slot + 1]), r=[cab], w=[zb])
            op("dve", lambda e: e.tensor_copy(cat[:, slot:slot + 1], zt[:, 512:513]), r=[zb], w=[cab])
            op("act", lambda e: e.activation(out=dst_t, in_=zt[:, 1:513], func=AF.Copy,
                                             scale=omut[:, slot:slot + 1]), r=[zb, omub], w=[dst_b])
            op("dve", lambda e: e.scalar_tensor_tensor(out=dst_t, in0=zt[:, 0:512], scalar=mut[:, slot:slot + 1],
                                                      in1=dst_t, op0=ALU.mult, op1=ALU.add),
               r=[zb, mub, dst_b], w=[dst_b])

        pt, pb = fm_mm(C_RW + 1536, 128)
        tmpt, tmpb = T("tmp")
        shift(pt, pb, 12, tmpt[:], tmpb)
        op("act", lambda e: e.activation(out=lot[:, 0, :], in_=tmpt[0:64, :], func=AF.Tanh), r=[tmpb], w=[lob])
        op("dve", lambda e: e.tensor_copy(lot[:, 1, :], tmpt[64:128, :]), r=[tmpb], w=[lob])
        for hc in range(4):
            pt, pb = fm.nxt()
            op("pe", lambda e, pt=pt, hc=hc: e.matmul(pt[:], wlt[:, hc * 128:(hc + 1) * 128], lot[:, 0, :],
                                                       start=True, stop=True), r=[wlb, lob], w=[pb])
            op("act", lambda e, pt=pt, hc=hc: e.activation(out=wrt[:, hc, :], in_=pt[:], func=AF.Sigmoid,
                                                            bias=cols["w0"][0][:, hc:hc + 1]),
               r=[pb, cols["w0"][1]], w=[wrb])
            pt, pb = fm.nxt()
            op("pe", lambda e, pt=pt, hc=hc: e.matmul(pt[:], alt[:, hc * 128:(hc + 1) * 128], lot[:, 1, :],
                                                       start=True, stop=True), r=[alb, lob], w=[pb])
            op("act", lambda e, pt=pt, hc=hc: e.activation(out=art[:, hc, :], in_=pt[:], func=AF.Sigmoid,
                                                            bias=cols["a0"][0][:, hc:hc + 1]),
               r=[pb, cols["a0"][1]], w=[arb])
        pt, pb = fm_mm(C_RW + 1664, 128)
        shift(pt, pb, 13, tmpt[:], tmpb)
        op("act", lambda e: e.activation(out=sgt[:], in_=tmpt[:], func=AF.Sigmoid), r=[tmpb], w=[sgb])
        for j in range(4):
            tmt, tmb = tm.nxt()
            op("pe", lambda e, j=j: e.matmul(tmt[:], sgt[:, j * 128:(j + 1) * 128], glt[:], start=True, stop=True),
               r=[sgb, glb], w=[tmb])
            op("act", lambda e, j=j: e.copy(out=got[:, j, :], in_=tmt[:]), r=[tmb], w=[gob])
        dma("pool", SC["GT"][b, t0:t0 + 512, :].rearrange("(j p) c -> p j c", p=128), got[:], r=[gob])

        if STOP == 4:
            return
        for hc in range(4):
            rt, rb = T("r")
            k0t, k0b = T("k0")
            vt, vb = T("v")
            pt, pb = fm_mm(C_RW + hc * 128, 128)
            shift(pt, pb, hc, rt[:], rb, "pool")
            pt, pb = fm_mm(C_RW + 512 + hc * 128, 128)
            shift(pt, pb, 4 + hc, k0t[:], k0b, "dve")
            pt, pb = fm_mm(C_RW + 1024 + hc * 128, 128)
            shift(pt, pb, 8 + hc, vt[:], vb, "pool")
            Lpt, Lpb = T("Lp")
            op("dve", lambda e: e.tensor_tensor_scan(out=Lpt[:], data0=rmt[:], data1=wrt[:, hc, :], initial=0.0,
                                                     op0=ALU.mult, op1=ALU.add), r=[rmb, wrb], w=[Lpb])
            Ept, Epb = T("Ep")
            Ent, Enb = T("En")
            Emt, Emb = T("Em")
            op("act", lambda e: e.activation(out=Ept[:], in_=Lpt[:], func=AF.Exp, scale=-DECAY_C), r=[Lpb], w=[Epb])
            op("act", lambda e: e.activation(out=Ent[:], in_=Lpt[:], func=AF.Exp, scale=DECAY_C), r=[Lpb], w=[Enb])
            op("pool", lambda e: e.tensor_tensor(out=Emt[:], in0=Lpt[:], in1=wrt[:, hc, :], op=ALU.subtract),
               r=[Lpb, wrb], w=[Emb])
            op("act", lambda e: e.activation(out=Emt[:], in_=Emt[:], func=AF.Exp, scale=-DECAY_C), r=[Emb], w=[Emb])
            op("dve", lambda e: e.tensor_copy(pct[:, hc, :], Ept[:].rearrange("p (c t) -> p c t", t=64)[:, :, 63]),
               r=[Epb], w=[pcb])
            kkt, kkb = T("kk")
            t1t, t1b = T("t1")
            op("dve", lambda e: e.tensor_scalar(out=kkt[:], in0=k0t[:], scalar1=cols["k_k"][0][:, hc:hc + 1],
                                                scalar2=None, op0=ALU.mult), r=[k0b, cols["k_k"][1]], w=[kkb])
            op("act", lambda e: e.activation(out=t1t[:], in_=kkt[:], func=AF.Square), r=[kkb], w=[t1b])
            spt, spb = ssp.nxt()
            op("pe", lambda e: e.matmul(spt[:], bot[:], t1t[:], start=True, stop=True), r=[bob, t1b], w=[spb])
            op("dve", lambda e: e.tensor_scalar(out=t1t[:], in0=spt[:], scalar1=1e-24, scalar2=None, op0=ALU.max),
               r=[spb], w=[t1b])
            op("act", lambda e: e.activation(out=t1t[:], in_=t1t[:], func=AF.Ln), r=[t1b], w=[t1b])
            op("act", lambda e: e.activation(out=t1t[:], in_=t1t[:], func=AF.Exp, scale=-0.5), r=[t1b], w=[t1b])
            kknt, kknb = T("kkn")
            op("dve", lambda e: e.tensor_tensor(out=kknt[:], in0=kkt[:], in1=t1t[:], op=ALU.mult),
               r=[kkb, t1b], w=[kknb])
            kt_, kb_ = T("k")
            op("dve", lambda e: e.tensor_scalar(out=kt_[:], in0=art[:, hc, :], scalar1=cols["k_a"][0][:, hc:hc + 1],
                                                scalar2=omkat[:, hc:hc + 1], op0=ALU.mult, op1=ALU.add),
               r=[arb, cols["k_a"][1], omkab], w=[kb_])
            op("pool", lambda e: e.tensor_tensor(out=kt_[:], in0=kt_[:], in1=k0t[:], op=ALU.mult),
               r=[kb_, k0b], w=[kb_])
            aTt, aTb = bft["aT"].nxt()
            op("dve", lambda e: e.scalar_tensor_tensor(out=aTt[:], in0=kknt[:], scalar=-1.0, in1=Emt[:],
                                                       op0=ALU.mult, op1=ALU.mult), r=[kknb, Emb], w=[aTb])
            bat, bab = T("ba")
            op("pool", lambda e: e.tensor_tensor(out=bat[:], in0=kknt[:], in1=art[:, hc, :], op=ALU.mult),
               r=[kknb, arb], w=[bab])
            btt, btb = T("bt")
            op("dve", lambda e: e.tensor_tensor(out=btt[:], in0=bat[:], in1=Ent[:], op=ALU.mult),
               r=[bab, Enb], w=[btb])
            bTt, bTb = bft["bT"].nxt()
            op("act", lambda e: e.copy(out=bTt[:], in_=btt[:]), r=[btb], w=[bTb])
            pcbc = pct[:, hc, :].unsqueeze(2).to_broadcast([128, 8, 64])
            BTt, BTb = bft["BT"].nxt()
            op("pool", lambda e: e.tensor_tensor(out=BTt[:].rearrange("p (c t) -> p c t", t=64),
                                                in0=btt[:].rearrange("p (c t) -> p c t", t=64), in1=pcbc,
                                                op=ALU.mult), r=[btb, pcb], w=[BTb])
            ktt, ktb = T("kt")
            op("pool", lambda e: e.tensor_tensor(out=ktt[:], in0=kt_[:], in1=Ent[:], op=ALU.mult),
               r=[kb_, Enb], w=[ktb])
            kTt, kTb = bft["kT"].nxt()
            op("act", lambda e: e.copy(out=kTt[:], in_=ktt[:]), r=[ktb], w=[kTb])
            KTt, KTb = bft["KT"].nxt()
            op("pool", lambda e: e.tensor_tensor(out=KTt[:].rearrange("p (c t) -> p c t", t=64),
                                                in0=ktt[:].rearrange("p (c t) -> p c t", t=64), in1=pcbc,
                                                op=ALU.mult), r=[ktb, pcb], w=[KTb])
            rTt, rTb = bft["rT"].nxt()
            op("pool", lambda e: e.tensor_tensor(out=rTt[:], in0=rt[:], in1=Ept[:], op=ALU.mult),
               r=[rb, Epb], w=[rTb])
            vTt, vTb = bft["vT"].nxt()
            op("act", lambda e: e.copy(out=vTt[:], in_=vt[:]), r=[vb], w=[vTb])
            rkt, rkb = bft["rk"].nxt()
            op("dve", lambda e: e.scalar_tensor_tensor(out=rkt[:], in0=rt[:], scalar=cols["r_k"][0][:, hc:hc + 1],
                                                       in1=kt_[:], op0=ALU.mult, op1=ALU.mult),
               r=[rb, cols["r_k"][1], kb_], w=[rkb])
            for kind, (tt_, tb_) in enumerate([(aTt, aTb), (bTt, bTb), (kTt, kTb), (rTt, rTb)]):
                dma("pool", SC["RWF"][b, kind, hc * 128:(hc + 1) * 128, t0:t0 + 512], tt_[:], r=[tb_])
            for j in range(4):
                tmt, tmb = tm.nxt()
                op("pe", lambda e, j=j: e.matmul(tmt[:, 0:2], rkt[:, j * 128:(j + 1) * 128], bst[:],
                                                 start=True, stop=True), r=[rkb, bsb], w=[tmb])
                op("dve", lambda e, j=j: e.tensor_copy(bont[:, j, 2 * hc:2 * hc + 2], tmt[:, 0:2]),
                   r=[tmb], w=[bonb])
                t2t, t2b = tr2.nxt()
                for kind, (tt_, tb_) in enumerate([(aTt, aTb), (BTt, BTb), (KTt, KTb), (vTt, vTb)]):
                    op("pe", lambda e, tt_=tt_, kind=kind, j=j: e.transpose(
                        t2t[:, kind * 128:(kind + 1) * 128], tt_[:, j * 128:(j + 1) * 128], idt[:]),
                       r=[tb_, idb], w=[t2b], sig=(kind == 3))
                en = "act" if j % 2 == 0 else "dve"
                if en == "act":
                    op("act", lambda e, j=j: e.copy(out=tkt[:, j, :, hc * 128:(hc + 1) * 128],
                                                    in_=t2t[:].rearrange("p (a c) -> p a c", c=128)),
                       r=[t2b], w=[tkbs[j][hc]])
                else:
                    op("dve", lambda e, j=j: e.tensor_copy(tkt[:, j, :, hc * 128:(hc + 1) * 128],
                                                           t2t[:].rearrange("p (a c) -> p a c", c=128)),
                       r=[t2b], w=[tkbs[j][hc]])
        for kind in range(4):
            dma("pool", SC["RWT"][b, kind, t0:t0 + 512, :].rearrange("(j p) c -> p j c", p=128),
                tkt[:, :, kind, :], r=[tkbs[j][hc] for j in range(4) for hc in range(4)])
        dma("pool", SC["BON"][b, t0:t0 + 512, :].rearrange("(j p) c -> p j c", p=128), bont[:], r=[bonb])
        for hc in range(4):
            dma("pool", SC["PC"][b, hc * 128:(hc + 1) * 128, t0 // 64:t0 // 64 + 8], pct[:, hc, :], r=[pcb])


def scratch_defs(S):
    return {
        "QT": ([NB, 512, S], BF16), "KCT": ([NB, 128, S], BF16), "VCT": ([NB, 128, S], BF16),
        "KST": ([NB, 128, S], BF16), "KWT": ([NB, 128, S], BF16),
        "VSA": ([NB, 2, 128, S // 128, 128], BF16), "VWA": ([NB, 2, 128, S // 128, 128], BF16), "GATE": ([NB, S, 24], F32),
        "GT": ([NB, S, 512], BF16),
        "RWF": ([NB, 4, 512, S], BF16), "RWT": ([NB, 4, S, 512], BF16),
        "BON": ([NB, S, 8], F32), "PC": ([NB, 512, S // 64], F32),
        "YMIX": ([NB, S, 1024], BF16), "X1": ([NB, S, 1024], F32), "XN": ([NB, S, 1024], BF16),
    }


INPUT_NAMES = ["x", "p", "g_mix_pre", "g_mix_post", "g_mlp_pre", "g_mlp_post", "w_in", "nsa_gate_bias",
               "cmp_pe_k", "cmp_k_w1", "cmp_k_b1", "cmp_k_w2", "cmp_pe_v", "cmp_v_w1", "cmp_v_b1", "cmp_v_w2",
               "shift_mu", "w0", "w_lora_up", "a0", "a_lora_up", "g_lora_up", "k_k", "k_a", "r_k", "lnx_w",
               "lnx_b", "w_out", "w_up", "w_down", "w_ple", "w_ple_gate"]


def run_phase(kb, fn, *args):
    with ExitStack() as es:
        old = kb.es
        kb.es = es
        try:
            fn(kb, *args)
        finally:
            kb.es = old
    kb.barrier()


def load_weight_bf16(kb, tl, Wt, Wb_, src, nk, ncols, gcol=None, stage_cols=None):
    with ExitStack() as es2:
        old = kb.es
        kb.es = es2
        wst = tl("wst", [128, ncols], F32, nbuf=2)
        kb.es = old
        for k in range(nk):
            st, sb = wst.nxt()
            kb.dma("sp", st[:], src[k * 128:(k + 1) * 128, :], w=[sb])
            if gcol is not None:
                if k % 2 == 0:
                    kb.op("dve", lambda e: e.tensor_scalar(out=Wt[:, k, :], in0=st[:], scalar1=gcol[0][:, k:k + 1],
                                                           scalar2=None, op0=ALU.mult), r=[sb, gcol[1]], w=[Wb_])
                else:
                    kb.op("act", lambda e: e.activation(out=Wt[:, k, :], in_=st[:], func=AF.Copy,
                                                        scale=gcol[0][:, k:k + 1]), r=[sb, gcol[1]], w=[Wb_])
            else:
                if k % 2 == 0:
                    kb.op("dve", lambda e: e.tensor_copy(Wt[:, k, :], st[:]), r=[sb], w=[Wb_])
                else:
                    kb.op("act", lambda e: e.copy(out=Wt[:, k, :], in_=st[:]), r=[sb], w=[Wb_])
        kb.barrier()


def rms_finish(kb, sst, ssb, rst, rsb, n):
    if n == 2:
        kb.op("dve", lambda e: e.tensor_tensor(out=rst[:, 0:1], in0=sst[:, 0:1], in1=sst[:, 1:2], op=ALU.add),
              r=[ssb], w=[rsb])
        kb.op("dve", lambda e: e.tensor_scalar(out=rst[:, 0:1], in0=rst[:, 0:1], scalar1=1.0 / D, scalar2=1e-6,
                                               op0=ALU.mult, op1=ALU.add), r=[rsb], w=[rsb])
    else:
        kb.op("dve", lambda e: e.tensor_scalar(out=rst[:, 0:1], in0=sst[:, 0:1], scalar1=1.0 / D, scalar2=1e-6,
                                               op0=ALU.mult, op1=ALU.add), r=[ssb], w=[rsb])
    kb.op("act", lambda e: e.activation(out=rst[:, 0:1], in_=rst[:, 0:1], func=AF.Sqrt), r=[rsb], w=[rsb])
    kb.op("dve", lambda e: e.reciprocal(out=rst[:, 0:1], in_=rst[:, 0:1]), r=[rsb], w=[rsb])


def phase_d1(kb, S, I, SC):
    op, dma = kb.op, kb.dma

    def tl(name, shape, dt, nbuf=1, psum=False):
        return Tl(kb, name, shape, dt, nbuf, psum)

    Wo = tl("Wo", [128, 8, 1024], BF16)
    Wot, Wob = Wo.nxt()
    load_weight_bf16(kb, tl, Wot, Wob, I["w_out"][0], 8, 1024)
    ident = tl("ident", [128, 128], BF16)
    idt, idb = ident.nxt()
    dma("sp", idt[:], I["ident_bf"][:, :], w=[idb])
    gbc = tl("gbc", [128, 1024], F32)
    gbt, gbb = gbc.nxt()
    dma("sp", gbt[:], I["g_mix_post"].partition_broadcast(128), w=[gbb])
    ym = tl("ym", [128, 4, 1024], BF16, nbuf=2)
    yT = tl("yT", [128, 8, 512], BF16, nbuf=2)
    tp = tl("tp", [128, 1024], BF16, nbuf=2, psum=True)
    mm = tl("mm", [128, 512], F32, nbuf=4, psum=True)
    xt = tl("xt", [128, 1024], F32, nbuf=2)
    junk = tl("junk", [128, 512], BF16)
    jt, jb = junk.nxt()
    ss = tl("ss", [128, 2], F32, nbuf=2)
    rstd = tl("rstd", [128, 1], F32, nbuf=2)
    tmp = tl("tmp", [128, 1024], F32, nbuf=2)
    x1 = tl("x1", [128, 1024], F32, nbuf=2)
    xnd = tl("xnd", [128, 1024], BF16, nbuf=2)
    for st_i in range(NB * S // 512):
        b = st_i // (S // 512)
        t0 = (st_i % (S // 512)) * 512
        ymt, ymb = ym.nxt()
        dma("sp", ymt[:], SC["YMIX"][b, t0:t0 + 512, :].rearrange("(j p) c -> p j c", p=128), w=[ymb])
        yTt, yTb = yT.nxt()
        for m in range(4):
            tpt, tpb = tp.nxt()
            for kk_ in range(2):
                k = 2 * m + kk_
                for j in range(4):
                    op("pe", lambda e: e.transpose(tpt[:, kk_ * 512 + j * 128: kk_ * 512 + (j + 1) * 128],
                                                   ymt[:, j, k * 128:(k + 1) * 128], idt[:]),
                       r=[ymb, idb], w=[tpb], sig=(kk_ == 1 and j == 3))
            if m % 2 == 0:
                op("act", lambda e: e.copy(out=yTt[:, 2 * m:2 * m + 2, :],
                                           in_=tpt[:].rearrange("p (a b) -> p a b", b=512)), r=[tpb], w=[yTb])
            else:
                op("dve", lambda e: e.tensor_copy(yTt[:, 2 * m:2 * m + 2, :],
                                                  tpt[:].rearrange("p (a b) -> p a b", b=512)), r=[tpb], w=[yTb])
        for j in range(4):
            xtt, xtb = xt.nxt()
            dma("sp", xtt[:], I["x"][b, t0 + j * 128:t0 + (j + 1) * 128, :], w=[xtb])
            sst, ssb = ss.nxt()
            rst, rsb = rstd.nxt()
            halves = []
            for hf in range(2):
                mt, mb = mm.nxt()
                for k in range(8):
                    op("pe", lambda e: e.matmul(mt[:], yTt[:, k, j * 128:(j + 1) * 128],
                                                Wot[:, k, hf * 512:(hf + 1) * 512], start=(k == 0), stop=(k == 7)),
                       r=[yTb, Wob], w=[mb], sig=(k == 7))
                op("act", lambda e: e.activation(out=jt[:], in_=mt[:], func=AF.Square,
                                                 accum_out=sst[:, hf:hf + 1]), r=[mb], w=[jb, ssb])
                halves.append((mt, mb))
            rms_finish(kb, sst, ssb, rst, rsb, 2)
            tmt, tmb = tmp.nxt()
            for hf in range(2):
                mt, mb = halves[hf]
                op("dve", lambda e: e.scalar_tensor_tensor(out=tmt[:, hf * 512:(hf + 1) * 512], in0=mt[:],
                                                           scalar=rst[:, 0:1], in1=gbt[:, hf * 512:(hf + 1) * 512],
                                                           op0=ALU.mult, op1=ALU.mult),
                   r=[mb, rsb, gbb], w=[tmb])
            x1t, x1b = x1.nxt()
            op("dve", lambda e: e.tensor_tensor(out=x1t[:], in0=tmt[:], in1=xtt[:], op=ALU.add),
               r=[tmb, xtb], w=[x1b])
            dma("pool", SC["X1"][b, t0 + j * 128:t0 + (j + 1) * 128, :], x1t[:], r=[x1b])
            s2t, s2b = ss.nxt()
            r2t, r2b = rstd.nxt()
            xnt, xnb = xnd.nxt()
            op("act", lambda e: e.activation(out=xnt[:], in_=x1t[:], func=AF.Square, accum_out=s2t[:, 0:1]),
               r=[x1b], w=[xnb, s2b])
            rms_finish(kb, s2t, s2b, r2t, r2b, 1)
            op("act", lambda e: e.activation(out=xnt[:], in_=x1t[:], func=AF.Copy, scale=r2t[:, 0:1]),
               r=[x1b, r2b], w=[xnb])
            dma("pool", SC["XN"][b, t0 + j * 128:t0 + (j + 1) * 128, :], xnt[:], r=[xnb])


def phase_d2(kb, S, I, SC, OUT):
    op, dma = kb.op, kb.dma

    def tl(name, shape, dt, nbuf=1, psum=False):
        return Tl(kb, name, shape, dt, nbuf, psum)

    gcol = tl("gcol", [128, 8], F32)
    gct, gcb = gcol.nxt()
    dma("sp", gct[:], I["g_mlp_pre"].rearrange("o (k p) -> p (o k)", p=128), w=[gcb],
        allow_slow_non_contiguous=True)
    Wu = tl("Wu", [128, 8, 4096], BF16)
    Wut, Wub = Wu.nxt()
    Wd = tl("Wd", [128, 32, 1024], BF16)
    Wdt, Wdb = Wd.nxt()
    Wg = tl("Wg", [128, 8, 1024], BF16)
    Wgt, Wgb = Wg.nxt()
    Wp = tl("Wp", [128, 2, 1024], BF16)
    Wpt, Wpb = Wp.nxt()
    load_weight_bf16(kb, tl, Wut, Wub, I["w_up"][0], 8, 4096, gcol=(gct, gcb))
    load_weight_bf16(kb, tl, Wdt, Wdb, I["w_down"][0], 32, 1024)
    load_weight_bf16(kb, tl, Wgt, Wgb, I["w_ple_gate"][0], 8, 1024)
    load_weight_bf16(kb, tl, Wpt, Wpb, I["w_ple"][0], 2, 1024)
    ident = tl("ident", [128, 128], BF16)
    idt, idb = ident.nxt()
    dma("sp", idt[:], I["ident_bf"][:, :], w=[idb])
    gbc = tl("gbc", [128, 1024], F32)
    gbt, gbb = gbc.nxt()
    dma("sp", gbt[:], I["g_mlp_post"].partition_broadcast(128), w=[gbb])

    x1 = tl("x1", [128, 2, 1024], F32, nbuf=1)
    xn = tl("xn", [128, 2, 1024], BF16, nbuf=2)
    hT = tl("hT", [128, 8, 256], BF16, nbuf=2)
    aT = tl("aT", [128, 32, 256], BF16)
    rl = tl("rl", [128, 512], BF16, nbuf=2)
    tmp = tl("tmp", [128, 1024], F32)
    pt_ = tl("pt", [128, 2, 256], F32)
    pb_ = tl("pb", [128, 2, 256], BF16)
    pT = tl("pT", [128, 2, 256], BF16)
    sg = tl("sg", [128, 512], F32, nbuf=1)
    ot = tl("ot", [128, 512], F32, nbuf=1)
    ss = tl("ss", [128, 2], F32, nbuf=2)
    rstd = tl("rstd", [128, 1], F32, nbuf=2)
    tp = tl("tp", [128, 1024], BF16, nbuf=2, psum=True)
    up = tl("up", [128, 512], F32, nbuf=2, psum=True)
    dn = tl("dn", [128, 512], F32, nbuf=2, psum=True)
    gp = tl("gp", [128, 512], F32, nbuf=1, psum=True)
    pp = tl("pp", [128, 512], F32, nbuf=1, psum=True)

    def transposes(srct, srcb, dstt, dstb):
        for m in range(2):
            tpt, tpb = tp.nxt()
            for kk_ in range(4):
                k = 4 * m + kk_
                for j in range(2):
                    op("pe", lambda e: e.transpose(tpt[:, kk_ * 256 + j * 128: kk_ * 256 + (j + 1) * 128],
                                                   srct[:, j, k * 128:(k + 1) * 128], idt[:]),
                       r=[srcb, idb], w=[tpb], sig=(kk_ == 3 and j == 1))
            if m % 2 == 0:
                op("act", lambda e: e.copy(out=dstt[:, 4 * m:4 * m + 4, :],
                                           in_=tpt[:].rearrange("p (a b) -> p a b", b=256)), r=[tpb], w=[dstb])
            else:
                op("dve", lambda e: e.tensor_copy(dstt[:, 4 * m:4 * m + 4, :],
                                                  tpt[:].rearrange("p (a b) -> p a b", b=256)), r=[tpb], w=[dstb])

    for ti in range(NB * S // 256):
        b = ti // (S // 256)
        t0 = (ti % (S // 256)) * 256
        x1t, x1b = x1.nxt()
        dma("sp", x1t[:], SC["X1"][b, t0:t0 + 256, :].rearrange("(j p) c -> p j c", p=128), w=[x1b])
        ptt, ptb = pt_.nxt()
        dma("sp", ptt[:], I["p"][0, b, t0:t0 + 256, :].rearrange("(j p) c -> p j c", p=128), w=[ptb])
        xnt, xnb = xn.nxt()
        dma("sp", xnt[:], SC["XN"][b, t0:t0 + 256, :].rearrange("(j p) c -> p j c", p=128), w=[xnb])
        hTt, hTb = hT.nxt()
        transposes(xnt, xnb, hTt, hTb)
        aTt, aTb = aT.nxt()
        for fp in range(16):
            ut, ub = up.nxt()
            for i in range(2):
                ffc = 2 * fp + i
                for k in range(8):
                    op("pe", lambda e: e.matmul(ut[:, i * 256:(i + 1) * 256], Wut[:, k, ffc * 128:(ffc + 1) * 128],
                                                hTt[:, k, :], start=(k == 0), stop=(k == 7)),
                       r=[Wub, hTb], w=[ub], sig=(k == 7 and i == 1), c=0.12)
            rlt, rlb = rl.nxt()
            op("act", lambda e: e.activation(out=rlt[:], in_=ut[:], func=AF.Relu), r=[ub], w=[rlb])
            op("dve" if fp % 4 != 3 else "pool", lambda e: e.tensor_tensor(
                out=aTt[:, 2 * fp:2 * fp + 2, :], in0=rlt[:].rearrange("p (a b) -> p a b", b=256),
                in1=rlt[:].rearrange("p (a b) -> p a b", b=256), op=ALU.mult), r=[rlb], w=[aTb])
        for j in range(2):
            sst, ssb = ss.nxt()
            rst, rsb = rstd.nxt()
            halves = []
            for hf in range(2):
                dt_, db_ = dn.nxt()
                for ffc in range(32):
                    op("pe", lambda e: e.matmul(dt_[:], aTt[:, ffc, j * 128:(j + 1) * 128],
                                                Wdt[:, ffc, hf * 512:(hf + 1) * 512], start=(ffc == 0),
                                                stop=(ffc == 31)), r=[aTb, Wdb], w=[db_], sig=(ffc == 31))
                jt, jb = rl.nxt()
                op("act", lambda e: e.activation(out=jt[:], in_=dt_[:], func=AF.Square,
                                                 accum_out=sst[:, hf:hf + 1]), r=[db_], w=[jb, ssb])
                halves.append((dt_, db_))
            rms_finish(kb, sst, ssb, rst, rsb, 2)
            tmt, tmb = tmp.nxt()
            for hf in range(2):
                dt_, db_ = halves[hf]
                op("dve", lambda e: e.scalar_tensor_tensor(out=tmt[:, hf * 512:(hf + 1) * 512], in0=dt_[:],
                                                           scalar=rst[:, 0:1], in1=gbt[:, hf * 512:(hf + 1) * 512],
                                                           op0=ALU.mult, op1=ALU.mult),
                   r=[db_, rsb, gbb], w=[tmb])
            op("dve", lambda e: e.tensor_tensor(out=x1t[:, j, :], in0=tmt[:], in1=x1t[:, j, :], op=ALU.add),
               r=[tmb, x1b], w=[x1b])
        xnt, xnb = xn.nxt()
        op("act", lambda e: e.copy(out=xnt[:], in_=x1t[:]), r=[x1b], w=[xnb])
        hTt, hTb = hT.nxt()
        transposes(xnt, xnb, hTt, hTb)
        pbt, pbb = pb_.nxt()
        op("dve", lambda e: e.tensor_copy(pbt[:], ptt[:]), r=[ptb], w=[pbb])
        tpt, tpb = tp.nxt()
        for kc in range(2):
            for j in range(2):
                op("pe", lambda e: e.transpose(tpt[:, kc * 256 + j * 128: kc * 256 + (j + 1) * 128],
                                               pbt[:, j, kc * 128:(kc + 1) * 128], idt[:]),
                   r=[pbb, idb], w=[tpb], sig=(kc == 1 and j == 1))
        pTt, pTb = pT.nxt()
        op("act", lambda e: e.copy(out=pTt[:], in_=tpt[:, 0:512].rearrange("p (a b) -> p a b", b=256)),
           r=[tpb], w=[pTb])
        for j in range(2):
            for hf in range(2):
                gt_, gb_ = gp.nxt()
                for k in range(8):
                    op("pe", lambda e: e.matmul(gt_[:], hTt[:, k, j * 128:(j + 1) * 128],
                                                Wgt[:, k, hf * 512:(hf + 1) * 512], start=(k == 0), stop=(k == 7)),
                       r=[hTb, Wgb], w=[gb_], sig=(k == 7))
                ppt, ppb = pp.nxt()
                for kc in range(2):
                    op("pe", lambda e: e.matmul(ppt[:], pTt[:, kc, j * 128:(j + 1) * 128],
                                                Wpt[:, kc, hf * 512:(hf + 1) * 512], start=(kc == 0), stop=(kc == 1)),
                       r=[pTb, Wpb], w=[ppb], sig=(kc == 1))
                sgt, sgb = sg.nxt()
                op("act", lambda e: e.activation(out=sgt[:], in_=gt_[:], func=AF.Sigmoid), r=[gb_], w=[sgb])
                ott, otb = ot.nxt()
                op("dve", lambda e: e.tensor_tensor(out=ott[:], in0=ppt[:], in1=sgt[:], op=ALU.mult),
                   r=[ppb, sgb], w=[otb])
                op("dve", lambda e: e.tensor_tensor(out=ott[:], in0=ott[:], in1=x1t[:, j, hf * 512:(hf + 1) * 512],
                                                    op=ALU.add), r=[otb, x1b], w=[otb])
                dma("pool", OUT[b, t0 + j * 128:t0 + (j + 1) * 128, hf * 512:(hf + 1) * 512], ott[:], r=[otb])


def nsa_dims(S):
    NCMP = (S - 32) // 16 + 1
    NNT = (NCMP + 127) // 128
    NSEL = S // 64
    return NCMP, NNT, NSEL, min(16, NSEL)


def nsa_consts(S):
    NCMP, NNT, NSEL, TOPK = nsa_dims(S)
    c = {}
    t = np.arange(S)
    qa = np.zeros((8, 4, S), np.float32)
    for h in range(8):
        sl = 2.0 ** (-(h + 1))
        qa[h, 0] = sl
        qa[h, 1] = sl
        qa[h, 2] = -sl * 128.0 * (t // 128)
        qa[h, 3] = -sl * (t % 128)
    c["QAUG"] = _bf(qa)
    ka = np.zeros((4, S), np.float32)
    ka[0] = 128.0 * (t // 128)
    ka[1] = t % 128
    ka[2] = 1.0
    ka[3] = 1.0
    c["KAUGP"] = _bf(ka)
    n = np.arange(NNT * 128)
    pos = 16 * n + 31
    kc = np.zeros((4, NNT * 128), np.float32)
    kc[0] = 128.0 * (pos // 128)
    kc[1] = pos % 128
    kc[2] = 1.0
    kc[3] = 1.0
    c["KAUGC"] = _bf(kc)
    et = np.zeros((128, S), np.float32)
    et[t // 64, t] = 1.0
    c["ETAB"] = _bf(et)
    sm = np.zeros((NNT * 128, 65), np.float32)
    c0 = np.arange(NCMP) * 16
    s0 = np.arange(NSEL) * 64
    ov = (np.minimum(c0[:, None] + 31, s0[None, :] + 63) - np.maximum(c0[:, None], s0[None, :]) + 1)
    sm[:NCMP, :NSEL] = np.clip(ov, 0, None) / 16.0
    sm[:NCMP, 64] = 1.0
    c["SELMAP"] = _bf(sm.reshape(NNT, 128, 65).transpose(1, 0, 2))
    cur = t // 64
    j = np.arange(NSEL)
    forced = (j[None, :] == 0) | (j[None, :] == cur[:, None]) | (j[None, :] == cur[:, None] - 1)
    fut = j[None, :] > cur[:, None]
    add = np.where(forced, 1e4, np.where(fut, -1.0, 0.0)).astype(np.float32)
    mul = np.where(forced | fut, 0.0, 1.0).astype(np.float32)
    c["SCADD"] = np.ascontiguousarray(add.reshape(S // 128, 128, NSEL))
    c["SCMUL"] = np.ascontiguousarray(mul.reshape(S // 128, 128, NSEL))
    return c


def phase_n(kb, S, I, SC):
    op, dma = kb.op, kb.dma
    NCMP, NNT, NSEL, TOPK = nsa_dims(S)
    NQT = S // 512
    NKT = S // 128

    def tl(name, shape, dt, nbuf=1, psum=False):
        return Tl(kb, name, shape, dt, nbuf, psum)

    def one(name, shape, dt, psum=False):
        return tl(name, shape, dt, 1, psum).nxt()

    idf, idfb = one("identf", [128, 128], F32)
    dma("sp", idf[:], I["ident_f"][:, :], w=[idfb])
    idbf, idbfb = one("identb", [128, 128], BF16)
    dma("sp", idbf[:], I["ident_bf"][:, :], w=[idbfb])
    etab, etb = one("etab", [128, S], BF16)
    dma("sp", etab[:], I["ETAB"][:, :], w=[etb])
    selmap, smb = one("selmap", [128, NNT, 65], BF16)
    dma("sp", selmap[:], I["SELMAP"][:, :, :], w=[smb])

    cw = {}
    for kv in ("k", "v"):
        w1f, w1fb = one("w1f" + kv, [64, 32, 128], F32)
        dma("sp", w1f[:], I["cmp_%s_w1" % kv][0].rearrange("(l d) h -> d l h", d=64), w=[w1fb])
        w1b, w1bb = one("w1b" + kv, [64, 32, 128], BF16)
        op("dve", lambda e: e.tensor_copy(w1b[:], w1f[:]), r=[w1fb], w=[w1bb])
        peT, peb = one("peT" + kv, [64, 32], F32)
        dma("sp", peT[:], I["cmp_pe_" + kv][0].rearrange("l d -> d l"), w=[peb], allow_slow_non_contiguous=True)
        b1c, b1b = one("b1c" + kv, [128, 1], F32)
        dma("sp", b1c[:], I["cmp_%s_b1" % kv].rearrange("o h -> h o"), w=[b1b], allow_slow_non_contiguous=True)
        w2f, w2fb = one("w2f" + kv, [128, 64], F32)
        dma("sp", w2f[:], I["cmp_%s_w2" % kv][0], w=[w2fb])
        w2b, w2bb = one("w2b" + kv, [128, 64], BF16)
        op("dve", lambda e: e.tensor_copy(w2b[:], w2f[:]), r=[w2fb], w=[w2bb])
        w2p, w2pb = one("w2p" + kv, [128, 128], BF16)
        op("dve", lambda e: e.memset(w2p[:], 0.0), w=[w2pb])
        op("dve", lambda e: e.tensor_copy(w2p[:, 64:128], w2f[:]), r=[w2fb], w=[w2pb])
        cw[kv] = dict(w1f=(w1f, w1fb), w1b=(w1b, w1bb), peT=(peT, peb), b1=(b1c, b1b), w2b=(w2b, w2bb),
                      w2p=(w2p, w2pb))

    sps = tl("sps", [128, 512], F32, nbuf=3, psum=True)
    ops_ = tl("ops", [128, 512], F32, nbuf=2, psum=True)
    trpt, _ = one("trp", [128, 8, 128], BF16, psum=True)
    _tb = Buf("trp0", excl=True)
    trpbufs = [_tb, _tb]
    fincnt = [0]
    stp = tl("stp", [128, 512], F32, nbuf=1, psum=True)
    ipp = tl("ipp", [128, 4, 65], F32, nbuf=1, psum=True)

    for kv in ("k", "v"):
        pt, pb = sps.nxt()
        w1f, w1fb = cw[kv]["w1f"]
        peT, peb = cw[kv]["peT"]
        for l in range(32):
            op("pe", lambda e: e.matmul(pt[:, 0:1], w1f[:, l, :], peT[:, l:l + 1], start=(l == 0), stop=(l == 31)),
               r=[w1fb, peb], w=[pb], sig=(l == 31))
        bt, btb = one("btot" + kv, [128, 1], F32)
        op("dve", lambda e: e.tensor_tensor(out=bt[:], in0=pt[:, 0:1], in1=cw[kv]["b1"][0][:], op=ALU.add),
           r=[pb, cw[kv]["b1"][1]], w=[btb])
        cw[kv]["bt"] = (bt, btb)

    ksl, kslb = one("ksl", [128, S], BF16)
    kwn, kwnb = one("kwn", [128, S], BF16)
    op("dve", lambda e: e.memset(kwn[:], 0.0), w=[kwnb])
    dma("sp", ksl[0:60, :], I["ETAB"][0:60, :], w=[kslb])
    dma("sp", ksl[60:64, :], I["KAUGP"][:, :], w=[kslb])
    dma("sp", kwn[60:64, :], I["KAUGP"][:, :], w=[kwnb])
    vsl, vslb = one("vsl", [128, NKT, 128], BF16)
    vwn, vwnb = one("vwn", [128, NKT, 128], BF16)
    qa = [one("qa%d" % r, [128, S], BF16) for r in range(4)]
    qasel = [Buf("qasel%d" % r) for r in range(4)]
    for r in range(4):
        op("dve", lambda e: e.memset(qa[r][0][:], 0.0), w=[qa[r][1], qasel[r]])
    kct, kctb = one("kct", [64, S], BF16)
    vct, vctb = one("vct", [64, S], BF16)
    hTk, hTkb = one("hTk", [128, NNT * 128], BF16)
    hTv, hTvb = one("hTv", [128, NNT * 128], BF16)
    op("dve", lambda e: e.memset(hTk[:], 0.0), w=[hTkb])
    op("dve", lambda e: e.memset(hTv[:], 0.0), w=[hTvb])
    kca, kcab = one("kca", [128, NNT * 128], BF16)
    op("dve", lambda e: e.memset(kca[:], 0.0), w=[kcab])
    dma("sp", kca[60:64, :], I["KAUGC"][:, :], w=[kcab])
    vca, vcab = one("vca", [128, NNT, 128], BF16)
    op("dve", lambda e: e.memset(vca[:], 0.0), w=[vcab])
    selbT, selbTb = one("selbT", [128, S], BF16)
    pT = tl("pT", [128, 512], BF16, nbuf=4)
    ccl = tl("ccl", [128, 512], F32, nbuf=2)
    osb = tl("osb", [128, 512], BF16, nbuf=2)
    gate = tl("gate", [128, 4, 24], F32, nbuf=2)
    imp, impb = one("imp", [128, 4, 64], F32)
    rec = tl("rec", [128, 4], F32, nbuf=2)
    coef = tl("coef", [128, 4], F32, nbuf=2)
    ytl = tl("ytl", [128, 4, 256], F32, nbuf=2)
    ybf = tl("ybf", [128, 4, 256], BF16, nbuf=2)
    scm = tl("scm", [128, NSEL], F32, nbuf=2)
    sca = tl("sca", [128, NSEL], F32, nbuf=2)
    score = tl("score", [128, NSEL], F32, nbuf=2)
    repl = tl("repl", [128, NSEL], F32, nbuf=2)
    mx = tl("mx", [128, 16], F32, nbuf=2)
    sbias = tl("sbias", [128, 128], F32, nbuf=4)
    for i_ in range(4):
        t_, b_ = sbias.nxt()
        op("dve", lambda e: e.memset(t_[:], 0.0), w=[b_])

    for b in range(NB):
        for g in range(2):
            dma("sp", ksl[64:128, :], SC["KST"][b, g * 64:(g + 1) * 64, :], w=[kslb])
            dma("sp", kwn[64:128, :], SC["KWT"][b, g * 64:(g + 1) * 64, :], w=[kwnb])
            dma("sp", vsl[:], SC["VSA"][b, g], w=[vslb])
            dma("sp", vwn[:], SC["VWA"][b, g], w=[vwnb])
            for r in range(4):
                h = 4 * g + r
                dma("sp", qa[r][0][64:128, :], SC["QT"][b, h * 64:(h + 1) * 64, :], w=[qa[r][1]])
                dma("sp", qa[r][0][60:64, :], I["QAUG"][h], w=[qa[r][1]])
            dma("sp", kct[:], SC["KCT"][b, g * 64:(g + 1) * 64, :], w=[kctb])
            dma("sp", vct[:], SC["VCT"][b, g * 64:(g + 1) * 64, :], w=[vctb])
            for kv, src, srcb, hT_, hTb_ in (("k", kct, kctb, hTk, hTkb), ("v", vct, vctb, hTv, hTvb)):
                pt, pb = sps.nxt()
                w1b, w1bb = cw[kv]["w1b"]
                for l in range(32):
                    op("pe", lambda e: e.matmul(pt[:, 0:NCMP], w1b[:, l, :], src[:, l:l + 16 * (NCMP - 1) + 1:16],
                                                start=(l == 0), stop=(l == 31)),
                       r=[w1bb, srcb], w=[pb], sig=(l == 31))
                op("act", lambda e: e.activation(out=hT_[:, 0:NCMP], in_=pt[:, 0:NCMP], func=AF.Gelu_apprx_tanh,
                                                 bias=cw[kv]["bt"][0][:, 0:1]), r=[pb, cw[kv]["bt"][1]], w=[hTb_])
            pt, pb = sps.nxt()
            op("pe", lambda e: e.matmul(pt[:, 0:NCMP], cw["k"]["w2p"][0][:], hTk[:, 0:NCMP], start=True, stop=True),
               r=[cw["k"]["w2p"][1], hTkb], w=[pb])
            op("act", lambda e: e.copy(out=kca[64:128, 0:NCMP], in_=pt[64:128, 0:NCMP]), r=[pb], w=[kcab])
            pt, pb = sps.nxt()
            for nt in range(NNT):
                op("pe", lambda e: e.matmul(pt[:, nt * 64:(nt + 1) * 64], hTv[:, nt * 128:(nt + 1) * 128],
                                            cw["v"]["w2b"][0][:], start=True, stop=True),
                   r=[cw["v"]["w2b"][1], hTvb], w=[pb], sig=(nt == NNT - 1))
            op("act", lambda e: e.copy(out=vca[:, :, 0:64],
                                       in_=pt[:, 0:NNT * 64].rearrange("p (a d) -> p a d", d=64)), r=[pb], w=[vcab])
            for nt in range(NNT):
                nn = min(128, NCMP - nt * 128)
                op("dve", lambda e: e.memset(vca[0:nn, nt, 64:128], 1.0), w=[vcab])

            for qt in range(NQT):
                q0 = qt * 512
                gt, gtb = gate.nxt()
                dma("sp", gt[:], SC["GATE"][b, q0:q0 + 512, :].rearrange("(j p) c -> p j c", p=128), w=[gtb])
                yt, ytb = ytl.nxt()

                def fin(ot_, ob_, r, br):
                    h = 4 * g + r
                    st_, sb_ = osb.nxt()
                    op("dve", lambda e: e.tensor_copy(st_[:], ot_[:]), r=[ob_], w=[sb_])
                    so = 4 * (fincnt[0] % 2)
                    tb_ = trpbufs[fincnt[0] % 2]
                    fincnt[0] += 1
                    tt_ = trpt[:, so:so + 4, :]
                    for sub in range(4):
                        op("pe", lambda e: e.transpose(tt_[:, sub, :], st_[:, sub * 128:(sub + 1) * 128], idbf[:]),
                           r=[sb_, idbfb], w=[tb_], sig=(sub == 3), c=0.08)
                    rt_, rb_ = rec.nxt()
                    op("dve", lambda e: e.tensor_scalar(out=rt_[:], in0=tt_[:, :, 64], scalar1=1e-30, scalar2=None,
                                                        op0=ALU.max), r=[tb_], w=[rb_])
                    op("dve", lambda e: e.reciprocal(out=rt_[:], in_=rt_[:]), r=[rb_], w=[rb_])
                    ct_, cb_ = coef.nxt()
                    op("dve", lambda e: e.tensor_tensor(out=ct_[:], in0=rt_[:], in1=gt[:, :, 3 * h + br], op=ALU.mult),
                       r=[rb_, gtb], w=[cb_])
                    for sub in range(4):
                        if br == 0:
                            op("dve", lambda e: e.tensor_scalar(out=yt[:, sub, r * 64:(r + 1) * 64],
                                                                in0=tt_[:, sub, 0:64], scalar1=ct_[:, sub:sub + 1],
                                                                scalar2=None, op0=ALU.mult),
                               r=[tb_, cb_], w=[ytb])
                        else:
                            op("dve", lambda e: e.scalar_tensor_tensor(out=yt[:, sub, r * 64:(r + 1) * 64],
                                                                       in0=tt_[:, sub, 0:64],
                                                                       scalar=ct_[:, sub:sub + 1],
                                                                       in1=yt[:, sub, r * 64:(r + 1) * 64],
                                                                       op0=ALU.mult, op1=ALU.add),
                               r=[tb_, cb_, ytb], w=[ytb])

                nts = [nt for nt in range(NNT) if 16 * (nt * 128) + 31 <= q0 + 511]
                for r in range(4):
                    ot_, ob_ = ops_.nxt()
                    pts = []
                    for ii, nt in enumerate(nts):
                        st_, sb_ = sps.nxt()
                        op("pe", lambda e: e.matmul(st_[:], kca[:, nt * 128:(nt + 1) * 128], qa[r][0][:, q0:q0 + 512],
                                                    start=True, stop=True), r=[kcab, qa[r][1]], w=[sb_])
                        pt_, pb_ = pT.nxt()
                        if not (16 * (nt * 128 + 127) + 31 <= q0):
                            ct_, cb_ = ccl.nxt()
                            op("dve", lambda e: e.tensor_scalar(out=ct_[:], in0=st_[:], scalar1=80.0, scalar2=None,
                                                                op0=ALU.min), r=[sb_], w=[cb_])
                            op("act", lambda e: e.activation(out=pt_[:], in_=ct_[:], func=AF.Exp), r=[cb_], w=[pb_])
                        else:
                            op("act", lambda e: e.activation(out=pt_[:], in_=st_[:], func=AF.Exp), r=[sb_], w=[pb_])
                        if not (16 * (nt * 128 + 127) + 31 <= q0):
                            op("pool", lambda e: e.affine_select(out=pt_[:], in_=pt_[:], pattern=[[1, 512]],
                                                                 compare_op=ALU.is_ge, fill=0.0,
                                                                 base=q0 - 16 * nt * 128 - 31,
                                                                 channel_multiplier=-16), r=[pb_], w=[pb_], c=0.6)
                        op("pe", lambda e: e.matmul(ot_[:], vca[:, nt, :], pt_[:], start=(ii == 0),
                                                    stop=(ii == len(nts) - 1)),
                           r=[vcab, pb_], w=[ob_])
                        pts.append((pt_, pb_))
                    it_, ib_ = ipp.nxt()
                    for sub in range(4):
                        for ii, nt in enumerate(nts):
                            op("pe", lambda e: e.matmul(it_[:, sub, :], pts[ii][0][:, sub * 128:(sub + 1) * 128],
                                                        selmap[:, nt, :], start=(ii == 0), stop=(ii == len(nts) - 1)),
                               r=[pts[ii][1], smb], w=[ib_], sig=(sub == 3 and ii == len(nts) - 1))
                    rt_, rb_ = rec.nxt()
                    op("dve", lambda e: e.tensor_scalar(out=rt_[:], in0=it_[:, :, 64], scalar1=1e-30, scalar2=None,
                                                        op0=ALU.max), r=[ib_], w=[rb_])
                    op("dve", lambda e: e.reciprocal(out=rt_[:], in_=rt_[:]), r=[rb_], w=[rb_])
                    for sub in range(4):
                        if r == 0:
                            op("dve", lambda e: e.tensor_scalar(out=imp[:, sub, :], in0=it_[:, sub, 0:64],
                                                                scalar1=rt_[:, sub:sub + 1], scalar2=None,
                                                                op0=ALU.mult), r=[ib_, rb_], w=[impb])
                        else:
                            op("dve", lambda e: e.scalar_tensor_tensor(out=imp[:, sub, :], in0=it_[:, sub, 0:64],
                                                                       scalar=rt_[:, sub:sub + 1], in1=imp[:, sub, :],
                                                                       op0=ALU.mult, op1=ALU.add),
                               r=[ib_, rb_, impb], w=[impb])
                    fin(ot_, ob_, r, 0)
                def stream(r, br, ktile, vtile, ktb, vtb, tiles, fold):
                    ot_, ob_ = ops_.nxt()
                    for ii, (kt, qlo, qhi, selk) in enumerate(tiles):
                        k0 = kt * 128
                        c0, c1 = qlo - q0, qhi - q0
                        nq = qhi - qlo
                        cc = 0.08 + 0.12 * nq / 512.0
                        st_, sb_ = sps.nxt()
                        rb_ = [ktb, qa[r][1]] + ([qasel[r]] if fold else [])
                        if fold and kt * 2 + 1 >= 60:
                            op("pe", lambda e: e.matmul(st_[:, c0:c1], ktile[:, k0:k0 + 128], qa[r][0][:, qlo:qhi],
                                                        start=True, stop=False), r=rb_, w=[sb_], sig=False, c=cc)
                            op("pe", lambda e: e.matmul(st_[:, c0:c1], etab[:, k0:k0 + 128], selbT[:, qlo:qhi],
                                                        start=False, stop=True), r=[etb, selbTb], w=[sb_], c=cc)
                        else:
                            op("pe", lambda e: e.matmul(st_[:, c0:c1], ktile[:, k0:k0 + 128], qa[r][0][:, qlo:qhi],
                                                        start=True, stop=True), r=rb_, w=[sb_], c=cc)
                        pt_, pb_ = pT.nxt()
                        op("act", lambda e: e.activation(out=pt_[:, c0:c1], in_=st_[:, c0:c1], func=AF.Exp),
                           r=[sb_], w=[pb_], c=0.12 + 0.45 * nq / 512.0)
                        if selk == 1:
                            op("pool", lambda e: e.affine_select(out=pt_[:, c0:c1], in_=pt_[:, c0:c1],
                                                                 pattern=[[1, nq]], compare_op=ALU.is_ge, fill=0.0,
                                                                 base=qlo - k0, channel_multiplier=-1),
                               r=[pb_], w=[pb_], c=0.15 + 0.45 * nq / 512.0)
                        elif selk == 2:
                            op("pool", lambda e: e.affine_select(out=pt_[:, c0:c1], in_=pt_[:, c0:c1],
                                                                 pattern=[[-1, nq]], compare_op=ALU.is_ge, fill=0.0,
                                                                 base=k0 - qlo + 511, channel_multiplier=1),
                               r=[pb_], w=[pb_], c=0.15 + 0.45 * nq / 512.0)
                        op("pe", lambda e: e.matmul(ot_[:, c0:c1], vtile[:, kt, :], pt_[:, c0:c1], start=(ii == 0),
                                                    stop=(ii == len(tiles) - 1)),
                           r=[vtb, pb_], w=[ob_], c=cc)
                    fin(ot_, ob_, r, br)

                def win_tiles():
                    full, part = [], []
                    for kt in range(max(0, (q0 - 512) // 128), (q0 + 511) // 128 + 1):
                        k0 = kt * 128
                        if k0 >= q0:
                            t_ = (kt, max(q0, k0), q0 + 512, 1)
                        else:
                            t_ = (kt, q0, min(q0 + 512, k0 + 128 + 511), 2)
                        (full if t_[2] - t_[1] == 512 else part).append(t_)
                    return full + part

                def sel_tiles():
                    full, part = [], []
                    for kt in range(0, (q0 + 511) // 128 + 1):
                        k0 = kt * 128
                        if k0 + 127 > q0:
                            t_ = (kt, max(q0, k0), q0 + 512, 1)
                        else:
                            t_ = (kt, q0, q0 + 512, 0)
                        (full if t_[2] - t_[1] == 512 else part).append(t_)
                    return full + part

                for r in range(4):
                    stream(r, 2, kwn, vwn, kwnb, vwnb, win_tiles(), False)
                spt, spb = stp.nxt()
                for sub in range(4):
                    q128 = qt * 4 + sub
                    mt_, mb_ = scm.nxt()
                    at_, ab_ = sca.nxt()
                    dma("sp", mt_[:], I["SCMUL"][q128], w=[mb_])
                    dma("sp", at_[:], I["SCADD"][q128], w=[ab_])
                    sct, scb = score.nxt()
                    op("dve", lambda e: e.tensor_tensor(out=sct[:], in0=imp[:, sub, 0:NSEL], in1=mt_[:], op=ALU.mult),
                       r=[impb, mb_], w=[scb])
                    op("dve", lambda e: e.tensor_tensor(out=sct[:], in0=sct[:], in1=at_[:], op=ALU.add),
                       r=[scb, ab_], w=[scb])
                    mxt, mxb = mx.nxt()
                    op("dve", lambda e: e.max(out=mxt[:, 0:8], in_=sct[:]), r=[scb], w=[mxb])
                    if TOPK == 16:
                        rpt, rpb = repl.nxt()
                        op("dve", lambda e: e.match_replace(out=rpt[:], in_to_replace=mxt[:, 0:8], in_values=sct[:],
                                                            imm_value=-1e30), r=[mxb, scb], w=[rpb])
                        op("dve", lambda e: e.max(out=mxt[:, 8:16], in_=rpt[:]), r=[rpb], w=[mxb])
                        thr = mxt[:, 15:16]
                    else:
                        thr = mxt[:, 7:8]
                    sbt, sbb = sbias.nxt()
                    op("dve", lambda e: e.tensor_scalar(out=sbt[:, 0:NSEL], in0=sct[:], scalar1=thr, scalar2=-30000.0,
                                                        op0=ALU.is_lt, op1=ALU.mult), r=[scb, mxb], w=[sbb])
                    op("pe", lambda e: e.transpose(spt[:, sub * 128:(sub + 1) * 128], sbt[:], idf[:]),
                       r=[sbb, idfb], w=[spb], sig=(sub == 3))
                op("act", lambda e: e.copy(out=selbT[:, q0:q0 + 512], in_=spt[:]), r=[spb], w=[selbTb])
                for r in range(4):
                    if r % 2 == 0:
                        op("dve", lambda e: e.tensor_copy(qa[r][0][0:60, q0:q0 + 512], spt[0:60, :]),
                           r=[spb], w=[qasel[r]])
                    else:
                        op("act", lambda e: e.copy(out=qa[r][0][0:60, q0:q0 + 512], in_=spt[0:60, :]),
                           r=[spb], w=[qasel[r]])
                for r in range(4):
                    stream(r, 1, ksl, vsl, kslb, vslb, sel_tiles(), True)
                ybt, ybb = ybf.nxt()
                op("act", lambda e: e.copy(out=ybt[:], in_=yt[:]), r=[ytb], w=[ybb])
                dma("pool", SC["YMIX"][b, q0:q0 + 512, g * 256:(g + 1) * 256].rearrange("(j p) c -> p j c", p=128),
                    ybt[:], r=[ybb])


def rwkv_consts():
    c = {}
    p = np.arange(128) % 64
    t = np.arange(64)
    c["MU_S"] = (p[:, None] < t[None, :]).astype(np.float32)
    c["MU_I"] = (p[:, None] <= t[None, :]).astype(np.float32)
    c["ML_S"] = (p[:, None] > t[None, :]).astype(np.float32)
    idm = (p[:, None] == t[None, :]).astype(np.float32)
    c["ID64F"] = np.ascontiguousarray(np.broadcast_to(idm[:, None, :], (128, 8, 64))).astype(np.float32)
    c["ID64B"] = _bf(c["ID64F"])
    return c


def phase_r(kb, S, I, SC):
    op, dma = kb.op, kb.dma
    MC = 4
    NMAC = S // (64 * MC)
    NCH = S // 64

    def tl(name, shape, dt, nbuf=1, psum=False):
        return Tl(kb, name, shape, dt, nbuf, psum)

    def one(name, shape, dt, psum=False):
        return tl(name, shape, dt, 1, psum).nxt()

    masks = {}
    for nm in ("MU_S", "MU_I", "ML_S"):
        t_, b_ = one(nm, [128, 64], F32)
        dma("sp", t_[:], I[nm][:, :], w=[b_])
        masks[nm] = (t_, b_)
    idf, idfb = one("id64f", [128, 8, 64], F32)
    dma("sp", idf[:], I["ID64F"][:, :, :], w=[idfb])
    idb, idbb = one("id64b", [128, 8, 64], BF16)
    dma("sp", idb[:], I["ID64B"][:, :, :], w=[idbb])
    lnw, lnwb = one("lnw", [128, 512], F32)
    dma("sp", lnw[:], I["lnx_w"].partition_broadcast(128), w=[lnwb])
    lnb, lnbb = one("lnb", [128, 512], F32)
    dma("sp", lnb[:], I["lnx_b"].partition_broadcast(128), w=[lnbb])

    FM = tl("FM", [128, 4, 8, MC * 64], BF16, nbuf=3)
    TM = tl("TM", [128, 4, MC, 512], BF16, nbuf=3)
    PCt = tl("PCt", [128, 8, MC], F32, nbuf=3)
    GTt = tl("GTt", [128, MC, 512], BF16, nbuf=3)
    BONt = tl("BONt", [128, MC, 8], F32, nbuf=3)
    ps = tl("ps", [128, 512], F32, nbuf=8, psum=True)

    def bt(name, nbuf=3):
        return tl(name, [128, 8, 64], BF16, nbuf=nbuf)

    NN, ARB, ARK, AAK = bt("NN", 4), bt("ARB", 4), bt("ARK", 4), bt("AAK", 4)
    Ys, Pbs = bt("Ys", 4), bt("Pbs", 3)
    XPs = tl("XPs", [128, 8, 2, 64], BF16, nbuf=4)
    Pm = tl("Pm", [128, 8, 64], F32, nbuf=4)
    WT, Gs, Ub = bt("WT", 4), bt("Gs", 4), bt("Ub", 3)
    Xl = tl("Xl", [128, 8, 64], F32, nbuf=4)
    M, Mb_ = one("M", [128, 8, 64], F32)
    Mbf, Mbfb = one("Mbf", [128, 8, 64], BF16)
    op("dve", lambda e: e.memset(M[:], 0.0), w=[Mb_])
    op("dve", lambda e: e.memset(Mbf[:], 0.0), w=[Mbfb])
    Mtmp = tl("Mtmp", [128, 8, 64], F32, nbuf=2)
    yt = tl("yt", [128, 8, 64], F32, nbuf=4)
    g1 = tl("g1", [128, 8, 64], F32, nbuf=2)
    g2 = tl("g2", [128, 8, 64], F32, nbuf=2)
    st1 = tl("st1", [128, 8], F32, nbuf=2)
    st2 = tl("st2", [128, 8], F32, nbuf=2)
    st3 = tl("st3", [128, 8], F32, nbuf=2)
    yo = tl("yo", [128, 8, 64], BF16, nbuf=3)

    macro = {}

    def load_macro(m):
        t0 = m * MC * 64
        fm, fmb = FM.nxt()
        tm, tmb = TM.nxt()
        pc, pcb = PCt.nxt()
        gt, gtb = GTt.nxt()
        bo, bob = BONt.nxt()
        for b in range(NB):
            ph = slice(b * 64, (b + 1) * 64)
            for kind in range(4):
                dma("sp", fm[ph, kind, :, :],
                    SC["RWF"][b, kind].rearrange("(h k) t -> k h t", k=64)[:, :, t0:t0 + MC * 64], w=[fmb])
                dma("sp", tm[ph, kind, :, :],
                    SC["RWT"][b, kind, t0:t0 + MC * 64, :].rearrange("(c t) ch -> t c ch", t=64), w=[tmb])
            dma("sp", pc[ph, :, :], SC["PC"][b].rearrange("(h k) c -> k h c", k=64)[:, :, m * MC:(m + 1) * MC],
                w=[pcb], allow_slow_non_contiguous=True)
            dma("sp", gt[ph, :, :], SC["GT"][b, t0:t0 + MC * 64, :].rearrange("(c t) ch -> t c ch", t=64), w=[gtb])
            dma("sp", bo[ph, :, :], SC["BON"][b, t0:t0 + MC * 64, :].rearrange("(c t) ch -> t c ch", t=64), w=[bob])
        macro[m] = dict(fm=(fm, fmb), tm=(tm, tmb), pc=(pc, pcb), gt=(gt, gtb), bo=(bo, bob))

    units = [(b, h) for h in range(8) for b in range(NB)]
    pre = {}

    def mm_all(dst, dstb, lhs_fn, rhs_fn, rbufs):
        for i, (b, h) in enumerate(units):
            ph = slice(b * 64, (b + 1) * 64)
            op("pe", lambda e: e.matmul(dst[ph, h * 64:(h + 1) * 64], lhs_fn(ph, h), rhs_fn(ph, h),
                                        start=True, stop=True), r=rbufs, w=[dstb], sig=(i == len(units) - 1), c=0.055)

    def v3(t_):
        return t_[:].rearrange("p (h c) -> p h c", c=64)

    def stage_p(c):
        m, cl = c // MC, c % MC
        fm, fmb = macro[m]["fm"]
        tm, tmb = macro[m]["tm"]
        cs = slice(cl * 64, (cl + 1) * 64)
        outs = {}
        XP, XPb = XPs.nxt()
        for lk, rk, tile_, mask in ((1, 0, None, "MU_S"), (1, 3, ARB, "MU_I"), (2, 0, AAK, "MU_S"),
                                    (2, 3, ARK, "MU_I"), (0, 1, NN, "ML_S")):
            pt, pb = ps.nxt()
            mm_all(pt, pb, lambda ph, h: fm[ph, lk, h, cs], lambda ph, h: fm[ph, rk, h, cs], [fmb])
            mk, mkb = masks[mask]
            if tile_ is None:
                op("dve", lambda e: e.tensor_tensor(out=XP[:, :, 0, :], in0=v3(pt),
                                                    in1=mk[:].unsqueeze(1).to_broadcast([128, 8, 64]), op=ALU.mult),
                   r=[pb, mkb], w=[XPb])
            else:
                ot, ob = tile_.nxt()
                op("dve", lambda e: e.tensor_tensor(out=ot[:], in0=v3(pt),
                                                    in1=mk[:].unsqueeze(1).to_broadcast([128, 8, 64]), op=ALU.mult),
                   r=[pb, mkb], w=[ob])
                outs[tile_] = (ot, ob)
        op("act", lambda e: e.copy(out=XP[:, :, 1, :], in_=idb[:]), r=[idbb], w=[XPb])
        yield
        Y, Yb = outs[NN]
        P, Pmb = Pm.nxt()
        op("dve", lambda e: e.tensor_copy(P[:], idf[:]), r=[idfb], w=[Pmb])
        TT, TTb = None, None
        for i in range(6):
            last = (i == 5)
            if not last:
                banks = [ps.nxt(), ps.nxt()]
                for bi in range(2):
                    bk, bkb = banks[bi]
                    us = [(b, h) for h in range(4 * bi, 4 * bi + 4) for b in range(NB)]
                    for ii, (b, h) in enumerate(us):
                        ph = slice(b * 64, (b + 1) * 64)
                        hh = h % 4
                        op("pe", lambda e: e.matmul(bk[ph, hh * 128:(hh + 1) * 128], Y[ph, h, :],
                                                    XP[ph, h, :, :].rearrange("p a c -> p (a c)"),
                                                    start=True, stop=True),
                           r=[Yb, XPb], w=[bkb], sig=(ii == len(us) - 1), c=0.06)
                Cp, Cpb = ps.nxt()
                mm_all(Cp, Cpb, lambda ph, h: XP[ph, h, 0, :], lambda ph, h: Y[ph, h, :], [Yb, XPb])
                XPn, XPnb = XPs.nxt()
                for bi in range(2):
                    bk, bkb = banks[bi]
                    bv = bk[:].rearrange("p (h a c) -> p h a c", a=2, c=64)
                    hs_ = slice(4 * bi, 4 * bi + 4)
                    op("dve", lambda e: e.tensor_tensor(out=P[:, hs_, :], in0=bv[:, :, 1, :], in1=P[:, hs_, :],
                                                        op=ALU.add), r=[bkb, Pmb], w=[Pmb], c=0.35)
                    op("dve", lambda e: e.tensor_copy(XPn[:, hs_, 0, :], bv[:, :, 0, :]), r=[bkb], w=[XPnb], c=0.3)
                op("act", lambda e: e.copy(out=XPn[:, :, 1, :], in_=P[:]), r=[Pmb], w=[XPnb])
                Yn, Ynb = Ys.nxt()
                op("act", lambda e: e.copy(out=Yn[:], in_=v3(Cp)), r=[Cpb], w=[Ynb])
                XP, XPb, Y, Yb = XPn, XPnb, Yn, Ynb
            else:
                Bp, Bpb = ps.nxt()
                mm_all(Bp, Bpb, lambda ph, h: Y[ph, h, :], lambda ph, h: XP[ph, h, 1, :], [Yb, XPb])
                op("dve", lambda e: e.tensor_tensor(out=P[:], in0=v3(Bp), in1=P[:], op=ALU.add),
                   r=[Bpb, Pmb], w=[Pmb])
                TT, TTb = Pbs.nxt()
                op("act", lambda e: e.copy(out=TT[:], in_=P[:]), r=[Pmb], w=[TTb])
            yield
        Wp, Wpb = ps.nxt()
        mm_all(Wp, Wpb, lambda ph, h: tm[ph, 0, cl, h * 64:(h + 1) * 64], lambda ph, h: TT[ph, h, :], [tmb, TTb])
        Gp, Gpb = ps.nxt()
        aak, aakb = outs[AAK]
        mm_all(Gp, Gpb, lambda ph, h: aak[ph, h, :], lambda ph, h: tm[ph, 3, cl, h * 64:(h + 1) * 64], [aakb, tmb])
        wt, wtb = WT.nxt()
        op("act", lambda e: e.copy(out=wt[:], in_=v3(Wp)), r=[Wpb], w=[wtb])
        gs, gsb = Gs.nxt()
        op("dve", lambda e: e.tensor_copy(gs[:], v3(Gp)), r=[Gpb], w=[gsb])
        yield
        Xp, Xpb = ps.nxt()
        mm_all(Xp, Xpb, lambda ph, h: TT[ph, h, :], lambda ph, h: gs[ph, h, :], [TTb, gsb])
        xl, xlb = Xl.nxt()
        op("act", lambda e: e.copy(out=xl[:], in_=v3(Xp)), r=[Xpb], w=[xlb])
        pre[c] = dict(wt=(wt, wtb), xl=(xl, xlb), arb=outs[ARB], ark=outs[ARK])
        yield

    ych = {}

    def stage_q(c):
        m, cl = c // MC, c % MC
        fm, fmb = macro[m]["fm"]
        tm, tmb = macro[m]["tm"]
        pc, pcb = macro[m]["pc"]
        cs = slice(cl * 64, (cl + 1) * 64)
        pr = pre.pop(c)
        wt, wtb = pr["wt"]
        xl, xlb = pr["xl"]
        arb, arbb = pr["arb"]
        ark, arkb = pr["ark"]
        Up, Upb = ps.nxt()
        mm_all(Up, Upb, lambda ph, h: wt[ph, h, :], lambda ph, h: Mbf[ph, h, :], [wtb, Mbfb])
        ub, ubb = Ub.nxt()
        op("dve", lambda e: e.tensor_tensor(out=ub[:], in0=v3(Up), in1=xl[:], op=ALU.add), r=[Upb, xlb], w=[ubb])
        mt_, mtb_ = Mtmp.nxt()
        op("pool", lambda e: e.tensor_tensor(out=mt_[:], in0=M[:],
                                             in1=pc[:, :, cl].unsqueeze(2).to_broadcast([128, 8, 64]), op=ALU.mult),
           r=[Mb_, pcb], w=[mtb_])
        yield
        Yp, Ypb = ps.nxt()
        Mp, Mpb = ps.nxt()
        for i, (b, h) in enumerate(units):
            ph = slice(b * 64, (b + 1) * 64)
            hs = slice(h * 64, (h + 1) * 64)
            op("pe", lambda e: e.matmul(Yp[ph, hs], fm[ph, 3, h, cs], Mbf[ph, h, :], start=True, stop=False),
               r=[fmb, Mbfb], w=[Ypb], sig=False, c=0.055)
            op("pe", lambda e: e.matmul(Yp[ph, hs], ark[ph, h, :], tm[ph, 3, cl, hs], start=False, stop=False),
               r=[arkb, tmb], w=[Ypb], sig=False, c=0.055)
            op("pe", lambda e: e.matmul(Yp[ph, hs], arb[ph, h, :], ub[ph, h, :], start=False, stop=True),
               r=[arbb, ubb], w=[Ypb], sig=(i == len(units) - 1), c=0.055)
        for i, (b, h) in enumerate(units):
            ph = slice(b * 64, (b + 1) * 64)
            hs = slice(h * 64, (h + 1) * 64)
            op("pe", lambda e: e.matmul(Mp[ph, hs], tm[ph, 2, cl, hs], tm[ph, 3, cl, hs], start=True, stop=False),
               r=[tmb], w=[Mpb], sig=False, c=0.055)
            op("pe", lambda e: e.matmul(Mp[ph, hs], tm[ph, 1, cl, hs], ub[ph, h, :], start=False, stop=True),
               r=[tmb, ubb], w=[Mpb], sig=(i == len(units) - 1), c=0.055)
        op("dve", lambda e: e.tensor_tensor(out=M[:], in0=v3(Mp), in1=mt_[:], op=ALU.add), r=[Mpb, mtb_], w=[Mb_])
        op("act", lambda e: e.copy(out=Mbf[:], in_=M[:]), r=[Mb_], w=[Mbfb])
        y_, yb_ = yt.nxt()
        op("act", lambda e: e.copy(out=y_[:], in_=v3(Yp)), r=[Ypb], w=[yb_])
        ych[c] = (y_, yb_)
        yield

    def stage_g(c):
        m, cl = c // MC, c % MC
        tm, tmb = macro[m]["tm"]
        gt, gtb = macro[m]["gt"]
        bo, bob = macro[m]["bo"]
        y_, yb_ = ych.pop(c)
        s1, s1b = st1.nxt()
        s2, s2b = st2.nxt()
        s3, s3b = st3.nxt()
        a1, a1b = g1.nxt()
        a2, a2b = g2.nxt()

        def bc(t_):
            return t_[:].unsqueeze(2).to_broadcast([128, 8, 64])
        op("dve", lambda e: e.tensor_reduce(out=s1[:], in_=y_[:], axis=AX.X, op=ALU.add), r=[yb_], w=[s1b])
        op("act", lambda e: e.activation(out=a1[:], in_=y_[:], func=AF.Square), r=[yb_], w=[a1b])
        op("dve", lambda e: e.tensor_reduce(out=s2[:], in_=a1[:], axis=AX.X, op=ALU.add), r=[a1b], w=[s2b])
        op("dve", lambda e: e.tensor_scalar(out=s1[:], in0=s1[:], scalar1=1.0 / 64, scalar2=None, op0=ALU.mult),
           r=[s1b], w=[s1b])
        op("dve", lambda e: e.tensor_tensor(out=s3[:], in0=s1[:], in1=s1[:], op=ALU.mult), r=[s1b], w=[s3b])
        op("dve", lambda e: e.scalar_tensor_tensor(out=s2[:], in0=s2[:], scalar=1.0 / 64, in1=s3[:], op0=ALU.mult,
                                                   op1=ALU.subtract), r=[s2b, s3b], w=[s2b])
        op("dve", lambda e: e.tensor_scalar(out=s2[:], in0=s2[:], scalar1=64e-5, scalar2=None, op0=ALU.add),
           r=[s2b], w=[s2b])
        op("act", lambda e: e.activation(out=s2[:], in_=s2[:], func=AF.Sqrt), r=[s2b], w=[s2b])
        op("dve", lambda e: e.reciprocal(out=s2[:], in_=s2[:]), r=[s2b], w=[s2b])
        op("pool", lambda e: e.tensor_tensor(out=a1[:], in0=y_[:], in1=bc(s1), op=ALU.subtract),
           r=[yb_, s1b], w=[a1b])
        op("pool", lambda e: e.tensor_tensor(out=a1[:], in0=a1[:], in1=bc(s2), op=ALU.mult), r=[a1b, s2b], w=[a1b])
        op("pool", lambda e: e.tensor_tensor(out=a1[:], in0=a1[:], in1=lnw[:].rearrange("p (h c) -> p h c", c=64),
                                             op=ALU.mult), r=[a1b, lnwb], w=[a1b])
        op("pool", lambda e: e.tensor_tensor(out=a1[:], in0=a1[:], in1=lnb[:].rearrange("p (h c) -> p h c", c=64),
                                             op=ALU.add), r=[a1b, lnbb], w=[a1b])
        op("dve", lambda e: e.tensor_tensor(out=a2[:], in0=tm[:, 3, cl, :].rearrange("p (h c) -> p h c", c=64),
                                            in1=bc(bo[:, cl, :]) if False else bo[:, cl, :].unsqueeze(2).to_broadcast([128, 8, 64]),
                                            op=ALU.mult), r=[tmb, bob], w=[a2b])
        op("pool", lambda e: e.tensor_tensor(out=a1[:], in0=a1[:], in1=a2[:], op=ALU.add), r=[a1b, a2b], w=[a1b])
        o_, ob_ = yo.nxt()
        op("dve", lambda e: e.tensor_tensor(out=o_[:], in0=a1[:], in1=gt[:, cl, :].rearrange("p (h c) -> p h c", c=64),
                                            op=ALU.mult), r=[a1b, gtb], w=[ob_])
        for b in range(NB):
            ph = slice(b * 64, (b + 1) * 64)
            dma("pool", SC["YMIX"][b, c * 64:(c + 1) * 64, 512:1024], o_[ph].rearrange("p h c -> p (h c)"), r=[ob_])
        yield

    def run_all(gens):
        gens = list(gens)
        while gens:
            nxt_ = []
            for g_ in gens:
                try:
                    next(g_)
                    nxt_.append(g_)
                except StopIteration:
                    pass
            gens = nxt_

    load_macro(0)
    for c2 in range(0, NCH + 4, 2):
        m_next = c2 // MC + 1
        if c2 % MC == 2 and m_next < NMAC:
            load_macro(m_next)
        tasks = []
        for cc in (c2, c2 + 1):
            if cc < NCH:
                tasks.append(stage_p(cc))
        qs = [stage_q(cc) for cc in (c2 - 2, c2 - 1) if 0 <= cc < NCH]
        if qs:
            def seq(gs):
                for g_ in gs:
                    yield from g_
            tasks.append(seq(qs))
        for cc in (c2 - 4, c2 - 3):
            if 0 <= cc < NCH:
                tasks.append(stage_g(cc))
        run_all(tasks)


def build(S, shapes, consts, dbg=(), dbg_in=(), phases="A,N,R,D1,D2"):
    nc = bass.Bass("TRN2", target_bir_lowering=False)
    I = {}
    for nm in INPUT_NAMES:
        shp = list(shapes[nm])
        I[nm] = dram(nc, nm, shp, F32, kind="ExternalInput")
    for nm, arr in consts.items():
        I[nm] = dram(nc, nm, arr.shape, BF16 if arr.dtype == NPBF else F32, kind="ExternalInput")
    SC = {}
    for nm, (shp, dt) in scratch_defs(S).items():
        kind = "Internal"
        if nm in dbg:
            kind = "ExternalOutput"
        if nm in dbg_in:
            kind = "ExternalInput"
        SC[nm] = dram(nc, nm, shp, dt, kind=kind)
    out = dram(nc, "out", [NB, S, D], F32, kind="ExternalOutput")
    phases = phases.split(",")
    with ExitStack() as es:
        kb = KB(nc, es)
        kb.recording = SCHED
        if "A" in phases:
            phase_a(nc, kb, S, I, SC)
            kb.barrier()
        if "N" in phases:
            run_phase(kb, phase_n, S, I, SC)
        if "R" in phases:
            run_phase(kb, phase_r, S, I, SC)
        if "D1" in phases:
            run_phase(kb, phase_d1, S, I, SC)
        if "D2" in phases:
            run_phase(kb, phase_d2, S, I, SC, out)
        kb.barrier()
        print("instructions:", kb.ninstr)
    return nc


SEQ = 4096
NCORES = 8


def kernel(**inputs):
    S = SEQ
    consts = host_consts(S)
    shapes = {k: tuple(np.asarray(v).shape) for k, v in inputs.items()}
    shapes["x"] = (NB, S, D)
    shapes["p"] = (1, NB, S, 256)
    nc = build(S, shapes, consts)
    base = {k: np.ascontiguousarray(np.asarray(v, dtype=np.float32)) for k, v in inputs.items()
            if k not in ("x", "p")}
    base.update(consts)
    x = np.asarray(inputs["x"], dtype=np.float32)
    p = np.asarray(inputs["p"], dtype=np.float32)
    in_maps = []
    for c in range(NCORES):
        m = dict(base)
        m["x"] = np.ascontiguousarray(x[NB * c:NB * (c + 1)])
        m["p"] = np.ascontiguousarray(p[:, NB * c:NB * (c + 1)])
        in_maps.append(m)
    res = run_bass_kernel_spmd(nc, in_maps, core_ids=list(range(NCORES)))
    return np.concatenate([np.asarray(r["out"], dtype=np.float32) for r in res.results], axis=0)
```

```python
import numpy as np
import ml_dtypes
from contextlib import ExitStack
import concourse.bass as bass
import concourse.mybir as mybir
from concourse.bass_utils import run_bass_kernel_spmd

F32 = mybir.dt.float32
BF16 = mybir.dt.bfloat16
AF = mybir.ActivationFunctionType
ALU = mybir.AluOpType
AX = mybir.AxisListType
NPBF = ml_dtypes.bfloat16

D = 1024
NB = 2
HD = 64
NCOLS = 3096
C_Q, C_KC, C_VC, C_KS, C_VS, C_KW, C_VW, C_G = 0, 512, 640, 768, 896, 1024, 1152, 1280
C_RW = 1304
DECAY_C = 0.6065306597126334
import os
STOP = int(os.environ.get('KSTOP', '99'))
SKIP = os.environ.get('KSKIP', '')
SCHED = os.environ.get('KSCHED', '1') == '1'


class Buf:
    __slots__ = ("name", "w", "rs", "excl", "wx", "_s")

    def __init__(self, name="", excl=False):
        self.name = name
        self.w = None
        self.rs = []
        self.excl = excl
        self.wx = []
        self._s = None


import types
import os


def _freeze(fn, depth=0):
    if not isinstance(fn, types.FunctionType) or fn.__closure__ is None or depth > 3:
        return fn
    cells = []
    for c in fn.__closure__:
        try:
            v = c.cell_contents
        except ValueError:
            cells.append(c)
            continue
        if isinstance(v, types.FunctionType):
            v = _freeze(v, depth + 1)
        cells.append(types.CellType(v))
    return types.FunctionType(fn.__code__, fn.__globals__, fn.__name__, fn.__defaults__, tuple(cells))


COST = {"pe": 0.2, "act": 0.58, "dve": 0.5, "pool": 1.2, "sp": 0.08}
SEM_LAT = 0.12
DMA_LAT = 3.0
SCHED_W = int(os.environ.get("KSCHEDW", "24"))


class Eng:
    def __init__(self, name, h, semi):
        self.name = name
        self.h = h
        self.semi = semi
        self.count = 0
        self.waited = {}
        self.dsems = []
        self.dn = 0


class KB:
    DK = 8

    def __init__(self, nc, es):
        self.nc = nc
        self.es = es
        self.sems = []
        self.E = {}
        for name, h in (("pe", nc.tensor), ("act", nc.scalar), ("dve", nc.vector),
                        ("pool", nc.gpsimd), ("sp", nc.sync)):
            self.E[name] = Eng(name, h, self.newsem("c_" + name))
        for name in ("sp", "pool", "act"):
            e = self.E[name]
            e.dsems = [self.newsem("d_%s%d" % (name, i)) for i in range(self.DK)]
        self.semmax = {}
        self.ninstr = 0
        self.recording = False
        self.sched_w = SCHED_W
        self.units = []
        self.cur_pe = None
        self.gen = 0

    def newsem(self, name):
        s = self.es.enter_context(self.nc.semaphore(name))
        self.sems.append(s)
        return len(self.sems) - 1

    def _wait(self, eng, semi, val):
        if eng.waited.get(semi, 0) >= val:
            return
        eng.h.wait_ge(self.sems[semi], val)
        eng.waited[semi] = val
        self.ninstr += 1

    def _deps(self, eng, r, w):
        for b in r:
            ev = b.w
            if ev is not None:
                if ev[2] is not None and ev[1] > ev[2].count:
                    raise RuntimeError("unsignaled producer for %s" % b.name)
                self._wait(eng, ev[0], ev[1])
            for ev in b.wx:
                self._wait(eng, ev[0], ev[1])
            if b.excl:
                for ev in b.rs:
                    if ev[2] is not eng:
                        self._wait(eng, ev[0], ev[1])
        pe = self.E["pe"]
        for b in w:
            for ev in b.wx:
                self._wait(eng, ev[0], ev[1])
            ev = b.w
            if ev is not None and not (ev[2] is eng and eng is pe):
                if ev[2] is not None and ev[1] > ev[2].count:
                    raise RuntimeError("unsignaled producer (waw) for %s" % b.name)
                self._wait(eng, ev[0], ev[1])
            for ev in b.rs:
                if not (ev[2] is eng and eng is pe):
                    if ev[2] is not None and ev[1] > ev[2].count:
                        raise RuntimeError("unsignaled reader for %s" % b.name)
                    self._wait(eng, ev[0], ev[1])

    def _rec(self, en, item, r, w, sig, cost, is_dma):
        if en == "pe" and self.cur_pe is not None:
            u = self.cur_pe
        else:
            u = dict(eng=en, items=[], deps=set(), cost=0.0, dma=is_dma, idx=len(self.units))
            self.units.append(u)
        ui = u["idx"]
        u["items"].append(item)
        u["cost"] += cost
        g = self.gen
        for b in r:
            st = getattr(b, "_s", None)
            if st is None or st[0] != g:
                st = [g, None, []]
                b._s = st
            if st[1] is not None and st[1] != ui:
                u["deps"].add(st[1])
            if b.excl:
                for x in st[2]:
                    if x != ui:
                        u["deps"].add(x)
        for b in w:
            st = getattr(b, "_s", None)
            if st is None or st[0] != g:
                st = [g, None, []]
                b._s = st
            if st[1] is not None and st[1] != ui:
                u["deps"].add(st[1])
            for x in st[2]:
                if x != ui:
                    u["deps"].add(x)
        for b in r:
            b._s[2].append(ui)
        for b in w:
            b._s[1] = ui
            b._s[2] = []
        if en == "pe":
            self.cur_pe = None if sig else u

    def flush(self):
        units = self.units
        self.units = []
        self.cur_pe = None
        self.gen += 1
        if not units:
            return
        n = len(units)
        pend = {en: [] for en in self.E}
        for u in units:
            pend[u["eng"]].append(u["idx"])
        ptr = {en: 0 for en in self.E}
        emitted = [False] * n
        finish = [0.0] * n
        tfree = {en: 0.0 for en in self.E}
        order = []
        live = [en for en in self.E if pend[en]]
        while len(order) < n:
            best = None
            for en in live:
                lst = pend[en]
                p = ptr[en]
                cnt = 0
                i = p
                tf = tfree[en]
                while i < len(lst) and cnt < self.sched_w:
                    ui = lst[i]
                    i += 1
                    if emitted[ui]:
                        continue
                    cnt += 1
                    u = units[ui]
                    ok = True
                    st = tf
                    for d in u["deps"]:
                        if not emitted[d]:
                            ok = False
                            break
                        f = finish[d] + (SEM_LAT if units[d]["eng"] != en or units[d]["dma"] else 0.03)
                        if f > st:
                            st = f
                    if not ok:
                        continue
                    if best is None or st < best[0] - 1e-9 or (abs(st - best[0]) <= 1e-9 and ui < best[1]):
                        best = (st, ui, en)
                    if st <= tf + 1e-9:
                        break
            st, ui, en = best
            u = units[ui]
            emitted[ui] = True
            if u["dma"]:
                tfree[en] = st + u["cost"]
                finish[ui] = st + DMA_LAT
            else:
                tfree[en] = st + u["cost"]
                finish[ui] = st + u["cost"]
            order.append(ui)
            lst = pend[en]
            while ptr[en] < len(lst) and emitted[lst[ptr[en]]]:
                ptr[en] += 1
            if ptr[en] >= len(lst):
                live.remove(en)
        self.sim_time = getattr(self, "sim_time", 0.0) + max(finish)
        rec = self.recording
        self.recording = False
        for ui in order:
            u = units[ui]
            for it in u["items"]:
                if it[0] == "op":
                    self.op(u["eng"], it[1], it[2], it[3], it[4])
                else:
                    self.dma(u["eng"], it[1], it[2], it[3], it[4], **it[5])
        self.recording = rec

    def op(self, en, fn, r=(), w=(), sig=True, c=None):
        if self.recording:
            self._rec(en, ("op", _freeze(fn), tuple(r), tuple(w), sig), r, w, sig,
                      COST[en] if c is None else c, False)
            return None
        eng = self.E[en]
        self._deps(eng, r, w)
        ins = fn(eng.h)
        self.ninstr += 1
        ticket = eng.count + 1
        if sig:
            ins.then_inc(self.sems[eng.semi], 1)
            eng.count = ticket
            self.semmax[eng.semi] = ticket
        ev = (eng.semi, ticket, eng)
        for b in r:
            b.rs.append(ev)
        for b in w:
            b.w = ev
            b.rs = []
            b.wx = []
        return ins

    def dma(self, qn, out, in_, r=(), w=(), **kw):
        if self.recording:
            self._rec(qn, ("dma", out, in_, tuple(r), tuple(w), kw), r, w, True, 0.08, True)
            return None
        eng = self.E[qn]
        self._deps(eng, r, w)
        i = eng.dn % self.DK
        tgt = 16 * (eng.dn // self.DK + 1)
        semi = eng.dsems[i]
        if tgt > 16:
            self._wait(eng, semi, tgt - 16)
        eng.h.dma_start(out=out, in_=in_, **kw).then_inc(self.sems[semi], 16)
        self.ninstr += 1
        eng.dn += 1
        self.semmax[semi] = tgt
        ev = (semi, tgt, None)
        for b in r:
            b.rs.append(ev)
        for b in w:
            if b.rs or b.w is None or b.w[2] is not None:
                b.wx = []
            else:
                b.wx.append(b.w)
            b.w = ev
            b.rs = []

    def barrier(self, engs=("pe", "act", "dve", "pool", "sp")):
        if self.recording:
            self.flush()
        for en in engs:
            eng = self.E[en]
            for semi, val in self.semmax.items():
                self._wait(eng, semi, val)


class Tl:
    uid = 0

    def __init__(self, kb, name, shape, dt, nbuf=1, psum=False):
        nc = kb.nc
        self.t = []
        self.b = []
        for i in range(nbuf):
            Tl.uid += 1
            nm = "%s_%d_%d" % (name, i, Tl.uid)
            if psum:
                t = kb.es.enter_context(nc.psum_tensor(nm, shape, dt))
            else:
                t = kb.es.enter_context(nc.sbuf_tensor(nm, shape, dt))
            self.t.append(t)
            self.b.append(Buf(nm, excl=psum))
        self.n = nbuf
        self.i = -1

    def nxt(self):
        self.i = (self.i + 1) % self.n
        return self.t[self.i], self.b[self.i]

    def cur(self):
        return self.t[self.i], self.b[self.i]


def _bf(a):
    return np.ascontiguousarray(a.astype(NPBF))


def host_consts(S):
    c = {}
    c["ident_bf"] = _bf(np.eye(128, dtype=np.float32))
    c["ident_f"] = np.eye(128, dtype=np.float32)
    bo = np.zeros((128, 128), np.float32)
    bo[:64, :64] = 1.0
    bo[64:, 64:] = 1.0
    c["blockones"] = bo
    bs = np.zeros((128, 2), np.float32)
    bs[:64, 0] = 1.0
    bs[64:, 1] = 1.0
    c["blocksel"] = _bf(bs)
    rm = np.ones((128, 512), np.float32)
    rm[:, ::64] = 0.0
    c["resetmask"] = rm
    c.update(nsa_consts(S))
    c.update(rwkv_consts())
    return c


def dram(nc, name, shape, dt, kind="Internal"):
    return nc.dram_tensor(name, list(shape), dt, kind=kind).ap()


def phase_a(nc, kb0, S, I, SC):
    with ExitStack() as es:
        kb = kb0
        kb.es_phase = es
        old_es = kb.es
        kb.es = es
        try:
            _phase_a(nc, kb, S, I, SC)
        finally:
            kb.es = old_es


def _phase_a(nc, kb, S, I, SC):
    NST = NB * S // 512
    op, dma = kb.op, kb.dma

    def tl(name, shape, dt, nbuf=1, psum=False):
        return Tl(kb, name, shape, dt, nbuf, psum)

    Wb = tl("Wb", [128, 8, NCOLS], BF16)
    Wbt, Wbb = Wb.nxt()
    gcol = tl("gcol", [128, 8], F32)
    gct, gcb = gcol.nxt()
    dma("sp", gct[:], I["g_mix_pre"].rearrange("o (k p) -> p (o k)", p=128), w=[gcb],
        allow_slow_non_contiguous=True)
    ident = tl("ident", [128, 128], BF16)
    idt, idb = ident.nxt()
    dma("sp", idt[:], I["ident_bf"][:, :], w=[idb])
    bones = tl("bones", [128, 128], F32)
    bot, bob = bones.nxt()
    dma("sp", bot[:], I["blockones"][:, :], w=[bob])
    bsel = tl("bsel", [128, 2], BF16)
    bst, bsb = bsel.nxt()
    dma("sp", bst[:], I["blocksel"][:, :], w=[bsb])
    rmask = tl("rmask", [128, 512], F32)
    rmt, rmb = rmask.nxt()
    dma("sp", rmt[:], I["resetmask"][:, :], w=[rmb])
    mu = tl("mu", [128, 14], F32)
    mut, mub = mu.nxt()
    dma("sp", mut[:], I["shift_mu"].rearrange("o (s p) -> p (o s)", p=128), w=[mub],
        allow_slow_non_contiguous=True)
    omu = tl("omu", [128, 14], F32)
    omut, omub = omu.nxt()
    op("dve", lambda e: e.tensor_scalar(out=omut[:], in0=mut[:], scalar1=-1.0, scalar2=1.0,
                                        op0=ALU.mult, op1=ALU.add), r=[mub], w=[omub])
    cols = {}
    for nm in ("w0", "a0", "k_k", "k_a", "r_k"):
        t = tl("c_" + nm, [128, 4], F32)
        tt, tb = t.nxt()
        src = I[nm]
        if nm == "r_k":
            src = src.rearrange("o h d -> o (h d)")
        dma("sp", tt[:], src.rearrange("o (s p) -> p (o s)", p=128), w=[tb],
            allow_slow_non_contiguous=True)
        cols[nm] = (tt, tb)
    omka = tl("omka", [128, 4], F32)
    omkat, omkab = omka.nxt()
    op("dve", lambda e: e.tensor_scalar(out=omkat[:], in0=cols["k_a"][0][:], scalar1=-1.0, scalar2=1.0,
                                        op0=ALU.mult, op1=ALU.add), r=[cols["k_a"][1]], w=[omkab])
    gbias = tl("gbias", [128, 24], F32)
    gbt, gbb = gbias.nxt()
    dma("sp", gbt[:], I["nsa_gate_bias"].partition_broadcast(128), w=[gbb])
    wlora = tl("wlora", [64, 512], F32)
    wlt, wlb = wlora.nxt()
    dma("sp", wlt[:], I["w_lora_up"][0], w=[wlb])
    alora = tl("alora", [64, 512], F32)
    alt, alb = alora.nxt()
    dma("sp", alt[:], I["a_lora_up"][0], w=[alb])
    glora_f = tl("glora_f", [128, 512], F32)
    glft, glfb = glora_f.nxt()
    dma("sp", glft[:], I["g_lora_up"][0], w=[glfb])
    glora = tl("glora", [128, 512], BF16)
    glt, glb = glora.nxt()
    op("dve", lambda e: e.tensor_copy(glt[:], glft[:]), r=[glfb], w=[glb])

    with ExitStack() as es2:
        old = kb.es
        kb.es = es2
        wst = tl("wst", [128, NCOLS], F32, nbuf=2)
        kb.es = old
        for k in range(8):
            st, sb = wst.nxt()
            dma("sp", st[:], I["w_in"][0, k * 128:(k + 1) * 128, :], w=[sb])
            if k % 2 == 0:
                op("dve", lambda e, st=st, k=k: e.tensor_scalar(out=Wbt[:, k, :], in0=st[:], scalar1=gct[:, k:k + 1],
                                                                 scalar2=None, op0=ALU.mult),
                   r=[sb, gcb], w=[Wbb])
            else:
                op("act", lambda e, st=st, k=k: e.activation(out=Wbt[:, k, :], in_=st[:], func=AF.Copy,
                                                              scale=gct[:, k:k + 1]), r=[sb, gcb], w=[Wbb])
        kb.barrier()
    if STOP == 0:
        return

    xt = tl("xt", [128, 1024], F32, nbuf=4)
    ss = tl("ss", [128, 4], F32, nbuf=2)
    rstd = tl("rstd", [128, 4], F32, nbuf=2)
    xn = tl("xn", [128, 1024], BF16, nbuf=4)
    hT = tl("hT", [128, 8, 512], BF16, nbuf=2)
    tp = tl("tp", [128, 1024], BF16, nbuf=2, psum=True)
    fm = tl("fm", [128, 512], F32, nbuf=3, psum=True)
    tm = tl("tm", [128, 512], F32, nbuf=1, psum=True)
    tr2 = tl("tr2", [128, 512], BF16, nbuf=1, psum=True)
    ssp = tl("ssp", [128, 512], F32, nbuf=1, psum=True)
    carry = tl("carry", [128, 14], F32)
    cat, cab = carry.nxt()
    Z = tl("Z", [128, 513], F32, nbuf=2)
    f32t = {nm: tl("f_" + nm, [128, 512], F32, nbuf=(2 if nm in ("r", "k0", "v") else 1)) for nm in
            ("r", "k0", "v", "sg", "Lp", "Ep", "Em", "En", "a", "kk", "t1", "kkn", "k", "ba", "bt", "kt", "tmp")}
    bft = {nm: tl("b_" + nm, [128, 512], BF16, nbuf=2) for nm in
           ("aT", "bT", "kT", "rT", "BT", "KT", "vT", "rk")}
    osb = tl("osb", [128, 512], BF16, nbuf=2)
    tokmaj = tl("tokmaj", [128, 4, 4, 512], BF16)
    tkt, tkb = tokmaj.nxt()
    tkbs = [[Buf("tk%d%d" % (j, hc)) for hc in range(4)] for j in range(4)]
    lor = tl("lor", [64, 2, 512], F32)
    lot, lob = lor.nxt()
    sgd = tl("sgd", [128, 512], BF16)
    sgt, sgb = sgd.nxt()
    vtok = tl("vtok", [128, 4, 2, 2, 128], BF16)
    vtt, vtb = vtok.nxt()
    op("dve", lambda e: e.memset(vtt[:], 1.0), w=[vtb])
    gtok = tl("gtok", [128, 4, 24], F32)
    gtt, gtb = gtok.nxt()
    gout = tl("gout", [128, 4, 512], BF16)
    got, gob = gout.nxt()
    bon = tl("bon", [128, 4, 8], F32)
    bont, bonb = bon.nxt()
    pcs = tl("pcs", [128, 4, 8], F32)
    pct, pcb = pcs.nxt()
    wraw = tl("wraw", [128, 4, 512], F32)
    wrt, wrb = wraw.nxt()
    araw = tl("araw", [128, 4, 512], F32)
    art, arb = araw.nxt()

    def T(nm):
        return f32t[nm].nxt()

    for st_i in range(NST):
        b = st_i // (S // 512)
        t0 = (st_i % (S // 512)) * 512
        first = (t0 == 0)
        hTt, hTb = hT.nxt()
        sst, ssb = ss.nxt()
        rst, rsb = rstd.nxt()
        xns = []
        xnl = []
        for j in range(4):
            xtt, xtb = xt.nxt()
            dma("sp", xtt[:], I["x"][b, t0 + j * 128:t0 + (j + 1) * 128, :], w=[xtb])
            xnt, xnb = xn.nxt()
            op("act", lambda e, xtt=xtt, j=j: e.activation(out=xnt[:], in_=xtt[:], func=AF.Square,
                                                            accum_out=sst[:, j:j + 1]),
               r=[xtb], w=[xnb, ssb])
            xns.append((xtt, xtb))
            xnl.append((xnt, xnb))
        op("dve", lambda e: e.tensor_scalar(out=rst[:], in0=sst[:], scalar1=1.0 / D, scalar2=1e-6,
                                            op0=ALU.mult, op1=ALU.add), r=[ssb], w=[rsb])
        op("act", lambda e: e.activation(out=rst[:], in_=rst[:], func=AF.Sqrt), r=[rsb], w=[rsb])
        op("dve", lambda e: e.reciprocal(out=rst[:], in_=rst[:]), r=[rsb], w=[rsb])
        for j in range(4):
            xtt, xtb = xns[j]
            xnt, xnb = xnl[j]
            if j % 2 == 0:
                op("dve", lambda e, xtt=xtt, xnt=xnt, j=j: e.tensor_scalar(out=xnt[:], in0=xtt[:],
                                                                            scalar1=rst[:, j:j + 1], scalar2=None,
                                                                            op0=ALU.mult),
                   r=[xtb, rsb], w=[xnb])
            else:
                op("act", lambda e, xtt=xtt, xnt=xnt, j=j: e.activation(out=xnt[:], in_=xtt[:], func=AF.Copy,
                                                                         scale=rst[:, j:j + 1]),
                   r=[xtb, rsb], w=[xnb])
        for m in range(4):
            tpt, tpb = tp.nxt()
            for kk_ in range(2):
                k = 2 * m + kk_
                for j in range(4):
                    xnt, xnb = xnl[j]
                    last = (kk_ == 1 and j == 3)
                    op("pe", lambda e, xnt=xnt, k=k, j=j, kk_=kk_, tpt=tpt: e.transpose(
                        tpt[:, kk_ * 512 + j * 128: kk_ * 512 + (j + 1) * 128],
                        xnt[:, k * 128:(k + 1) * 128], idt[:]),
                       r=[xnb, idb], w=[tpb], sig=last)
            en = "act" if m % 2 == 0 else "dve"
            if en == "act":
                op("act", lambda e, tpt=tpt, m=m: e.copy(out=hTt[:, 2 * m:2 * m + 2, :],
                                                         in_=tpt[:].rearrange("p (a b) -> p a b", b=512)),
                   r=[tpb], w=[hTb])
            else:
                op("dve", lambda e, tpt=tpt, m=m: e.tensor_copy(hTt[:, 2 * m:2 * m + 2, :],
                                                                tpt[:].rearrange("p (a b) -> p a b", b=512)),
                   r=[tpb], w=[hTb])

        if STOP == 1:
            return

        def fm_mm(c0, width):
            pt, pb = fm.nxt()
            for k in range(8):
                op("pe", lambda e, k=k, pt=pt: e.matmul(pt[0:width, :], Wbt[:, k, c0:c0 + width], hTt[:, k, :],
                                                         start=(k == 0), stop=(k == 7)),
                   r=[Wbb, hTb], w=[pb], sig=(k == 7))
            return pt, pb

        for ci, (c0, dst, row0, scale) in enumerate(
                [(C_Q + 128 * i, "QT", 128 * i, 0.125) for i in range(4)] +
                [(C_KC, "KCT", 0, 1.0), (C_VC, "VCT", 0, 1.0), (C_KS, "KST", 0, 1.0), (C_KW, "KWT", 0, 1.0)]):
            pt, pb = fm_mm(c0, 128)
            ot, ob = osb.nxt()
            if ci % 2 == 0:
                op("act", lambda e, pt=pt, ot=ot, scale=scale: e.activation(out=ot[:], in_=pt[:], func=AF.Copy,
                                                                            scale=scale), r=[pb], w=[ob])
            else:
                op("dve", lambda e, pt=pt, ot=ot, scale=scale: e.tensor_scalar(out=ot[:], in0=pt[:], scalar1=scale,
                                                                               scalar2=None, op0=ALU.mult),
                   r=[pb], w=[ob])
            dma("pool", SC[dst][b, row0:row0 + 128, t0:t0 + 512], ot[:], r=[ob])

        if STOP == 2:
            return
        for j in range(4):
            tmt, tmb = tm.nxt()
            for gi, (c0, wd_, o0) in enumerate([(C_VS, 128, 0), (C_VW, 128 if 'n128' in SKIP else 152, 128)]):
                for k in range(8):
                    op("pe", lambda e, k=k, j=j, c0=c0, wd_=wd_, o0=o0: e.matmul(
                        tmt[:, o0:o0 + wd_], hTt[:, k, j * 128:(j + 1) * 128], Wbt[:, k, c0:c0 + wd_],
                        start=(k == 0), stop=(k == 7)),
                       r=[Wbb, hTb], w=[tmb], sig=(k == 7 and gi == 1))
            if 'cp' not in SKIP:
                op("act", lambda e, j=j: e.copy(out=vtt[:, j, :, :, 0:64],
                                                in_=tmt[:, 0:256].rearrange("p (a g d) -> p a g d", a=2, g=2)),
                   r=[tmb], w=[vtb])
            if 'ad' not in SKIP:
                op("act", lambda e, j=j: e.copy(out=gtt[:, j, :], in_=tmt[:, 256:280]), r=[tmb], w=[gtb])
                op("dve", lambda e, j=j: e.tensor_tensor(out=gtt[:, j, :], in0=gtt[:, j, :], in1=gbt[:],
                                                     op=ALU.add), r=[gtb, gbb], w=[gtb])
        if 'sig' not in SKIP:
            op("act", lambda e: e.activation(out=gtt[:], in_=gtt[:], func=AF.Sigmoid), r=[gtb], w=[gtb])
        if 'dv' not in SKIP:
            for sw, nm_ in enumerate(("VSA", "VWA")):
                for g in range(2):
                    dma("pool", SC[nm_][b, g, :, t0 // 128:t0 // 128 + 4, :], vtt[:, :, sw, g, :], r=[vtb])
        if 'dg' not in SKIP:
            dma("pool", SC["GATE"][b, t0:t0 + 512, :].rearrange("(j p) c -> p j c", p=128), gtt[:], r=[gtb])

        if STOP == 3:
            return
        def shift(pt, pb, slot, dst_t, dst_b, eng2="dve"):
            zt, zb = Z.nxt()
            op("act", lambda e: e.copy(out=zt[:, 1:513], in_=pt[:]), r=[pb], w=[zb])
            if first:
                op("dve", lambda e: e.memset(zt[:, 0:1], 0.0), w=[zb])
            else:
                op("dve", lambda e: e.tensor_copy(zt[:, 0:1], cat[:, slot:slot + 1]), r=[cab], w=[zb])
            op("dve", lambda e: e.tensor_copy(cat[:, slot:slot + 1], zt[:, 512:513]), r=[zb], w=[cab])
            op("act", lambda e: e.activation(out=dst_t, in_=zt[:, 1:513], func=AF.Copy,
                                             scale=omut[:, slot:slot + 1]), r=[zb, omub], w=[dst_b])
            op("dve", lambda e: e.scalar_tensor_tensor(out=dst_t, in0=zt[:, 0:512], scalar=mut[:, slot:slot + 1],
                                                      in1=dst_t, op0=ALU.mult, op1=ALU.add),
               r=[zb, mub, dst_b], w=[dst_b])

        pt, pb = fm_mm(C_RW + 1536, 128)
        tmpt, tmpb = T("tmp")
        shift(pt, pb, 12, tmpt[:], tmpb)
        op("act", lambda e: e.activation(out=lot[:, 0, :], in_=tmpt[0:64, :], func=AF.Tanh), r=[tmpb], w=[lob])
        op("dve", lambda e: e.tensor_copy(lot[:, 1, :], tmpt[64:128, :]), r=[tmpb], w=[lob])
        for hc in range(4):
            pt, pb = fm.nxt()
            op("pe", lambda e, pt=pt, hc=hc: e.matmul(pt[:], wlt[:, hc * 128:(hc + 1) * 128], lot[:, 0, :],
                                                       start=True, stop=True), r=[wlb, lob], w=[pb])
            op("act", lambda e, pt=pt, hc=hc: e.activation(out=wrt[:, hc, :], in_=pt[:], func=AF.Sigmoid,
                                                            bias=cols["w0"][0][:, hc:hc + 1]),
               r=[pb, cols["w0"][1]], w=[wrb])
            pt, pb = fm.nxt()
            op("pe", lambda e, pt=pt, hc=hc: e.matmul(pt[:], alt[:, hc * 128:(hc + 1) * 128], lot[:, 1, :],
                                                       start=True, stop=True), r=[alb, lob], w=[pb])
            op("act", lambda e, pt=pt, hc=hc: e.activation(out=art[:, hc, :], in_=pt[:], func=AF.Sigmoid,
                                                            bias=cols["a0"][0][:, hc:hc + 1]),
               r=[pb, cols["a0"][1]], w=[arb])
        pt, pb = fm_mm(C_RW + 1664, 128)
        shift(pt, pb, 13, tmpt[:], tmpb)
        op("act", lambda e: e.activation(out=sgt[:], in_=tmpt[:], func=AF.Sigmoid), r=[tmpb], w=[sgb])
        for j in range(4):
            tmt, tmb = tm.nxt()
            op("pe", lambda e, j=j: e.matmul(tmt[:], sgt[:, j * 128:(j + 1) * 128], glt[:], start=True, stop=True),
               r=[sgb, glb], w=[tmb])
            op("act", lambda e, j=j: e.copy(out=got[:, j, :], in_=tmt[:]), r=[tmb], w=[gob])
        dma("pool", SC["GT"][b, t0:t0 + 512, :].rearrange("(j p) c -> p j c", p=128), got[:], r=[gob])

        if STOP == 4:
            return
        for hc in range(4):
            rt, rb = T("r")
            k0t, k0b = T("k0")
            vt, vb = T("v")
            pt, pb = fm_mm(C_RW + hc * 128, 128)
            shift(pt, pb, hc, rt[:], rb, "pool")
            pt, pb = fm_mm(C_RW + 512 + hc * 128, 128)
            shift(pt, pb, 4 + hc, k0t[:], k0b, "dve")
            pt, pb = fm_mm(C_RW + 1024 + hc * 128, 128)
            shift(pt, pb, 8 + hc, vt[:], vb, "pool")
            Lpt, Lpb = T("Lp")
            op("dve", lambda e: e.tensor_tensor_scan(out=Lpt[:], data0=rmt[:], data1=wrt[:, hc, :], initial=0.0,
                                                     op0=ALU.mult, op1=ALU.add), r=[rmb, wrb], w=[Lpb])
            Ept, Epb = T("Ep")
            Ent, Enb = T("En")
            Emt, Emb = T("Em")
            op("act", lambda e: e.activation(out=Ept[:], in_=Lpt[:], func=AF.Exp, scale=-DECAY_C), r=[Lpb], w=[Epb])
            op("act", lambda e: e.activation(out=Ent[:], in_=Lpt[:], func=AF.Exp, scale=DECAY_C), r=[Lpb], w=[Enb])
            op("pool", lambda e: e.tensor_tensor(out=Emt[:], in0=Lpt[:], in1=wrt[:, hc, :], op=ALU.subtract),
               r=[Lpb, wrb], w=[Emb])
            op("act", lambda e: e.activation(out=Emt[:], in_=Emt[:], func=AF.Exp, scale=-DECAY_C), r=[Emb], w=[Emb])
            op("dve", lambda e: e.tensor_copy(pct[:, hc, :], Ept[:].rearrange("p (c t) -> p c t", t=64)[:, :, 63]),
               r=[Epb], w=[pcb])
            kkt, kkb = T("kk")
            t1t, t1b = T("t1")
            op("dve", lambda e: e.tensor_scalar(out=kkt[:], in0=k0t[:], scalar1=cols["k_k"][0][:, hc:hc + 1],
                                                scalar2=None, op0=ALU.mult), r=[k0b, cols["k_k"][1]], w=[kkb])
            op("act", lambda e: e.activation(out=t1t[:], in_=kkt[:], func=AF.Square), r=[kkb], w=[t1b])
            spt, spb = ssp.nxt()
            op("pe", lambda e: e.matmul(spt[:], bot[:], t1t[:], start=True, stop=True), r=[bob, t1b], w=[spb])
            op("dve", lambda e: e.tensor_scalar(out=t1t[:], in0=spt[:], scalar1=1e-24, scalar2=None, op0=ALU.max),
               r=[spb], w=[t1b])
            op("act", lambda e: e.activation(out=t1t[:], in_=t1t[:], func=AF.Ln), r=[t1b], w=[t1b])
            op("act", lambda e: e.activation(out=t1t[:], in_=t1t[:], func=AF.Exp, scale=-0.5), r=[t1b], w=[t1b])
            kknt, kknb = T("kkn")
            op("dve", lambda e: e.tensor_tensor(out=kknt[:], in0=kkt[:], in1=t1t[:], op=ALU.mult),
               r=[kkb, t1b], w=[kknb])
            kt_, kb_ = T("k")
            op("dve", lambda e: e.tensor_scalar(out=kt_[:], in0=art[:, hc, :], scalar1=cols["k_a"][0][:, hc:hc + 1],
                                                scalar2=omkat[:, hc:hc + 1], op0=ALU.mult, op1=ALU.add),
               r=[arb, cols["k_a"][1], omkab], w=[kb_])
            op("pool", lambda e: e.tensor_tensor(out=kt_[:], in0=kt_[:], in1=k0t[:], op=ALU.mult),
               r=[kb_, k0b], w=[kb_])
            aTt, aTb = bft["aT"].nxt()
            op("dve", lambda e: e.scalar_tensor_tensor(out=aTt[:], in0=kknt[:], scalar=-1.0, in1=Emt[:],
                                                       op0=ALU.mult, op1=ALU.mult), r=[kknb, Emb], w=[aTb])
            bat, bab = T("ba")
            op("pool", lambda e: e.tensor_tensor(out=bat[:], in0=kknt[:], in1=art[:, hc, :], op=ALU.mult),
               r=[kknb, arb], w=[bab])
            btt, btb = T("bt")
            op("dve", lambda e: e.tensor_tensor(out=btt[:], in0=bat[:], in1=Ent[:], op=ALU.mult),
               r=[bab, Enb], w=[btb])
            bTt, bTb = bft["bT"].nxt()
            op("act", lambda e: e.copy(out=bTt[:], in_=btt[:]), r=[btb], w=[bTb])
            pcbc = pct[:, hc, :].unsqueeze(2).to_broadcast([128, 8, 64])
            BTt, BTb = bft["BT"].nxt()
            op("pool", lambda e: e.tensor_tensor(out=BTt[:].rearrange("p (c t) -> p c t", t=64),
                                                in0=btt[:].rearrange("p (c t) -> p c t", t=64), in1=pcbc,
                                                op=ALU.mult), r=[btb, pcb], w=[BTb])
            ktt, ktb = T("kt")
            op("pool", lambda e: e.tensor_tensor(out=ktt[:], in0=kt_[:], in1=Ent[:], op=ALU.mult),
               r=[kb_, Enb], w=[ktb])
            kTt, kTb = bft["kT"].nxt()
            op("act", lambda e: e.copy(out=kTt[:], in_=ktt[:]), r=[ktb], w=[kTb])
            KTt, KTb = bft["KT"].nxt()
            op("pool", lambda e: e.tensor_tensor(out=KTt[:].rearrange("p (c t) -> p c t", t=64),
                                                in0=ktt[:].rearrange("p (c t) -> p c t", t=64), in1=pcbc,
                                                op=ALU.mult), r=[ktb, pcb], w=[KTb])
            rTt, rTb = bft["rT"].nxt()
            op("pool", lambda e: e.tensor_tensor(out=rTt[:], in0=rt[:], in1=Ept[:], op=ALU.mult),
               r=[rb, Epb], w=[rTb])
            vTt, vTb = bft["vT"].nxt()
            op("act", lambda e: e.copy(out=vTt[:], in_=vt[:]), r=[vb], w=[vTb])
            rkt, rkb = bft["rk"].nxt()
            op("dve", lambda e: e.scalar_tensor_tensor(out=rkt[:], in0=rt[:], scalar=cols["r_k"][0][:, hc:hc + 1],
                                                       in1=kt_[:], op0=ALU.mult, op1=ALU.mult),
               r=[rb, cols["r_k"][1], kb_], w=[rkb])
            for kind, (tt_, tb_) in enumerate([(aTt, aTb), (bTt, bTb), (kTt, kTb), (rTt, rTb)]):
                dma("pool", SC["RWF"][b, kind, hc * 128:(hc + 1) * 128, t0:t0 + 512], tt_[:], r=[tb_])
            for j in range(4):
                tmt, tmb = tm.nxt()
                op("pe", lambda e, j=j: e.matmul(tmt[:, 0:2], rkt[:, j * 128:(j + 1) * 128], bst[:],
                                                 start=True, stop=True), r=[rkb, bsb], w=[tmb])
                op("dve", lambda e, j=j: e.tensor_copy(bont[:, j, 2 * hc:2 * hc + 2], tmt[:, 0:2]),
                   r=[tmb], w=[bonb])
                t2t, t2b = tr2.nxt()
                for kind, (tt_, tb_) in enumerate([(aTt, aTb), (BTt, BTb), (KTt, KTb), (vTt, vTb)]):
                    op("pe", lambda e, tt_=tt_, kind=kind, j=j: e.transpose(
                        t2t[:, kind * 128:(kind + 1) * 128], tt_[:, j * 128:(j + 1) * 128], idt[:]),
                       r=[tb_, idb], w=[t2b], sig=(kind == 3))
                en = "act" if j % 2 == 0 else "dve"
                if en == "act":
                    op("act", lambda e, j=j: e.copy(out=tkt[:, j, :, hc * 128:(hc + 1) * 128],
                                                    in_=t2t[:].rearrange("p (a c) -> p a c", c=128)),
                       r=[t2b], w=[tkbs[j][hc]])
                else:
                    op("dve", lambda e, j=j: e.tensor_copy(tkt[:, j, :, hc * 128:(hc + 1) * 128],
                                                           t2t[:].rearrange("p (a c) -> p a c", c=128)),
                       r=[t2b], w=[tkbs[j][hc]])
        for kind in range(4):
            dma("pool", SC["RWT"][b, kind, t0:t0 + 512, :].rearrange("(j p) c -> p j c", p=128),
                tkt[:, :, kind, :], r=[tkbs[j][hc] for j in range(4) for hc in range(4)])
        dma("pool", SC["BON"][b, t0:t0 + 512, :].rearrange("(j p) c -> p j c", p=128), bont[:], r=[bonb])
        for hc in range(4):
            dma("pool", SC["PC"][b, hc * 128:(hc + 1) * 128, t0 // 64:t0 // 64 + 8], pct[:, hc, :], r=[pcb])


def scratch_defs(S):
    return {
        "QT": ([NB, 512, S], BF16), "KCT": ([NB, 128, S], BF16), "VCT": ([NB, 128, S], BF16),
        "KST": ([NB, 128, S], BF16), "KWT": ([NB, 128, S], BF16),
        "VSA": ([NB, 2, 128, S // 128, 128], BF16), "VWA": ([NB, 2, 128, S // 128, 128], BF16), "GATE": ([NB, S, 24], F32),
        "GT": ([NB, S, 512], BF16),
        "RWF": ([NB, 4, 512, S], BF16), "RWT": ([NB, 4, S, 512], BF16),
        "BON": ([NB, S, 8], F32), "PC": ([NB, 512, S // 64], F32),
        "YMIX": ([NB, S, 1024], BF16), "X1": ([NB, S, 1024], F32), "XN": ([NB, S, 1024], BF16),
    }


INPUT_NAMES = ["x", "p", "g_mix_pre", "g_mix_post", "g_mlp_pre", "g_mlp_post", "w_in", "nsa_gate_bias",
               "cmp_pe_k", "cmp_k_w1", "cmp_k_b1", "cmp_k_w2", "cmp_pe_v", "cmp_v_w1", "cmp_v_b1", "cmp_v_w2",
               "shift_mu", "w0", "w_lora_up", "a0", "a_lora_up", "g_lora_up", "k_k", "k_a", "r_k", "lnx_w",
               "lnx_b", "w_out", "w_up", "w_down", "w_ple", "w_ple_gate"]


def run_phase(kb, fn, *args):
    with ExitStack() as es:
        old = kb.es
        kb.es = es
        kb.sched_w = 64 if fn is phase_n else SCHED_W
        try:
            fn(kb, *args)
        finally:
            kb.es = old
    kb.barrier()
    kb.sched_w = SCHED_W


def load_weight_bf16(kb, tl, Wt, Wb_, src, nk, ncols, gcol=None, stage_cols=None):
    with ExitStack() as es2:
        old = kb.es
        kb.es = es2
        wst = tl("wst", [128, ncols], F32, nbuf=2)
        kb.es = old
        for k in range(nk):
            st, sb = wst.nxt()
            kb.dma("sp", st[:], src[k * 128:(k + 1) * 128, :], w=[sb])
            if gcol is not None:
                if k % 2 == 0:
                    kb.op("dve", lambda e: e.tensor_scalar(out=Wt[:, k, :], in0=st[:], scalar1=gcol[0][:, k:k + 1],
                                                           scalar2=None, op0=ALU.mult), r=[sb, gcol[1]], w=[Wb_])
                else:
                    kb.op("act", lambda e: e.activation(out=Wt[:, k, :], in_=st[:], func=AF.Copy,
                                                        scale=gcol[0][:, k:k + 1]), r=[sb, gcol[1]], w=[Wb_])
            else:
                if k % 2 == 0:
                    kb.op("dve", lambda e: e.tensor_copy(Wt[:, k, :], st[:]), r=[sb], w=[Wb_])
                else:
                    kb.op("act", lambda e: e.copy(out=Wt[:, k, :], in_=st[:]), r=[sb], w=[Wb_])
        kb.barrier()


def rms_finish(kb, sst, ssb, rst, rsb, n):
    if n == 2:
        kb.op("dve", lambda e: e.tensor_tensor(out=rst[:, 0:1], in0=sst[:, 0:1], in1=sst[:, 1:2], op=ALU.add),
              r=[ssb], w=[rsb])
        kb.op("dve", lambda e: e.tensor_scalar(out=rst[:, 0:1], in0=rst[:, 0:1], scalar1=1.0 / D, scalar2=1e-6,
                                               op0=ALU.mult, op1=ALU.add), r=[rsb], w=[rsb])
    else:
        kb.op("dve", lambda e: e.tensor_scalar(out=rst[:, 0:1], in0=sst[:, 0:1], scalar1=1.0 / D, scalar2=1e-6,
                                               op0=ALU.mult, op1=ALU.add), r=[ssb], w=[rsb])
    kb.op("act", lambda e: e.activation(out=rst[:, 0:1], in_=rst[:, 0:1], func=AF.Sqrt), r=[rsb], w=[rsb])
    kb.op("dve", lambda e: e.reciprocal(out=rst[:, 0:1], in_=rst[:, 0:1]), r=[rsb], w=[rsb])


def phase_d1(kb, S, I, SC):
    op, dma = kb.op, kb.dma

    def tl(name, shape, dt, nbuf=1, psum=False):
        return Tl(kb, name, shape, dt, nbuf, psum)

    Wo = tl("Wo", [128, 8, 1024], BF16)
    Wot, Wob = Wo.nxt()
    load_weight_bf16(kb, tl, Wot, Wob, I["w_out"][0], 8, 1024)
    ident = tl("ident", [128, 128], BF16)
    idt, idb = ident.nxt()
    dma("sp", idt[:], I["ident_bf"][:, :], w=[idb])
    gbc = tl("gbc", [128, 1024], F32)
    gbt, gbb = gbc.nxt()
    dma("sp", gbt[:], I["g_mix_post"].partition_broadcast(128), w=[gbb])
    ym = tl("ym", [128, 4, 1024], BF16, nbuf=2)
    yT = tl("yT", [128, 8, 512], BF16, nbuf=2)
    tp = tl("tp", [128, 1024], BF16, nbuf=2, psum=True)
    mm = tl("mm", [128, 512], F32, nbuf=4, psum=True)
    xt = tl("xt", [128, 1024], F32, nbuf=2)
    junk = tl("junk", [128, 512], BF16)
    jt, jb = junk.nxt()
    ss = tl("ss", [128, 2], F32, nbuf=2)
    rstd = tl("rstd", [128, 1], F32, nbuf=2)
    tmp = tl("tmp", [128, 1024], F32, nbuf=2)
    x1 = tl("x1", [128, 1024], F32, nbuf=2)
    xnd = tl("xnd", [128, 1024], BF16, nbuf=2)
    for st_i in range(NB * S // 512):
        b = st_i // (S // 512)
        t0 = (st_i % (S // 512)) * 512
        ymt, ymb = ym.nxt()
        dma("sp", ymt[:], SC["YMIX"][b, t0:t0 + 512, :].rearrange("(j p) c -> p j c", p=128), w=[ymb])
        yTt, yTb = yT.nxt()
        for m in range(4):
            tpt, tpb = tp.nxt()
            for kk_ in range(2):
                k = 2 * m + kk_
                for j in range(4):
                    op("pe", lambda e: e.transpose(tpt[:, kk_ * 512 + j * 128: kk_ * 512 + (j + 1) * 128],
                                                   ymt[:, j, k * 128:(k + 1) * 128], idt[:]),
                       r=[ymb, idb], w=[tpb], sig=(kk_ == 1 and j == 3))
            if m % 2 == 0:
                op("act", lambda e: e.copy(out=yTt[:, 2 * m:2 * m + 2, :],
                                           in_=tpt[:].rearrange("p (a b) -> p a b", b=512)), r=[tpb], w=[yTb])
            else:
                op("dve", lambda e: e.tensor_copy(yTt[:, 2 * m:2 * m + 2, :],
                                                  tpt[:].rearrange("p (a b) -> p a b", b=512)), r=[tpb], w=[yTb])
        for j in range(4):
            xtt, xtb = xt.nxt()
            dma("sp", xtt[:], I["x"][b, t0 + j * 128:t0 + (j + 1) * 128, :], w=[xtb])
            sst, ssb = ss.nxt()
            rst, rsb = rstd.nxt()
            halves = []
            for hf in range(2):
                mt, mb = mm.nxt()
                for k in range(8):
                    op("pe", lambda e: e.matmul(mt[:], yTt[:, k, j * 128:(j + 1) * 128],
                                                Wot[:, k, hf * 512:(hf + 1) * 512], start=(k == 0), stop=(k == 7)),
                       r=[yTb, Wob], w=[mb], sig=(k == 7))
                op("act", lambda e: e.activation(out=jt[:], in_=mt[:], func=AF.Square,
                                                 accum_out=sst[:, hf:hf + 1]), r=[mb], w=[jb, ssb])
                halves.append((mt, mb))
            rms_finish(kb, sst, ssb, rst, rsb, 2)
            tmt, tmb = tmp.nxt()
            for hf in range(2):
                mt, mb = halves[hf]
                op("dve", lambda e: e.scalar_tensor_tensor(out=tmt[:, hf * 512:(hf + 1) * 512], in0=mt[:],
                                                           scalar=rst[:, 0:1], in1=gbt[:, hf * 512:(hf + 1) * 512],
                                                           op0=ALU.mult, op1=ALU.mult),
                   r=[mb, rsb, gbb], w=[tmb])
            x1t, x1b = x1.nxt()
            op("dve", lambda e: e.tensor_tensor(out=x1t[:], in0=tmt[:], in1=xtt[:], op=ALU.add),
               r=[tmb, xtb], w=[x1b])
            dma("pool", SC["X1"][b, t0 + j * 128:t0 + (j + 1) * 128, :], x1t[:], r=[x1b])
            s2t, s2b = ss.nxt()
            r2t, r2b = rstd.nxt()
            xnt, xnb = xnd.nxt()
            op("act", lambda e: e.activation(out=xnt[:], in_=x1t[:], func=AF.Square, accum_out=s2t[:, 0:1]),
               r=[x1b], w=[xnb, s2b])
            rms_finish(kb, s2t, s2b, r2t, r2b, 1)
            op("act", lambda e: e.activation(out=xnt[:], in_=x1t[:], func=AF.Copy, scale=r2t[:, 0:1]),
               r=[x1b, r2b], w=[xnb])
            dma("pool", SC["XN"][b, t0 + j * 128:t0 + (j + 1) * 128, :], xnt[:], r=[xnb])


def phase_d2(kb, S, I, SC, OUT):
    op, dma = kb.op, kb.dma

    def tl(name, shape, dt, nbuf=1, psum=False):
        return Tl(kb, name, shape, dt, nbuf, psum)

    gcol = tl("gcol", [128, 8], F32)
    gct, gcb = gcol.nxt()
    dma("sp", gct[:], I["g_mlp_pre"].rearrange("o (k p) -> p (o k)", p=128), w=[gcb],
        allow_slow_non_contiguous=True)
    Wu = tl("Wu", [128, 8, 4096], BF16)
    Wut, Wub = Wu.nxt()
    Wd = tl("Wd", [128, 32, 1024], BF16)
    Wdt, Wdb = Wd.nxt()
    Wg = tl("Wg", [128, 8, 1024], BF16)
    Wgt, Wgb = Wg.nxt()
    Wp = tl("Wp", [128, 2, 1024], BF16)
    Wpt, Wpb = Wp.nxt()
    load_weight_bf16(kb, tl, Wut, Wub, I["w_up"][0], 8, 4096, gcol=(gct, gcb))
    load_weight_bf16(kb, tl, Wdt, Wdb, I["w_down"][0], 32, 1024)
    load_weight_bf16(kb, tl, Wgt, Wgb, I["w_ple_gate"][0], 8, 1024)
    load_weight_bf16(kb, tl, Wpt, Wpb, I["w_ple"][0], 2, 1024)
    ident = tl("ident", [128, 128], BF16)
    idt, idb = ident.nxt()
    dma("sp", idt[:], I["ident_bf"][:, :], w=[idb])
    gbc = tl("gbc", [128, 1024], F32)
    gbt, gbb = gbc.nxt()
    dma("sp", gbt[:], I["g_mlp_post"].partition_broadcast(128), w=[gbb])

    x1 = tl("x1", [128, 2, 1024], F32, nbuf=1)
    xn = tl("xn", [128, 2, 1024], BF16, nbuf=2)
    hT = tl("hT", [128, 8, 256], BF16, nbuf=2)
    aT = tl("aT", [128, 32, 256], BF16)
    rl = tl("rl", [128, 512], BF16, nbuf=2)
    tmp = tl("tmp", [128, 1024], F32)
    pt_ = tl("pt", [128, 2, 256], F32)
    pb_ = tl("pb", [128, 2, 256], BF16)
    pT = tl("pT", [128, 2, 256], BF16)
    sg = tl("sg", [128, 512], F32, nbuf=1)
    ot = tl("ot", [128, 512], F32, nbuf=1)
    ss = tl("ss", [128, 2], F32, nbuf=2)
    rstd = tl("rstd", [128, 1], F32, nbuf=2)
    tp = tl("tp", [128, 1024], BF16, nbuf=2, psum=True)
    up = tl("up", [128, 512], F32, nbuf=2, psum=True)
    dn = tl("dn", [128, 512], F32, nbuf=2, psum=True)
    gp = tl("gp", [128, 512], F32, nbuf=1, psum=True)
    pp = tl("pp", [128, 512], F32, nbuf=1, psum=True)

    def transposes(srct, srcb, dstt, dstb):
        for m in range(2):
            tpt, tpb = tp.nxt()
            for kk_ in range(4):
                k = 4 * m + kk_
                for j in range(2):
                    op("pe", lambda e: e.transpose(tpt[:, kk_ * 256 + j * 128: kk_ * 256 + (j + 1) * 128],
                                                   srct[:, j, k * 128:(k + 1) * 128], idt[:]),
                       r=[srcb, idb], w=[tpb], sig=(kk_ == 3 and j == 1))
            if m % 2 == 0:
                op("act", lambda e: e.copy(out=dstt[:, 4 * m:4 * m + 4, :],
                                           in_=tpt[:].rearrange("p (a b) -> p a b", b=256)), r=[tpb], w=[dstb])
            else:
                op("dve", lambda e: e.tensor_copy(dstt[:, 4 * m:4 * m + 4, :],
                                                  tpt[:].rearrange("p (a b) -> p a b", b=256)), r=[tpb], w=[dstb])

    for ti in range(NB * S // 256):
        b = ti // (S // 256)
        t0 = (ti % (S // 256)) * 256
        x1t, x1b = x1.nxt()
        dma("sp", x1t[:], SC["X1"][b, t0:t0 + 256, :].rearrange("(j p) c -> p j c", p=128), w=[x1b])
        ptt, ptb = pt_.nxt()
        dma("sp", ptt[:], I["p"][0, b, t0:t0 + 256, :].rearrange("(j p) c -> p j c", p=128), w=[ptb])
        xnt, xnb = xn.nxt()
        dma("sp", xnt[:], SC["XN"][b, t0:t0 + 256, :].rearrange("(j p) c -> p j c", p=128), w=[xnb])
        hTt, hTb = hT.nxt()
        transposes(xnt, xnb, hTt, hTb)
        aTt, aTb = aT.nxt()
        for fp in range(16):
            ut, ub = up.nxt()
            for i in range(2):
                ffc = 2 * fp + i
                for k in range(8):
                    op("pe", lambda e: e.matmul(ut[:, i * 256:(i + 1) * 256], Wut[:, k, ffc * 128:(ffc + 1) * 128],
                                                hTt[:, k, :], start=(k == 0), stop=(k == 7)),
                       r=[Wub, hTb], w=[ub], sig=(k == 7 and i == 1), c=0.12)
            rlt, rlb = rl.nxt()
            op("act", lambda e: e.activation(out=rlt[:], in_=ut[:], func=AF.Relu), r=[ub], w=[rlb])
            op("dve" if fp % 4 != 3 else "pool", lambda e: e.tensor_tensor(
                out=aTt[:, 2 * fp:2 * fp + 2, :], in0=rlt[:].rearrange("p (a b) -> p a b", b=256),
                in1=rlt[:].rearrange("p (a b) -> p a b", b=256), op=ALU.mult), r=[rlb], w=[aTb])
        for j in range(2):
            sst, ssb = ss.nxt()
            rst, rsb = rstd.nxt()
            halves = []
            for hf in range(2):
                dt_, db_ = dn.nxt()
                for ffc in range(32):
                    op("pe", lambda e: e.matmul(dt_[:], aTt[:, ffc, j * 128:(j + 1) * 128],
                                                Wdt[:, ffc, hf * 512:(hf + 1) * 512], start=(ffc == 0),
                                                stop=(ffc == 31)), r=[aTb, Wdb], w=[db_], sig=(ffc == 31))
                jt, jb = rl.nxt()
                op("act", lambda e: e.activation(out=jt[:], in_=dt_[:], func=AF.Square,
                                                 accum_out=sst[:, hf:hf + 1]), r=[db_], w=[jb, ssb])
                halves.append((dt_, db_))
            rms_finish(kb, sst, ssb, rst, rsb, 2)
            tmt, tmb = tmp.nxt()
            for hf in range(2):
                dt_, db_ = halves[hf]
                op("dve", lambda e: e.scalar_tensor_tensor(out=tmt[:, hf * 512:(hf + 1) * 512], in0=dt_[:],
                                                           scalar=rst[:, 0:1], in1=gbt[:, hf * 512:(hf + 1) * 512],
                                                           op0=ALU.mult, op1=ALU.mult),
                   r=[db_, rsb, gbb], w=[tmb])
            op("dve", lambda e: e.tensor_tensor(out=x1t[:, j, :], in0=tmt[:], in1=x1t[:, j, :], op=ALU.add),
               r=[tmb, x1b], w=[x1b])
        xnt, xnb = xn.nxt()
        op("act", lambda e: e.copy(out=xnt[:], in_=x1t[:]), r=[x1b], w=[xnb])
        hTt, hTb = hT.nxt()
        transposes(xnt, xnb, hTt, hTb)
        pbt, pbb = pb_.nxt()
        op("dve", lambda e: e.tensor_copy(pbt[:], ptt[:]), r=[ptb], w=[pbb])
        tpt, tpb = tp.nxt()
        for kc in range(2):
            for j in range(2):
                op("pe", lambda e: e.transpose(tpt[:, kc * 256 + j * 128: kc * 256 + (j + 1) * 128],
                                               pbt[:, j, kc * 128:(kc + 1) * 128], idt[:]),
                   r=[pbb, idb], w=[tpb], sig=(kc == 1 and j == 1))
        pTt, pTb = pT.nxt()
        op("act", lambda e: e.copy(out=pTt[:], in_=tpt[:, 0:512].rearrange("p (a b) -> p a b", b=256)),
           r=[tpb], w=[pTb])
        for j in range(2):
            for hf in range(2):
                gt_, gb_ = gp.nxt()
                for k in range(8):
                    op("pe", lambda e: e.matmul(gt_[:], hTt[:, k, j * 128:(j + 1) * 128],
                                                Wgt[:, k, hf * 512:(hf + 1) * 512], start=(k == 0), stop=(k == 7)),
                       r=[hTb, Wgb], w=[gb_], sig=(k == 7))
                ppt, ppb = pp.nxt()
                for kc in range(2):
                    op("pe", lambda e: e.matmul(ppt[:], pTt[:, kc, j * 128:(j + 1) * 128],
                                                Wpt[:, kc, hf * 512:(hf + 1) * 512], start=(kc == 0), stop=(kc == 1)),
                       r=[pTb, Wpb], w=[ppb], sig=(kc == 1))
                sgt, sgb = sg.nxt()
                op("act", lambda e: e.activation(out=sgt[:], in_=gt_[:], func=AF.Sigmoid), r=[gb_], w=[sgb])
                ott, otb = ot.nxt()
                op("dve", lambda e: e.tensor_tensor(out=ott[:], in0=ppt[:], in1=sgt[:], op=ALU.mult),
                   r=[ppb, sgb], w=[otb])
                op("dve", lambda e: e.tensor_tensor(out=ott[:], in0=ott[:], in1=x1t[:, j, hf * 512:(hf + 1) * 512],
                                                    op=ALU.add), r=[otb, x1b], w=[otb])
                dma("pool", OUT[b, t0 + j * 128:t0 + (j + 1) * 128, hf * 512:(hf + 1) * 512], ott[:], r=[otb])


def nsa_dims(S):
    NCMP = (S - 32) // 16 + 1
    NNT = (NCMP + 127) // 128
    NSEL = S // 64
    return NCMP, NNT, NSEL, min(16, NSEL)


def nsa_consts(S):
    NCMP, NNT, NSEL, TOPK = nsa_dims(S)
    c = {}
    t = np.arange(S)
    qa = np.zeros((8, 4, S), np.float32)
    for h in range(8):
        sl = 2.0 ** (-(h + 1))
        qa[h, 0] = sl
        qa[h, 1] = sl
        qa[h, 2] = -sl * 128.0 * (t // 128)
        qa[h, 3] = -sl * (t % 128)
    c["QAUG"] = _bf(qa)
    ka = np.zeros((4, S), np.float32)
    ka[0] = 128.0 * (t // 128)
    ka[1] = t % 128
    ka[2] = 1.0
    ka[3] = 1.0
    c["KAUGP"] = _bf(ka)
    n = np.arange(NNT * 128)
    pos = 16 * n + 31
    kc = np.zeros((4, NNT * 128), np.float32)
    kc[0] = 128.0 * (pos // 128)
    kc[1] = pos % 128
    kc[2] = 1.0
    kc[3] = 1.0
    c["KAUGC"] = _bf(kc)
    et = np.zeros((128, S), np.float32)
    et[t // 64, t] = 1.0
    c["ETAB"] = _bf(et)
    sm = np.zeros((NNT * 128, 65), np.float32)
    c0 = np.arange(NCMP) * 16
    s0 = np.arange(NSEL) * 64
    ov = (np.minimum(c0[:, None] + 31, s0[None, :] + 63) - np.maximum(c0[:, None], s0[None, :]) + 1)
    sm[:NCMP, :NSEL] = np.clip(ov, 0, None) / 16.0
    sm[:NCMP, 64] = 1.0
    c["SELMAP"] = _bf(sm.reshape(NNT, 128, 65).transpose(1, 0, 2))
    cur = t // 64
    j = np.arange(NSEL)
    forced = (j[None, :] == 0) | (j[None, :] == cur[:, None]) | (j[None, :] == cur[:, None] - 1)
    fut = j[None, :] > cur[:, None]
    add = np.where(forced, 1e4, np.where(fut, -1.0, 0.0)).astype(np.float32)
    mul = np.where(forced | fut, 0.0, 1.0).astype(np.float32)
    c["SCADD"] = np.ascontiguousarray(add.reshape(S // 128, 128, NSEL))
    c["SCMUL"] = np.ascontiguousarray(mul.reshape(S // 128, 128, NSEL))
    return c


def phase_n(kb, S, I, SC):
    op, dma = kb.op, kb.dma
    NCMP, NNT, NSEL, TOPK = nsa_dims(S)
    NQT = S // 512
    NKT = S // 128

    def tl(name, shape, dt, nbuf=1, psum=False):
        return Tl(kb, name, shape, dt, nbuf, psum)

    def one(name, shape, dt, psum=False):
        return tl(name, shape, dt, 1, psum).nxt()

    idf, idfb = one("identf", [128, 128], F32)
    dma("sp", idf[:], I["ident_f"][:, :], w=[idfb])
    idbf, idbfb = one("identb", [128, 128], BF16)
    dma("sp", idbf[:], I["ident_bf"][:, :], w=[idbfb])
    etab, etb = one("etab", [128, S], BF16)
    dma("sp", etab[:], I["ETAB"][:, :], w=[etb])
    selmap, smb = one("selmap", [128, NNT, 65], BF16)
    dma("sp", selmap[:], I["SELMAP"][:, :, :], w=[smb])

    cw = {}
    for kv in ("k", "v"):
        w1f, w1fb = one("w1f" + kv, [64, 32, 128], F32)
        dma("sp", w1f[:], I["cmp_%s_w1" % kv][0].rearrange("(l d) h -> d l h", d=64), w=[w1fb])
        w1b, w1bb = one("w1b" + kv, [64, 32, 128], BF16)
        op("dve", lambda e: e.tensor_copy(w1b[:], w1f[:]), r=[w1fb], w=[w1bb])
        peT, peb = one("peT" + kv, [64, 32], F32)
        dma("sp", peT[:], I["cmp_pe_" + kv][0].rearrange("l d -> d l"), w=[peb], allow_slow_non_contiguous=True)
        b1c, b1b = one("b1c" + kv, [128, 1], F32)
        dma("sp", b1c[:], I["cmp_%s_b1" % kv].rearrange("o h -> h o"), w=[b1b], allow_slow_non_contiguous=True)
        w2f, w2fb = one("w2f" + kv, [128, 64], F32)
        dma("sp", w2f[:], I["cmp_%s_w2" % kv][0], w=[w2fb])
        w2b, w2bb = one("w2b" + kv, [128, 64], BF16)
        op("dve", lambda e: e.tensor_copy(w2b[:], w2f[:]), r=[w2fb], w=[w2bb])
        w2p, w2pb = one("w2p" + kv, [128, 128], BF16)
        op("dve", lambda e: e.memset(w2p[:], 0.0), w=[w2pb])
        op("dve", lambda e: e.tensor_copy(w2p[:, 64:128], w2f[:]), r=[w2fb], w=[w2pb])
        cw[kv] = dict(w1f=(w1f, w1fb), w1b=(w1b, w1bb), peT=(peT, peb), b1=(b1c, b1b), w2b=(w2b, w2bb),
                      w2p=(w2p, w2pb))

    sps = tl("sps", [128, 512], F32, nbuf=3, psum=True)
    ops_ = tl("ops", [128, 512], F32, nbuf=2, psum=True)
    trpt, _ = one("trp", [128, 8, 128], BF16, psum=True)
    _tb = Buf("trp0", excl=True)
    trpbufs = [_tb, _tb]
    fincnt = [0]
    stp = tl("stp", [128, 512], F32, nbuf=1, psum=True)
    ipp = tl("ipp", [128, 4, 65], F32, nbuf=1, psum=True)

    for kv in ("k", "v"):
        pt, pb = sps.nxt()
        w1f, w1fb = cw[kv]["w1f"]
        peT, peb = cw[kv]["peT"]
        for l in range(32):
            op("pe", lambda e: e.matmul(pt[:, 0:1], w1f[:, l, :], peT[:, l:l + 1], start=(l == 0), stop=(l == 31)),
               r=[w1fb, peb], w=[pb], sig=(l == 31))
        bt, btb = one("btot" + kv, [128, 1], F32)
        op("dve", lambda e: e.tensor_tensor(out=bt[:], in0=pt[:, 0:1], in1=cw[kv]["b1"][0][:], op=ALU.add),
           r=[pb, cw[kv]["b1"][1]], w=[btb])
        cw[kv]["bt"] = (bt, btb)

    ksl, kslb = one("ksl", [128, S], BF16)
    kwn, kwnb = one("kwn", [128, S], BF16)
    op("dve", lambda e: e.memset(kwn[:], 0.0), w=[kwnb])
    dma("sp", ksl[0:60, :], I["ETAB"][0:60, :], w=[kslb])
    dma("sp", ksl[60:64, :], I["KAUGP"][:, :], w=[kslb])
    dma("sp", kwn[60:64, :], I["KAUGP"][:, :], w=[kwnb])
    vsl, vslb = one("vsl", [128, NKT, 128], BF16)
    vwn, vwnb = one("vwn", [128, NKT, 128], BF16)
    qa = [one("qa%d" % r, [128, S], BF16) for r in range(4)]
    qasel = [Buf("qasel%d" % r) for r in range(4)]
    for r in range(4):
        op("dve", lambda e: e.memset(qa[r][0][:], 0.0), w=[qa[r][1], qasel[r]])
    kct, kctb = one("kct", [64, S], BF16)
    vct, vctb = one("vct", [64, S], BF16)
    hTk, hTkb = one("hTk", [128, NNT * 128], BF16)
    hTv, hTvb = one("hTv", [128, NNT * 128], BF16)
    op("dve", lambda e: e.memset(hTk[:], 0.0), w=[hTkb])
    op("dve", lambda e: e.memset(hTv[:], 0.0), w=[hTvb])
    kca, kcab = one("kca", [128, NNT * 128], BF16)
    op("dve", lambda e: e.memset(kca[:], 0.0), w=[kcab])
    dma("sp", kca[60:64, :], I["KAUGC"][:, :], w=[kcab])
    vca, vcab = one("vca", [128, NNT, 128], BF16)
    op("dve", lambda e: e.memset(vca[:], 0.0), w=[vcab])
    selbT, selbTb = one("selbT", [128, S], BF16)
    pT = tl("pT", [128, 512], BF16, nbuf=4)
    ccl = tl("ccl", [128, 512], F32, nbuf=2)
    osb = tl("osb", [128, 512], BF16, nbuf=2)
    gate = tl("gate", [128, 4, 24], F32, nbuf=2)
    imp, impb = one("imp", [128, 4, 64], F32)
    rec = tl("rec", [128, 4], F32, nbuf=2)
    coef = tl("coef", [128, 4], F32, nbuf=2)
    ytl = tl("ytl", [128, 4, 256], F32, nbuf=2)
    ybf = tl("ybf", [128, 4, 256], BF16, nbuf=2)
    scm = tl("scm", [128, NSEL], F32, nbuf=2)
    sca = tl("sca", [128, NSEL], F32, nbuf=2)
    score = tl("score", [128, NSEL], F32, nbuf=2)
    repl = tl("repl", [128, NSEL], F32, nbuf=2)
    mx = tl("mx", [128, 16], F32, nbuf=2)
    sbias = tl("sbias", [128, 128], F32, nbuf=4)
    for i_ in range(4):
        t_, b_ = sbias.nxt()
        op("dve", lambda e: e.memset(t_[:], 0.0), w=[b_])

    for b in range(NB):
        for g in range(2):
            dma("sp", ksl[64:128, :], SC["KST"][b, g * 64:(g + 1) * 64, :], w=[kslb])
            dma("sp", kwn[64:128, :], SC["KWT"][b, g * 64:(g + 1) * 64, :], w=[kwnb])
            dma("sp", vsl[:], SC["VSA"][b, g], w=[vslb])
            dma("sp", vwn[:], SC["VWA"][b, g], w=[vwnb])
            for r in range(4):
                h = 4 * g + r
                dma("sp", qa[r][0][64:128, :], SC["QT"][b, h * 64:(h + 1) * 64, :], w=[qa[r][1]])
                dma("sp", qa[r][0][60:64, :], I["QAUG"][h], w=[qa[r][1]])
            dma("sp", kct[:], SC["KCT"][b, g * 64:(g + 1) * 64, :], w=[kctb])
            dma("sp", vct[:], SC["VCT"][b, g * 64:(g + 1) * 64, :], w=[vctb])
            for kv, src, srcb, hT_, hTb_ in (("k", kct, kctb, hTk, hTkb), ("v", vct, vctb, hTv, hTvb)):
                pt, pb = sps.nxt()
                w1b, w1bb = cw[kv]["w1b"]
                for l in range(32):
                    op("pe", lambda e: e.matmul(pt[:, 0:NCMP], w1b[:, l, :], src[:, l:l + 16 * (NCMP - 1) + 1:16],
                                                start=(l == 0), stop=(l == 31)),
                       r=[w1bb, srcb], w=[pb], sig=(l == 31))
                op("act", lambda e: e.activation(out=hT_[:, 0:NCMP], in_=pt[:, 0:NCMP], func=AF.Gelu_apprx_tanh,
                                                 bias=cw[kv]["bt"][0][:, 0:1]), r=[pb, cw[kv]["bt"][1]], w=[hTb_])
            pt, pb = sps.nxt()
            op("pe", lambda e: e.matmul(pt[:, 0:NCMP], cw["k"]["w2p"][0][:], hTk[:, 0:NCMP], start=True, stop=True),
               r=[cw["k"]["w2p"][1], hTkb], w=[pb])
            op("act", lambda e: e.copy(out=kca[64:128, 0:NCMP], in_=pt[64:128, 0:NCMP]), r=[pb], w=[kcab])
            pt, pb = sps.nxt()
            for nt in range(NNT):
                op("pe", lambda e: e.matmul(pt[:, nt * 64:(nt + 1) * 64], hTv[:, nt * 128:(nt + 1) * 128],
                                            cw["v"]["w2b"][0][:], start=True, stop=True),
                   r=[cw["v"]["w2b"][1], hTvb], w=[pb], sig=(nt == NNT - 1))
            op("act", lambda e: e.copy(out=vca[:, :, 0:64],
                                       in_=pt[:, 0:NNT * 64].rearrange("p (a d) -> p a d", d=64)), r=[pb], w=[vcab])
            for nt in range(NNT):
                nn = min(128, NCMP - nt * 128)
                op("dve", lambda e: e.memset(vca[0:nn, nt, 64:128], 1.0), w=[vcab])

            for qt in range(NQT):
                q0 = qt * 512
                gt, gtb = gate.nxt()
                dma("sp", gt[:], SC["GATE"][b, q0:q0 + 512, :].rearrange("(j p) c -> p j c", p=128), w=[gtb])
                yt, ytb = ytl.nxt()

                def fin(ot_, ob_, r, br):
                    h = 4 * g + r
                    st_, sb_ = osb.nxt()
                    op("dve", lambda e: e.tensor_copy(st_[:], ot_[:]), r=[ob_], w=[sb_])
                    so = 4 * (fincnt[0] % 2)
                    tb_ = trpbufs[fincnt[0] % 2]
                    fincnt[0] += 1
                    tt_ = trpt[:, so:so + 4, :]
                    for sub in range(4):
                        op("pe", lambda e: e.transpose(tt_[:, sub, :], st_[:, sub * 128:(sub + 1) * 128], idbf[:]),
                           r=[sb_, idbfb], w=[tb_], sig=(sub == 3), c=0.08)
                    rt_, rb_ = rec.nxt()
                    op("dve", lambda e: e.tensor_scalar(out=rt_[:], in0=tt_[:, :, 64], scalar1=1e-30, scalar2=None,
                                                        op0=ALU.max), r=[tb_], w=[rb_])
                    op("dve", lambda e: e.reciprocal(out=rt_[:], in_=rt_[:]), r=[rb_], w=[rb_])
                    ct_, cb_ = coef.nxt()
                    op("dve", lambda e: e.tensor_tensor(out=ct_[:], in0=rt_[:], in1=gt[:, :, 3 * h + br], op=ALU.mult),
                       r=[rb_, gtb], w=[cb_])
                    for sub in range(4):
                        if br == 0:
                            op("dve", lambda e: e.tensor_scalar(out=yt[:, sub, r * 64:(r + 1) * 64],
                                                                in0=tt_[:, sub, 0:64], scalar1=ct_[:, sub:sub + 1],
                                                                scalar2=None, op0=ALU.mult),
                               r=[tb_, cb_], w=[ytb])
                        else:
                            op("dve", lambda e: e.scalar_tensor_tensor(out=yt[:, sub, r * 64:(r + 1) * 64],
                                                                       in0=tt_[:, sub, 0:64],
                                                                       scalar=ct_[:, sub:sub + 1],
                                                                       in1=yt[:, sub, r * 64:(r + 1) * 64],
                                                                       op0=ALU.mult, op1=ALU.add),
                               r=[tb_, cb_, ytb], w=[ytb])

                nts = [nt for nt in range(NNT) if 16 * (nt * 128) + 31 <= q0 + 511]
                for r in range(4):
                    ot_, ob_ = ops_.nxt()
                    pts = []
                    for ii, nt in enumerate(nts):
                        st_, sb_ = sps.nxt()
                        op("pe", lambda e: e.matmul(st_[:], kca[:, nt * 128:(nt + 1) * 128], qa[r][0][:, q0:q0 + 512],
                                                    start=True, stop=True), r=[kcab, qa[r][1]], w=[sb_])
                        pt_, pb_ = pT.nxt()
                        if not (16 * (nt * 128 + 127) + 31 <= q0):
                            ct_, cb_ = ccl.nxt()
                            op("dve", lambda e: e.tensor_scalar(out=ct_[:], in0=st_[:], scalar1=80.0, scalar2=None,
                                                                op0=ALU.min), r=[sb_], w=[cb_])
                            op("act", lambda e: e.activation(out=pt_[:], in_=ct_[:], func=AF.Exp), r=[cb_], w=[pb_])
                        else:
                            op("act", lambda e: e.activation(out=pt_[:], in_=st_[:], func=AF.Exp), r=[sb_], w=[pb_])
                        if not (16 * (nt * 128 + 127) + 31 <= q0):
                            op("pool", lambda e: e.affine_select(out=pt_[:], in_=pt_[:], pattern=[[1, 512]],
                                                                 compare_op=ALU.is_ge, fill=0.0,
                                                                 base=q0 - 16 * nt * 128 - 31,
                                                                 channel_multiplier=-16), r=[pb_], w=[pb_], c=0.6)
                        op("pe", lambda e: e.matmul(ot_[:], vca[:, nt, :], pt_[:], start=(ii == 0),
                                                    stop=(ii == len(nts) - 1)),
                           r=[vcab, pb_], w=[ob_])
                        pts.append((pt_, pb_))
                    it_, ib_ = ipp.nxt()
                    for sub in range(4):
                        for ii, nt in enumerate(nts):
                            op("pe", lambda e: e.matmul(it_[:, sub, :], pts[ii][0][:, sub * 128:(sub + 1) * 128],
                                                        selmap[:, nt, :], start=(ii == 0), stop=(ii == len(nts) - 1)),
                               r=[pts[ii][1], smb], w=[ib_], sig=(sub == 3 and ii == len(nts) - 1))
                    rt_, rb_ = rec.nxt()
                    op("dve", lambda e: e.tensor_scalar(out=rt_[:], in0=it_[:, :, 64], scalar1=1e-30, scalar2=None,
                                                        op0=ALU.max), r=[ib_], w=[rb_])
                    op("dve", lambda e: e.reciprocal(out=rt_[:], in_=rt_[:]), r=[rb_], w=[rb_])
                    for sub in range(4):
                        if r == 0:
                            op("dve", lambda e: e.tensor_scalar(out=imp[:, sub, :], in0=it_[:, sub, 0:64],
                                                                scalar1=rt_[:, sub:sub + 1], scalar2=None,
                                                                op0=ALU.mult), r=[ib_, rb_], w=[impb])
                        else:
                            op("dve", lambda e: e.scalar_tensor_tensor(out=imp[:, sub, :], in0=it_[:, sub, 0:64],
                                                                       scalar=rt_[:, sub:sub + 1], in1=imp[:, sub, :],
                                                                       op0=ALU.mult, op1=ALU.add),
                               r=[ib_, rb_, impb], w=[impb])
                    fin(ot_, ob_, r, 0)
                def stream(r, br, ktile, vtile, ktb, vtb, tiles, fold):
                    ot_, ob_ = ops_.nxt()
                    for ii, (kt, qlo, qhi, selk) in enumerate(tiles):
                        k0 = kt * 128
                        c0, c1 = qlo - q0, qhi - q0
                        nq = qhi - qlo
                        cc = 0.08 + 0.12 * nq / 512.0
                        st_, sb_ = sps.nxt()
                        rb_ = [ktb, qa[r][1]] + ([qasel[r]] if fold else [])
                        if fold and kt * 2 + 1 >= 60:
                            op("pe", lambda e: e.matmul(st_[:, c0:c1], ktile[:, k0:k0 + 128], qa[r][0][:, qlo:qhi],
                                                        start=True, stop=False), r=rb_, w=[sb_], sig=False, c=cc)
                            op("pe", lambda e: e.matmul(st_[:, c0:c1], etab[:, k0:k0 + 128], selbT[:, qlo:qhi],
                                                        start=False, stop=True), r=[etb, selbTb], w=[sb_], c=cc)
                        else:
                            op("pe", lambda e: e.matmul(st_[:, c0:c1], ktile[:, k0:k0 + 128], qa[r][0][:, qlo:qhi],
                                                        start=True, stop=True), r=rb_, w=[sb_], c=cc)
                        pt_, pb_ = pT.nxt()
                        op("act", lambda e: e.activation(out=pt_[:, c0:c1], in_=st_[:, c0:c1], func=AF.Exp),
                           r=[sb_], w=[pb_], c=0.12 + 0.45 * nq / 512.0)
                        if selk == 1:
                            op("pool", lambda e: e.affine_select(out=pt_[:, c0:c1], in_=pt_[:, c0:c1],
                                                                 pattern=[[1, nq]], compare_op=ALU.is_ge, fill=0.0,
                                                                 base=qlo - k0, channel_multiplier=-1),
                               r=[pb_], w=[pb_], c=0.15 + 0.45 * nq / 512.0)
                        elif selk == 2:
                            op("pool", lambda e: e.affine_select(out=pt_[:, c0:c1], in_=pt_[:, c0:c1],
                                                                 pattern=[[-1, nq]], compare_op=ALU.is_ge, fill=0.0,
                                                                 base=k0 - qlo + 511, channel_multiplier=1),
                               r=[pb_], w=[pb_], c=0.15 + 0.45 * nq / 512.0)
                        op("pe", lambda e: e.matmul(ot_[:, c0:c1], vtile[:, kt, :], pt_[:, c0:c1], start=(ii == 0),
                                                    stop=(ii == len(tiles) - 1)),
                           r=[vtb, pb_], w=[ob_], c=cc)
                    fin(ot_, ob_, r, br)

                def win_tiles():
                    full, part = [], []
                    for kt in range(max(0, (q0 - 512) // 128), (q0 + 511) // 128 + 1):
                        k0 = kt * 128
                        if k0 >= q0:
                            t_ = (kt, max(q0, k0), q0 + 512, 1)
                        else:
                            t_ = (kt, q0, min(q0 + 512, k0 + 128 + 511), 2)
                        (full if t_[2] - t_[1] == 512 else part).append(t_)
                    return full + part

                def sel_tiles():
                    full, part = [], []
                    for kt in range(0, (q0 + 511) // 128 + 1):
                        k0 = kt * 128
                        if k0 + 127 > q0:
                            t_ = (kt, max(q0, k0), q0 + 512, 1)
                        else:
                            t_ = (kt, q0, q0 + 512, 0)
                        (full if t_[2] - t_[1] == 512 else part).append(t_)
                    return full + part

                for r in range(4):
                    stream(r, 2, kwn, vwn, kwnb, vwnb, win_tiles(), False)
                spt, spb = stp.nxt()
                for sub in range(4):
                    q128 = qt * 4 + sub
                    mt_, mb_ = scm.nxt()
                    at_, ab_ = sca.nxt()
                    dma("sp", mt_[:], I["SCMUL"][q128], w=[mb_])
                    dma("sp", at_[:], I["SCADD"][q128], w=[ab_])
                    sct, scb = score.nxt()
                    op("dve", lambda e: e.tensor_tensor(out=sct[:], in0=imp[:, sub, 0:NSEL], in1=mt_[:], op=ALU.mult),
                       r=[impb, mb_], w=[scb])
                    op("dve", lambda e: e.tensor_tensor(out=sct[:], in0=sct[:], in1=at_[:], op=ALU.add),
                       r=[scb, ab_], w=[scb])
                    mxt, mxb = mx.nxt()
                    op("dve", lambda e: e.max(out=mxt[:, 0:8], in_=sct[:]), r=[scb], w=[mxb])
                    if TOPK == 16:
                        rpt, rpb = repl.nxt()
                        op("dve", lambda e: e.match_replace(out=rpt[:], in_to_replace=mxt[:, 0:8], in_values=sct[:],
                                                            imm_value=-1e30), r=[mxb, scb], w=[rpb])
                        op("dve", lambda e: e.max(out=mxt[:, 8:16], in_=rpt[:]), r=[rpb], w=[mxb])
                        thr = mxt[:, 15:16]
                    else:
                        thr = mxt[:, 7:8]
                    sbt, sbb = sbias.nxt()
                    op("dve", lambda e: e.tensor_scalar(out=sbt[:, 0:NSEL], in0=sct[:], scalar1=thr, scalar2=-30000.0,
                                                        op0=ALU.is_lt, op1=ALU.mult), r=[scb, mxb], w=[sbb])
                    op("pe", lambda e: e.transpose(spt[:, sub * 128:(sub + 1) * 128], sbt[:], idf[:]),
                       r=[sbb, idfb], w=[spb], sig=(sub == 3))
                op("act", lambda e: e.copy(out=selbT[:, q0:q0 + 512], in_=spt[:]), r=[spb], w=[selbTb])
                for r in range(4):
                    if r % 2 == 0:
                        op("dve", lambda e: e.tensor_copy(qa[r][0][0:60, q0:q0 + 512], spt[0:60, :]),
                           r=[spb], w=[qasel[r]])
                    else:
                        op("act", lambda e: e.copy(out=qa[r][0][0:60, q0:q0 + 512], in_=spt[0:60, :]),
                           r=[spb], w=[qasel[r]])
                for r in range(4):
                    stream(r, 1, ksl, vsl, kslb, vslb, sel_tiles(), True)
                ybt, ybb = ybf.nxt()
                op("act", lambda e: e.copy(out=ybt[:], in_=yt[:]), r=[ytb], w=[ybb])
                dma("pool", SC["YMIX"][b, q0:q0 + 512, g * 256:(g + 1) * 256].rearrange("(j p) c -> p j c", p=128),
                    ybt[:], r=[ybb])


def rwkv_consts():
    c = {}
    p = np.arange(128) % 64
    t = np.arange(64)
    c["MU_S"] = (p[:, None] < t[None, :]).astype(np.float32)
    c["MU_I"] = (p[:, None] <= t[None, :]).astype(np.float32)
    c["ML_S"] = (p[:, None] > t[None, :]).astype(np.float32)
    idm = (p[:, None] == t[None, :]).astype(np.float32)
    c["ID64F"] = np.ascontiguousarray(np.broadcast_to(idm[:, None, :], (128, 8, 64))).astype(np.float32)
    c["ID64B"] = _bf(c["ID64F"])
    return c


def phase_r(kb, S, I, SC):
    op, dma = kb.op, kb.dma
    MC = 4
    NMAC = S // (64 * MC)
    NCH = S // 64

    def tl(name, shape, dt, nbuf=1, psum=False):
        return Tl(kb, name, shape, dt, nbuf, psum)

    def one(name, shape, dt, psum=False):
        return tl(name, shape, dt, 1, psum).nxt()

    masks = {}
    for nm in ("MU_S", "MU_I", "ML_S"):
        t_, b_ = one(nm, [128, 64], F32)
        dma("sp", t_[:], I[nm][:, :], w=[b_])
        masks[nm] = (t_, b_)
    idf, idfb = one("id64f", [128, 8, 64], F32)
    dma("sp", idf[:], I["ID64F"][:, :, :], w=[idfb])
    idb, idbb = one("id64b", [128, 8, 64], BF16)
    dma("sp", idb[:], I["ID64B"][:, :, :], w=[idbb])
    lnw, lnwb = one("lnw", [128, 512], F32)
    dma("sp", lnw[:], I["lnx_w"].partition_broadcast(128), w=[lnwb])
    lnb, lnbb = one("lnb", [128, 512], F32)
    dma("sp", lnb[:], I["lnx_b"].partition_broadcast(128), w=[lnbb])

    FM = tl("FM", [128, 4, 8, MC * 64], BF16, nbuf=3)
    TM = tl("TM", [128, 4, MC, 512], BF16, nbuf=3)
    PCt = tl("PCt", [128, 8, MC], F32, nbuf=3)
    GTt = tl("GTt", [128, MC, 512], BF16, nbuf=3)
    BONt = tl("BONt", [128, MC, 8], F32, nbuf=3)
    ps = tl("ps", [128, 512], F32, nbuf=8, psum=True)

    def bt(name, nbuf=3):
        return tl(name, [128, 8, 64], BF16, nbuf=nbuf)

    NN, ARB, ARK, AAK = bt("NN", 4), bt("ARB", 4), bt("ARK", 4), bt("AAK", 4)
    Ys, Pbs = bt("Ys", 4), bt("Pbs", 3)
    XPs = tl("XPs", [128, 8, 2, 64], BF16, nbuf=4)
    Pm = tl("Pm", [128, 8, 64], F32, nbuf=4)
    WT, Gs, Ub = bt("WT", 4), bt("Gs", 4), bt("Ub", 3)
    Xl = tl("Xl", [128, 8, 64], F32, nbuf=4)
    M, Mb_ = one("M", [128, 8, 64], F32)
    Mbf, Mbfb = one("Mbf", [128, 8, 64], BF16)
    op("dve", lambda e: e.memset(M[:], 0.0), w=[Mb_])
    op("dve", lambda e: e.memset(Mbf[:], 0.0), w=[Mbfb])
    Mtmp = tl("Mtmp", [128, 8, 64], F32, nbuf=2)
    yt = tl("yt", [128, 8, 64], F32, nbuf=4)
    g1 = tl("g1", [128, 8, 64], F32, nbuf=2)
    g2 = tl("g2", [128, 8, 64], F32, nbuf=2)
    st1 = tl("st1", [128, 8], F32, nbuf=2)
    st2 = tl("st2", [128, 8], F32, nbuf=2)
    st3 = tl("st3", [128, 8], F32, nbuf=2)
    yo = tl("yo", [128, 8, 64], BF16, nbuf=3)

    macro = {}

    def load_macro(m):
        t0 = m * MC * 64
        fm, fmb = FM.nxt()
        tm, tmb = TM.nxt()
        pc, pcb = PCt.nxt()
        gt, gtb = GTt.nxt()
        bo, bob = BONt.nxt()
        for b in range(NB):
            ph = slice(b * 64, (b + 1) * 64)
            for kind in range(4):
                dma("sp", fm[ph, kind, :, :],
                    SC["RWF"][b, kind].rearrange("(h k) t -> k h t", k=64)[:, :, t0:t0 + MC * 64], w=[fmb])
                dma("sp", tm[ph, kind, :, :],
                    SC["RWT"][b, kind, t0:t0 + MC * 64, :].rearrange("(c t) ch -> t c ch", t=64), w=[tmb])
            dma("sp", pc[ph, :, :], SC["PC"][b].rearrange("(h k) c -> k h c", k=64)[:, :, m * MC:(m + 1) * MC],
                w=[pcb], allow_slow_non_contiguous=True)
            dma("sp", gt[ph, :, :], SC["GT"][b, t0:t0 + MC * 64, :].rearrange("(c t) ch -> t c ch", t=64), w=[gtb])
            dma("sp", bo[ph, :, :], SC["BON"][b, t0:t0 + MC * 64, :].rearrange("(c t) ch -> t c ch", t=64), w=[bob])
        macro[m] = dict(fm=(fm, fmb), tm=(tm, tmb), pc=(pc, pcb), gt=(gt, gtb), bo=(bo, bob))

    units = [(b, h) for h in range(8) for b in range(NB)]
    pre = {}

    def mm_all(dst, dstb, lhs_fn, rhs_fn, rbufs):
        for i, (b, h) in enumerate(units):
            ph = slice(b * 64, (b + 1) * 64)
            op("pe", lambda e: e.matmul(dst[ph, h * 64:(h + 1) * 64], lhs_fn(ph, h), rhs_fn(ph, h),
                                        start=True, stop=True), r=rbufs, w=[dstb], sig=(i == len(units) - 1), c=0.055)

    def v3(t_):
        return t_[:].rearrange("p (h c) -> p h c", c=64)

    def stage_p(c):
        m, cl = c // MC, c % MC
        fm, fmb = macro[m]["fm"]
        tm, tmb = macro[m]["tm"]
        cs = slice(cl * 64, (cl + 1) * 64)
        outs = {}
        XP, XPb = XPs.nxt()
        for lk, rk, tile_, mask in ((1, 0, None, "MU_S"), (1, 3, ARB, "MU_I"), (2, 0, AAK, "MU_S"),
                                    (2, 3, ARK, "MU_I"), (0, 1, NN, "ML_S")):
            pt, pb = ps.nxt()
            mm_all(pt, pb, lambda ph, h: fm[ph, lk, h, cs], lambda ph, h: fm[ph, rk, h, cs], [fmb])
            mk, mkb = masks[mask]
            if tile_ is None:
                op("dve", lambda e: e.tensor_tensor(out=XP[:, :, 0, :], in0=v3(pt),
                                                    in1=mk[:].unsqueeze(1).to_broadcast([128, 8, 64]), op=ALU.mult),
                   r=[pb, mkb], w=[XPb])
            else:
                ot, ob = tile_.nxt()
                op("dve", lambda e: e.tensor_tensor(out=ot[:], in0=v3(pt),
                                                    in1=mk[:].unsqueeze(1).to_broadcast([128, 8, 64]), op=ALU.mult),
                   r=[pb, mkb], w=[ob])
                outs[tile_] = (ot, ob)
        op("act", lambda e: e.copy(out=XP[:, :, 1, :], in_=idb[:]), r=[idbb], w=[XPb])
        yield
        Y, Yb = outs[NN]
        P, Pmb = Pm.nxt()
        op("dve", lambda e: e.tensor_copy(P[:], idf[:]), r=[idfb], w=[Pmb])
        TT, TTb = None, None
        for i in range(6):
            last = (i == 5)
            if not last:
                banks = [ps.nxt(), ps.nxt()]
                for bi in range(2):
                    bk, bkb = banks[bi]
                    us = [(b, h) for h in range(4 * bi, 4 * bi + 4) for b in range(NB)]
                    for ii, (b, h) in enumerate(us):
                        ph = slice(b * 64, (b + 1) * 64)
                        hh = h % 4
                        op("pe", lambda e: e.matmul(bk[ph, hh * 128:(hh + 1) * 128], Y[ph, h, :],
                                                    XP[ph, h, :, :].rearrange("p a c -> p (a c)"),
                                                    start=True, stop=True),
                           r=[Yb, XPb], w=[bkb], sig=(ii == len(us) - 1), c=0.06)
                Cp, Cpb = ps.nxt()
                mm_all(Cp, Cpb, lambda ph, h: XP[ph, h, 0, :], lambda ph, h: Y[ph, h, :], [Yb, XPb])
                XPn, XPnb = XPs.nxt()
                for bi in range(2):
                    bk, bkb = banks[bi]
                    bv = bk[:].rearrange("p (h a c) -> p h a c", a=2, c=64)
                    hs_ = slice(4 * bi, 4 * bi + 4)
                    op("dve", lambda e: e.tensor_tensor(out=P[:, hs_, :], in0=bv[:, :, 1, :], in1=P[:, hs_, :],
                                                        op=ALU.add), r=[bkb, Pmb], w=[Pmb], c=0.35)
                    op("dve", lambda e: e.tensor_copy(XPn[:, hs_, 0, :], bv[:, :, 0, :]), r=[bkb], w=[XPnb], c=0.3)
                op("act", lambda e: e.copy(out=XPn[:, :, 1, :], in_=P[:]), r=[Pmb], w=[XPnb])
                Yn, Ynb = Ys.nxt()
                op("act", lambda e: e.copy(out=Yn[:], in_=v3(Cp)), r=[Cpb], w=[Ynb])
                XP, XPb, Y, Yb = XPn, XPnb, Yn, Ynb
            else:
                Bp, Bpb = ps.nxt()
                mm_all(Bp, Bpb, lambda ph, h: Y[ph, h, :], lambda ph, h: XP[ph, h, 1, :], [Yb, XPb])
                op("dve", lambda e: e.tensor_tensor(out=P[:], in0=v3(Bp), in1=P[:], op=ALU.add),
                   r=[Bpb, Pmb], w=[Pmb])
                TT, TTb = Pbs.nxt()
                op("act", lambda e: e.copy(out=TT[:], in_=P[:]), r=[Pmb], w=[TTb])
            yield
        Wp, Wpb = ps.nxt()
        mm_all(Wp, Wpb, lambda ph, h: tm[ph, 0, cl, h * 64:(h + 1) * 64], lambda ph, h: TT[ph, h, :], [tmb, TTb])
        Gp, Gpb = ps.nxt()
        aak, aakb = outs[AAK]
        mm_all(Gp, Gpb, lambda ph, h: aak[ph, h, :], lambda ph, h: tm[ph, 3, cl, h * 64:(h + 1) * 64], [aakb, tmb])
        wt, wtb = WT.nxt()
        op("act", lambda e: e.copy(out=wt[:], in_=v3(Wp)), r=[Wpb], w=[wtb])
        gs, gsb = Gs.nxt()
        op("dve", lambda e: e.tensor_copy(gs[:], v3(Gp)), r=[Gpb], w=[gsb])
        yield
        Xp, Xpb = ps.nxt()
        mm_all(Xp, Xpb, lambda ph, h: TT[ph, h, :], lambda ph, h: gs[ph, h, :], [TTb, gsb])
        xl, xlb = Xl.nxt()
        op("act", lambda e: e.copy(out=xl[:], in_=v3(Xp)), r=[Xpb], w=[xlb])
        pre[c] = dict(wt=(wt, wtb), xl=(xl, xlb), arb=outs[ARB], ark=outs[ARK])
        yield

    ych = {}

    def stage_q(c):
        m, cl = c // MC, c % MC
        fm, fmb = macro[m]["fm"]
        tm, tmb = macro[m]["tm"]
        pc, pcb = macro[m]["pc"]
        cs = slice(cl * 64, (cl + 1) * 64)
        pr = pre.pop(c)
        wt, wtb = pr["wt"]
        xl, xlb = pr["xl"]
        arb, arbb = pr["arb"]
        ark, arkb = pr["ark"]
        Up, Upb = ps.nxt()
        mm_all(Up, Upb, lambda ph, h: wt[ph, h, :], lambda ph, h: Mbf[ph, h, :], [wtb, Mbfb])
        ub, ubb = Ub.nxt()
        op("dve", lambda e: e.tensor_tensor(out=ub[:], in0=v3(Up), in1=xl[:], op=ALU.add), r=[Upb, xlb], w=[ubb])
        mt_, mtb_ = Mtmp.nxt()
        op("pool", lambda e: e.tensor_tensor(out=mt_[:], in0=M[:],
                                             in1=pc[:, :, cl].unsqueeze(2).to_broadcast([128, 8, 64]), op=ALU.mult),
           r=[Mb_, pcb], w=[mtb_])
        yield
        Yp, Ypb = ps.nxt()
        Mp, Mpb = ps.nxt()
        for i, (b, h) in enumerate(units):
            ph = slice(b * 64, (b + 1) * 64)
            hs = slice(h * 64, (h + 1) * 64)
            op("pe", lambda e: e.matmul(Yp[ph, hs], fm[ph, 3, h, cs], Mbf[ph, h, :], start=True, stop=False),
               r=[fmb, Mbfb], w=[Ypb], sig=False, c=0.055)
            op("pe", lambda e: e.matmul(Yp[ph, hs], ark[ph, h, :], tm[ph, 3, cl, hs], start=False, stop=False),
               r=[arkb, tmb], w=[Ypb], sig=False, c=0.055)
            op("pe", lambda e: e.matmul(Yp[ph, hs], arb[ph, h, :], ub[ph, h, :], start=False, stop=True),
               r=[arbb, ubb], w=[Ypb], sig=(i == len(units) - 1), c=0.055)
        for i, (b, h) in enumerate(units):
            ph = slice(b * 64, (b + 1) * 64)
            hs = slice(h * 64, (h + 1) * 64)
            op("pe", lambda e: e.matmul(Mp[ph, hs], tm[ph, 2, cl, hs], tm[ph, 3, cl, hs], start=True, stop=False),
               r=[tmb], w=[Mpb], sig=False, c=0.055)
            op("pe", lambda e: e.matmul(Mp[ph, hs], tm[ph, 1, cl, hs], ub[ph, h, :], start=False, stop=True),
               r=[tmb, ubb], w=[Mpb], sig=(i == len(units) - 1), c=0.055)
        op("dve", lambda e: e.tensor_tensor(out=M[:], in0=v3(Mp), in1=mt_[:], op=ALU.add), r=[Mpb, mtb_], w=[Mb_])
        op("act", lambda e: e.copy(out=Mbf[:], in_=M[:]), r=[Mb_], w=[Mbfb])
        y_, yb_ = yt.nxt()
        op("act", lambda e: e.copy(out=y_[:], in_=v3(Yp)), r=[Ypb], w=[yb_])
        ych[c] = (y_, yb_)
        yield

    def stage_g(c):
        m, cl = c // MC, c % MC
        tm, tmb = macro[m]["tm"]
        gt, gtb = macro[m]["gt"]
        bo, bob = macro[m]["bo"]
        y_, yb_ = ych.pop(c)
        s1, s1b = st1.nxt()
        s2, s2b = st2.nxt()
        s3, s3b = st3.nxt()
        a1, a1b = g1.nxt()
        a2, a2b = g2.nxt()

        def bc(t_):
            return t_[:].unsqueeze(2).to_broadcast([128, 8, 64])
        op("dve", lambda e: e.tensor_reduce(out=s1[:], in_=y_[:], axis=AX.X, op=ALU.add), r=[yb_], w=[s1b])
        op("act", lambda e: e.activation(out=a1[:], in_=y_[:], func=AF.Square), r=[yb_], w=[a1b])
        op("dve", lambda e: e.tensor_reduce(out=s2[:], in_=a1[:], axis=AX.X, op=ALU.add), r=[a1b], w=[s2b])
        op("dve", lambda e: e.tensor_scalar(out=s1[:], in0=s1[:], scalar1=1.0 / 64, scalar2=None, op0=ALU.mult),
           r=[s1b], w=[s1b])
        op("dve", lambda e: e.tensor_tensor(out=s3[:], in0=s1[:], in1=s1[:], op=ALU.mult), r=[s1b], w=[s3b])
        op("dve", lambda e: e.scalar_tensor_tensor(out=s2[:], in0=s2[:], scalar=1.0 / 64, in1=s3[:], op0=ALU.mult,
                                                   op1=ALU.subtract), r=[s2b, s3b], w=[s2b])
        op("dve", lambda e: e.tensor_scalar(out=s2[:], in0=s2[:], scalar1=64e-5, scalar2=None, op0=ALU.add),
           r=[s2b], w=[s2b])
        op("act", lambda e: e.activation(out=s2[:], in_=s2[:], func=AF.Sqrt), r=[s2b], w=[s2b])
        op("dve", lambda e: e.reciprocal(out=s2[:], in_=s2[:]), r=[s2b], w=[s2b])
        op("pool", lambda e: e.tensor_tensor(out=a1[:], in0=y_[:], in1=bc(s1), op=ALU.subtract),
           r=[yb_, s1b], w=[a1b])
        op("pool", lambda e: e.tensor_tensor(out=a1[:], in0=a1[:], in1=bc(s2), op=ALU.mult), r=[a1b, s2b], w=[a1b])
        op("pool", lambda e: e.tensor_tensor(out=a1[:], in0=a1[:], in1=lnw[:].rearrange("p (h c) -> p h c", c=64),
                                             op=ALU.mult), r=[a1b, lnwb], w=[a1b])
        op("pool", lambda e: e.tensor_tensor(out=a1[:], in0=a1[:], in1=lnb[:].rearrange("p (h c) -> p h c", c=64),
                                             op=ALU.add), r=[a1b, lnbb], w=[a1b])
        op("dve", lambda e: e.tensor_tensor(out=a2[:], in0=tm[:, 3, cl, :].rearrange("p (h c) -> p h c", c=64),
                                            in1=bc(bo[:, cl, :]) if False else bo[:, cl, :].unsqueeze(2).to_broadcast([128, 8, 64]),
                                            op=ALU.mult), r=[tmb, bob], w=[a2b])
        op("pool", lambda e: e.tensor_tensor(out=a1[:], in0=a1[:], in1=a2[:], op=ALU.add), r=[a1b, a2b], w=[a1b])
        o_, ob_ = yo.nxt()
        op("dve", lambda e: e.tensor_tensor(out=o_[:], in0=a1[:], in1=gt[:, cl, :].rearrange("p (h c) -> p h c", c=64),
                                            op=ALU.mult), r=[a1b, gtb], w=[ob_])
        for b in range(NB):
            ph = slice(b * 64, (b + 1) * 64)
            dma("pool", SC["YMIX"][b, c * 64:(c + 1) * 64, 512:1024], o_[ph].rearrange("p h c -> p (h c)"), r=[ob_])
        yield

    def run_all(gens):
        gens = list(gens)
        while gens:
            nxt_ = []
            for g_ in gens:
                try:
                    next(g_)
                    nxt_.append(g_)
                except StopIteration:
                    pass
            gens = nxt_

    load_macro(0)
    for c2 in range(0, NCH + 4, 2):
        m_next = c2 // MC + 1
        if c2 % MC == 2 and m_next < NMAC:
            load_macro(m_next)
        tasks = []
        for cc in (c2, c2 + 1):
            if cc < NCH:
                tasks.append(stage_p(cc))
        qs = [stage_q(cc) for cc in (c2 - 2, c2 - 1) if 0 <= cc < NCH]
        if qs:
            def seq(gs):
                for g_ in gs:
                    yield from g_
            tasks.append(seq(qs))
        for cc in (c2 - 4, c2 - 3):
            if 0 <= cc < NCH:
                tasks.append(stage_g(cc))
        run_all(tasks)


def build(S, shapes, consts, dbg=(), dbg_in=(), phases="A,N,R,D1,D2"):
    nc = bass.Bass("TRN2", target_bir_lowering=False)
    I = {}
    for nm in INPUT_NAMES:
        shp = list(shapes[nm])
        I[nm] = dram(nc, nm, shp, F32, kind="ExternalInput")
    for nm, arr in consts.items():
        I[nm] = dram(nc, nm, arr.shape, BF16 if arr.dtype == NPBF else F32, kind="ExternalInput")
    SC = {}
    for nm, (shp, dt) in scratch_defs(S).items():
        kind = "Internal"
        if nm in dbg:
            kind = "ExternalOutput"
        if nm in dbg_in:
            kind = "ExternalInput"
        SC[nm] = dram(nc, nm, shp, dt, kind=kind)
    out = dram(nc, "out", [NB, S, D], F32, kind="ExternalOutput")
    phases = phases.split(",")
    with ExitStack() as es:
        kb = KB(nc, es)
        kb.recording = SCHED
        if "A" in phases:
            phase_a(nc, kb, S, I, SC)
            kb.barrier()
        if "N" in phases:
            run_phase(kb, phase_n, S, I, SC)
        if "R" in phases:
            run_phase(kb, phase_r, S, I, SC)
        if "D1" in phases:
            run_phase(kb, phase_d1, S, I, SC)
        if "D2" in phases:
            run_phase(kb, phase_d2, S, I, SC, out)
        kb.barrier()
        print("instructions:", kb.ninstr)
    return nc


SEQ = 4096
NCORES = 8


def kernel(**inputs):
    S = SEQ
    consts = host_consts(S)
    shapes = {k: tuple(np.asarray(v).shape) for k, v in inputs.items()}
    shapes["x"] = (NB, S, D)
    shapes["p"] = (1, NB, S, 256)
    nc = build(S, shapes, consts)
    base = {k: np.ascontiguousarray(np.asarray(v, dtype=np.float32)) for k, v in inputs.items()
            if k not in ("x", "p")}
    base.update(consts)
    x = np.asarray(inputs["x"], dtype=np.float32)
    p = np.asarray(inputs["p"], dtype=np.float32)
    in_maps = []
    for c in range(NCORES):
        m = dict(base)
        m["x"] = np.ascontiguousarray(x[NB * c:NB * (c + 1)])
        m["p"] = np.ascontiguousarray(p[:, NB * c:NB * (c + 1)])
        in_maps.append(m)
    res = run_bass_kernel_spmd(nc, in_maps, core_ids=list(range(NCORES)))
    return np.concatenate([np.asarray(r["out"], dtype=np.float32) for r in res.results], axis=0)
```
